# Optimizing a Trainium2 kernel written in Bass

```python
import jax, jax.numpy as jnp
from jax import lax
import numpy as np

D_MODEL = 1024
BATCH = 8
SEQ = 2048
DEPTH = 2

CHUNK = 64
N_MEM = 256
HEAD_DIM = 64
A_HEADS = 6
A_WIDTH = A_HEADS * HEAD_DIM
A_DECAY_LORA = 32
A_ICLR_LORA = 32
A_VRES_LORA = 32
A_GATE_LORA = 64
A_GN_EPS = 64e-5
A_PROJ = 3 * A_WIDTH + A_DECAY_LORA + A_ICLR_LORA + A_GATE_LORA
B_HEADS = 6
B_WIDTH = B_HEADS * HEAD_DIM
B_PREV_CHUNKS = 8
B_BAND = (B_PREV_CHUNKS + 1) * CHUNK
REL_MAX = 256
N_REL = CHUNK - 1 + REL_MAX + 1
C_GROUPS = 4
C_GROUP_DIM = 64
C_WIDTH = C_GROUPS * C_GROUP_DIM
POOL_WINDOWS = (2, 4, 8, 16)
MIX_WIDTH = A_WIDTH + B_WIDTH + C_WIDTH
IN_PROJ = A_PROJ + 3 * B_WIDTH + C_WIDTH
X_HEADS = 4
X_HEAD_DIM = D_MODEL // X_HEADS
D_FF = 2816
RMS_EPS = 1e-6
NEG_INF = -1e30

kernel_name = "hybrid_rwkv7_chunkattn_pool_macaron"


def rms_norm(x, g):
    xf = x.astype(jnp.float32)
    y = xf * lax.rsqrt(jnp.mean(xf * xf, axis=-1, keepdims=True) + RMS_EPS)
    return (y * g.astype(jnp.float32)).astype(x.dtype)


def swiglu(x, w_in, w_out):
    gate, up = jnp.split(x @ w_in, 2, axis=-1)
    return (jax.nn.silu(gate) * up) @ w_out


def token_shift(p):
    return jnp.pad(p, ((0, 0), (1, 0), (0, 0)))[:, :-1]


def rwkv7_mixer(p, v_first, mu, w0, w_up, a0, a_up, g_up, k_k, k_a, r_k, gn_g, gn_b, vres):
    B_, T, _ = p.shape
    f32 = jnp.float32
    p = p + mu * (token_shift(p) - p)
    r, k, v, wd, ad, gd = jnp.split(
        p, [A_WIDTH, 2 * A_WIDTH, 3 * A_WIDTH, 3 * A_WIDTH + A_DECAY_LORA,
            3 * A_WIDTH + A_DECAY_LORA + A_ICLR_LORA], axis=-1)
    w = -jax.nn.softplus(-(w0 + jnp.tanh(wd) @ w_up)) - 0.5
    decay = jnp.exp(-jnp.exp(w.astype(f32)))
    a = jax.nn.sigmoid(a0 + ad @ a_up)
    g = jax.nn.sigmoid(gd) @ g_up
    if vres is None:
        v_first = v
    else:
        v0, v_down, v_up = vres
        v = v + (v_first - v) * jax.nn.sigmoid(v0 + (v @ v_down) @ v_up)
    kk = (k * k_k).reshape(B_, T, A_HEADS, HEAD_DIM).astype(f32)
    kk = kk / jnp.maximum(jnp.sqrt(jnp.sum(kk * kk, axis=-1, keepdims=True)), 1e-12)
    k = k * (1.0 + (a - 1.0) * k_a)
    hs = lambda t: t.reshape(B_, T, A_HEADS, HEAD_DIM).astype(f32)
    r_h, k_h, v_h, w_h, a_h = hs(r), hs(k), hs(v), hs(decay), hs(a)

    def step(S, inp):
        r_t, k_t, v_t, w_t, kk_t, a_t = inp
        sa = jnp.einsum('bhvk,bhk->bhv', S, -kk_t)
        S = (S * w_t[:, :, None, :] + sa[..., None] * (kk_t * a_t)[:, :, None, :]
             + v_t[..., None] * k_t[:, :, None, :])
        return S, jnp.einsum('bhvk,bhk->bhv', S, r_t)

    xs = tuple(jnp.moveaxis(t, 1, 0) for t in (r_h, k_h, v_h, w_h, kk, a_h))
    S0 = jnp.zeros((B_, A_HEADS, HEAD_DIM, HEAD_DIM), f32)
    _, y = lax.scan(step, S0, xs)
    y = jnp.moveaxis(y, 0, 1)
    mean = jnp.mean(y, axis=-1, keepdims=True)
    var = jnp.mean(jnp.square(y - mean), axis=-1, keepdims=True)
    y = ((y - mean) * lax.rsqrt(var + A_GN_EPS)).reshape(B_, T, A_WIDTH) * gn_g + gn_b
    bonus = (jnp.sum(r_h * k_h * r_k, axis=-1, keepdims=True) * v_h).reshape(B_, T, A_WIDTH)
    return ((y + bonus) * g).astype(p.dtype), v_first


def chunk_attention(q, k, v, q_gain, k_gain, rel_bias):
    B_, T, _ = q.shape
    NC = T // CHUNK
    q = rms_norm(q.reshape(B_, T, B_HEADS, HEAD_DIM), q_gain)
    k = rms_norm(k.reshape(B_, T, B_HEADS, HEAD_DIM), k_gain)
    v = v.reshape(B_, T, B_HEADS, HEAD_DIM)
    qc = q.reshape(B_, NC, CHUNK, B_HEADS, HEAD_DIM)
    pad = ((0, 0), (B_PREV_CHUNKS * CHUNK, 0), (0, 0), (0, 0))
    kc = jnp.pad(k, pad).reshape(B_, NC + B_PREV_CHUNKS, CHUNK, B_HEADS, HEAD_DIM)
    vc = jnp.pad(v, pad).reshape(B_, NC + B_PREV_CHUNKS, CHUNK, B_HEADS, HEAD_DIM)
    band_idx = np.arange(NC)[:, None] + np.arange(B_PREV_CHUNKS + 1)[None, :]
    kb = kc[:, band_idx].reshape(B_, NC, B_BAND, B_HEADS, HEAD_DIM)
    vb = vc[:, band_idx].reshape(B_, NC, B_BAND, B_HEADS, HEAD_DIM)
    s = jnp.einsum('bnihd,bnjhd->bhnij', qc, kb).astype(jnp.float32) * (HEAD_DIM ** -0.5)
    dist = B_PREV_CHUNKS * CHUNK + np.arange(CHUNK)[:, None] - np.arange(B_BAND)[None, :]
    rel_idx = np.clip(dist, -(CHUNK - 1), REL_MAX) + (CHUNK - 1)
    bias = rel_bias.astype(jnp.float32)[:, rel_idx]
    valid = (np.arange(NC)[:, None] + (np.arange(B_BAND) // CHUNK)[None, :]) >= B_PREV_CHUNKS
    s = jnp.where(valid[None, None, :, None, :], s + bias[:, None], NEG_INF)
    prob = jax.nn.softmax(s, axis=-1).astype(v.dtype)
    o = jnp.einsum('bhnij,bnjhd->bnihd', prob, vb)
    return o.reshape(B_, T, B_WIDTH)


def multiscale_pool(u, pool_w, pool_scale):
    B_, T, _ = u.shape
    uf = u.astype(jnp.float32).reshape(B_, T, C_GROUPS, C_GROUP_DIM)
    cs = jnp.cumsum(uf, axis=1)
    t1 = jnp.arange(1, T + 1, dtype=jnp.float32)
    outs = []
    for gi, win in enumerate(POOL_WINDOWS):
        c = cs[:, :, gi]
        prev = jnp.pad(c, ((0, 0), (win, 0), (0, 0)))[:, :T]
        cnt = jnp.minimum(t1, float(win))[None, :, None]
        outs.append((c - prev) / cnt - uf[:, :, gi])
    pooled = jnp.stack(outs, axis=2)
    y = jnp.einsum('btgc,gcd->btgd', pooled, pool_w.astype(jnp.float32)).reshape(B_, T, C_WIDTH)
    return (y * pool_scale).astype(u.dtype)


def memory_cross_attention(h, mem_n, wq, wkv, wo, q_gain, k_gain):
    B_, T, _ = h.shape
    M = mem_n.shape[1]
    q = rms_norm((h @ wq).reshape(B_, T, X_HEADS, X_HEAD_DIM), q_gain)
    k, v = jnp.split(mem_n @ wkv, 2, axis=-1)
    k = rms_norm(k.reshape(B_, M, X_HEADS, X_HEAD_DIM), k_gain)
    v = v.reshape(B_, M, X_HEADS, X_HEAD_DIM)
    s = jnp.einsum('bthd,bmhd->bhtm', q, k).astype(jnp.float32) * (X_HEAD_DIM ** -0.5)
    prob = jax.nn.softmax(s, axis=-1).astype(v.dtype)
    o = jnp.einsum('bhtm,bmhd->bthd', prob, v).reshape(B_, T, D_MODEL)
    return o @ wo


def setup_inputs(seed: int = 0) -> dict:
    key = jax.random.key(seed)
    ks = iter(jax.random.split(key, 64))
    f32 = jnp.float32
    L, D = DEPTH, D_MODEL

    def nrm(shape, scale):
        return scale * jax.random.normal(next(ks), shape, f32)

    def gain(shape, base=1.0, noise=0.1):
        return base + noise * jax.random.normal(next(ks), shape, f32)

    return {
        "x": nrm((BATCH, SEQ, D), 1.0),
        "mem": nrm((BATCH, N_MEM, D), 1.0),
        "norm_ffn1": gain((L, D)),
        "ffn1_wi": nrm((L, D, 2 * D_FF), D ** -0.5),
        "ffn1_wo": nrm((L, D_FF, D), D_FF ** -0.5),
        "norm_mix": gain((L, D)),
        "w_in": nrm((L, D, IN_PROJ), D ** -0.5),
        "w_out": nrm((L, MIX_WIDTH, D), MIX_WIDTH ** -0.5),
        "a_mu": jax.random.uniform(next(ks), (L, A_PROJ), f32),
        "a_w0": nrm((L, A_WIDTH), 0.5),
        "a_w_up": nrm((L, A_DECAY_LORA, A_WIDTH), 0.5 * A_DECAY_LORA ** -0.5),
        "a_a0": nrm((L, A_WIDTH), 0.5),
        "a_a_up": nrm((L, A_ICLR_LORA, A_WIDTH), 0.5 * A_ICLR_LORA ** -0.5),
        "a_g_up": nrm((L, A_GATE_LORA, A_WIDTH), A_GATE_LORA ** -0.5),
        "a_k_k": gain((L, A_WIDTH), 0.85, 0.05),
        "a_k_a": gain((L, A_WIDTH), 1.0, 0.05),
        "a_r_k": nrm((L, A_HEADS, HEAD_DIM), 0.1),
        "a_gn_g": gain((L, A_WIDTH)),
        "a_gn_b": nrm((L, A_WIDTH), 0.02),
        "a_v0": nrm((L - 1, A_WIDTH), 0.5),
        "a_v_down": nrm((L - 1, A_WIDTH, A_VRES_LORA), A_WIDTH ** -0.5),
        "a_v_up": nrm((L - 1, A_VRES_LORA, A_WIDTH), 0.5 * A_VRES_LORA ** -0.5),
        "b_q_gain": gain((L, HEAD_DIM)),
        "b_k_gain": gain((L, HEAD_DIM)),
        "b_rel_bias": nrm((L, B_HEADS, N_REL), 0.5),
        "c_pool_w": nrm((L, C_GROUPS, C_GROUP_DIM, C_GROUP_DIM), C_GROUP_DIM ** -0.5),
        "c_pool_scale": gain((L, C_WIDTH)),
        "norm_cross": gain((L, D)),
        "norm_mem": gain((L, D)),
        "x_wq": nrm((L, D, D), D ** -0.5),
        "x_wkv": nrm((L, D, 2 * D), D ** -0.5),
        "x_wo": nrm((L, D, D), D ** -0.5),
        "x_q_gain": gain((L, X_HEAD_DIM)),
        "x_k_gain": gain((L, X_HEAD_DIM)),
        "norm_ffn2": gain((L, D)),
        "ffn2_wi": nrm((L, D, 2 * D_FF), D ** -0.5),
        "ffn2_wo": nrm((L, D_FF, D), D_FF ** -0.5),
    }


def reference(x, mem, norm_ffn1, ffn1_wi, ffn1_wo, norm_mix, w_in, w_out,
              a_mu, a_w0, a_w_up, a_a0, a_a_up, a_g_up, a_k_k, a_k_a, a_r_k, a_gn_g, a_gn_b,
              a_v0, a_v_down, a_v_up, b_q_gain, b_k_gain, b_rel_bias, c_pool_w, c_pool_scale,
              norm_cross, norm_mem, x_wq, x_wkv, x_wo, x_q_gain, x_k_gain,
              norm_ffn2, ffn2_wi, ffn2_wo):
    split_pts = [A_PROJ, A_PROJ + B_WIDTH, A_PROJ + 2 * B_WIDTH, A_PROJ + 3 * B_WIDTH]
    v_first = None
    for l in range(DEPTH):
        x = x + 0.5 * swiglu(rms_norm(x, norm_ffn1[l]), ffn1_wi[l], ffn1_wo[l])
        h = rms_norm(x, norm_mix[l])
        p = h @ w_in[l]
        p_a, p_q, p_k, p_v, p_c = jnp.split(p, split_pts, axis=-1)
        vres = None if l == 0 else (a_v0[l - 1], a_v_down[l - 1], a_v_up[l - 1])
        y_a, v_first = rwkv7_mixer(p_a, v_first, a_mu[l], a_w0[l], a_w_up[l], a_a0[l], a_a_up[l],
                                   a_g_up[l], a_k_k[l], a_k_a[l], a_r_k[l], a_gn_g[l], a_gn_b[l], vres)
        y_b = chunk_attention(p_q, p_k, p_v, b_q_gain[l], b_k_gain[l], b_rel_bias[l])
        y_c = multiscale_pool(p_c, c_pool_w[l], c_pool_scale[l])
        x = x + jnp.concatenate([y_a, y_b, y_c], axis=-1) @ w_out[l]
        x = x + memory_cross_attention(rms_norm(x, norm_cross[l]), rms_norm(mem, norm_mem[l]),
                                       x_wq[l], x_wkv[l], x_wo[l], x_q_gain[l], x_k_gain[l])
        x = x + 0.5 * swiglu(rms_norm(x, norm_ffn2[l]), ffn2_wi[l], ffn2_wo[l])
    return x
```

```python
import math
import numpy as np
import concourse.bass as bass
import concourse.mybir as mybir
from concourse.bass_utils import run_bass_kernel_spmd
from contextlib import ExitStack

F32 = mybir.dt.float32
BF16 = mybir.dt.bfloat16
AF = mybir.ActivationFunctionType
ALU = mybir.AluOpType

D = 1024
T = 2048
L = 2
NMEM = 256
DFF = 2816
KC = D // 128
JC = DFF // 128
INP = 2688
RMS_EPS = 1e-6
GN_EPS = 64e-5
SLOT_ELEMS = 3072
NSLOT = 4
SCR_UNITS = 27920
TB = 256
NCH = TB // 64
NQT = TB // 128
SDEC = math.exp(-0.5)
NEG = -30000.0

C_ID, C_BO, C_MG, C_ML, C_PF, C_IW, C_SM, C_N = 0, 128, 256, 512, 640, 672, 704, 960
PV_FFN1, PV_MIX, PV_CROSS, PV_MEM, PV_FFN2, PV_XQ, PV_XK = 0, 8, 16, 24, 32, 40, 42
PV_MU, PV_W0, PV_A0, PV_KK, PV_KA, PV_RK, PV_GNG, PV_GNB, PV_V0, PV_BQG, PV_BKG, PV_PSC = 44, 54, 57, 60, 63, 66, 69, 72, 75, 78, 79, 80
PV_LORA, PV_VUP, PV_VDOWN, PV_PW, PV_N = 96, 480, 864, 960, 1216


class Ref:
    tok = None


def _units(ops):
    units, cur, open_, live = [], [], False, set()
    for it in ops:
        kind, args, _ = it
        cur.append(it)
        if kind == "op":
            if args[0] == "pe":
                meta = args[6]
                if meta is not None:
                    if len(meta) > 3:
                        open_ = not meta[4]
                    if meta[2] != 0:
                        live.add("P%d" % meta[2])
            else:
                for k in args[2]:
                    live.discard(k)
        if not open_ and not live:
            units.append(cur)
            cur = []
    if cur:
        units.append(cur)
    return units


def split_at_mix_write(ops):
    head, tail, hit = [], [], False
    for u in _units(ops):
        if not hit and any(k[0] == "mix" for it in u for k in it[1][3] if isinstance(k, tuple)):
            hit = True
        (tail if hit else head).extend(u)
    return head, tail


def merge(a, b):
    return interleave(a, b) if len(a) >= len(b) else interleave(b, a)


def interleave(a, b):
    ua, ub = _units(a), _units(b)
    if not ub:
        return list(a)
    out, step, nb = [], max(1, len(ua) // (len(ub) + 1)), 0
    for i, x in enumerate(ua):
        out.extend(x)
        if (i + 1) % step == 0 and nb < len(ub):
            out.extend(ub[nb])
            nb += 1
    for x in ub[nb:]:
        out.extend(x)
    return out


class Sched:
    ENG = ("pe", "act", "dve", "pool", "sp")

    def __init__(self, nc, stack, same_engine_sync=True):
        self.nc = nc
        self.stack = stack
        self.sem = {e: stack.enter_context(nc.semaphore("sem_" + e)) for e in self.ENG}
        self.count = {e: 0 for e in self.ENG}
        self.prog = {e: [] for e in self.ENG}
        self.waited = {e: {} for e in self.ENG}
        self.lastw = {}
        self.readers = {}
        self.dma_sems = {}
        self.same_engine_sync = same_engine_sync
        self.n_sem = 0
        self._rec = None
        self.pe_filler = None
        self._rec_stack = []
        self._pe_recent = []

    def sbuf(self, name, shape, dtype=F32):
        return self.stack.enter_context(self.nc.sbuf_tensor(name, list(shape), dtype))

    def psum(self, name, shape, dtype=F32):
        return self.stack.enter_context(self.nc.psum_tensor(name, list(shape), dtype))

    def _deps(self, reads, writes):
        deps = []
        for k in reads:
            t = self.lastw.get(k)
            if t is not None:
                deps.append(t)
        for k in writes:
            t = self.lastw.get(k)
            if t is not None:
                deps.append(t)
            r = self.readers.get(k)
            if r:
                deps.extend(r.values())
        return deps

    def _emit_waits(self, eng, deps):
        need = {}
        for t in deps:
            sem, val, owner = t
            if owner == eng and (eng == "pe" or not self.same_engine_sync):
                continue
            if owner is not None and val > self.count[owner]:
                raise RuntimeError("wait on a not-yet-incrementing instruction (%s waits %s>=%d)" % (eng, owner, val))
            key = sem.name
            if key not in need or need[key][1] < val:
                need[key] = (sem, val)
        w = self.waited[eng]
        first = True
        for key, (sem, val) in need.items():
            if w.get(key, 0) >= val:
                continue
            w[key] = val
            if first and eng == "pe" and self.pe_filler is not None:
                for _ in range(self.pe_filler[0]):
                    self.prog[eng].append(self.pe_filler[1])
            first = False
            self.prog[eng].append(lambda e, sem=sem, val=val: e.wait_ge(sem, val))

    def _record(self, tok, reads, writes):
        key = tok[0].name
        for k in reads:
            r = self.readers.setdefault(k, {})
            if key not in r or r[key][1] < tok[1]:
                r[key] = tok
        for k in writes:
            self.lastw[k] = tok
            self.readers[k] = {}

    def rec_begin(self):
        self._rec_stack.append(self._rec)
        self._rec = []

    def rec_end(self):
        r, self._rec = self._rec, self._rec_stack.pop()
        return r

    def _pe_guard(self, meta):
        base, n, bank = meta[0], meta[1], meta[2]
        G = frozenset(range(base // 32, (base + n + 31) // 32))
        if len(G) == 4:
            self._pe_recent = []
            return [], None
        waits = [t for (g, b, t) in self._pe_recent if not (g & G) and b == bank]
        self._pe_recent = [(g, b, t) for (g, b, t) in self._pe_recent if not (g & G)]
        return waits, G

    def replay(self, item):
        kind, args, ref = item
        ref.tok = (self.op if kind == "op" else self.dma)(*args)

    def op(self, eng, fn, reads=(), writes=(), inc=True, after=None, pe_meta=None):
        if self._rec is not None:
            ref = Ref()
            self._rec.append(("op", (eng, fn, tuple(reads), tuple(writes), inc, after, pe_meta), ref))
            return ref
        while isinstance(after, Ref):
            after = after.tok
        forced = [after] if after is not None else []
        G = None
        if pe_meta is not None:
            w, G = self._pe_guard(pe_meta)
            forced += w
            if G is not None:
                inc = True
        self._emit_waits(eng, self._deps(reads, writes))
        for (sem_a, val_a, _) in forced:
            if self.waited[eng].get(sem_a.name, 0) < val_a:
                self.waited[eng][sem_a.name] = val_a
                self.prog[eng].append(lambda e, sem_a=sem_a, val_a=val_a: e.wait_ge(sem_a, val_a))
        sem = self.sem[eng]
        if inc:
            self.count[eng] += 1
            val = self.count[eng]
            self.prog[eng].append(lambda e, fn=fn, sem=sem: fn(e).then_inc(sem, 1))
        else:
            val = self.count[eng] + 1
            self.prog[eng].append(lambda e, fn=fn: fn(e))
        tok = (sem, val, eng)
        if G is not None:
            self._pe_recent.append((G, pe_meta[2], tok))
        self._record(tok, reads, writes)
        return tok

    def dma(self, q, out, in_, reads=(), writes=(), semkey=None):
        if self._rec is not None:
            ref = Ref()
            self._rec.append(("dma", (q, out, in_, tuple(reads), tuple(writes), semkey), ref))
            return ref
        self._emit_waits(q, self._deps(reads, writes))
        if semkey is None:
            semkey = writes[0]
        if semkey not in self.dma_sems:
            self.n_sem += 1
            s = self.stack.enter_context(self.nc.semaphore("dsem%d" % self.n_sem))
            self.dma_sems[semkey] = [s, 0]
        ent = self.dma_sems[semkey]
        ent[1] += 16
        sem, val = ent[0], ent[1]
        self.prog[q].append(lambda e, out=out, in_=in_, sem=sem: e.dma_start(out=out, in_=in_).then_inc(sem, 16))
        tok = (sem, val, None)
        self._record(tok, reads, writes)
        return tok

    def wait_keys(self, eng, keys):
        self._emit_waits(eng, self._deps(keys, ()))

    def barrier(self):
        toks = [(self.sem[e], self.count[e], e) for e in self.ENG if self.count[e] > 0]
        toks += [(s, v, None) for (s, v) in self.dma_sems.values()]
        for e in self.ENG:
            if e == "pool":
                continue
            self._emit_waits(e, [t for t in toks if t[2] != e])

    def emit(self):
        with self.nc.Block() as block:
            @block.sync
            def _(e):
                for f in self.prog["sp"]:
                    f(e)

            @block.gpsimd
            def _(e):
                for f in self.prog["pool"]:
                    f(e)

            @block.scalar
            def _(e):
                for f in self.prog["act"]:
                    f(e)

            @block.vector
            def _(e):
                for f in self.prog["dve"]:
                    f(e)

            @block.tensor
            def _(e):
                for f in self.prog["pe"]:
                    f(e)


class MK:
    def __init__(self, nc, st, cfg):
        self.nc = nc
        self.cfg = cfg
        S = self.S = Sched(nc, st, same_engine_sync=cfg.get("same_engine_sync", True))
        dt = nc.dram_tensor

        def din(name, shape):
            return dt(name, list(shape), F32, kind="ExternalInput").ap()

        self.d_xT = din("xT", [D, T])
        self.d_memT = din("memT", [D, NMEM])
        self.d_cst = din("cst", [128, C_N])
        self.d_lyr = din("lyr", [L, 128, PV_N])
        self.d_bias = din("biasT", [L, 128, 6 * 640])
        self.d_ffn_wi = [din("ffn1_wi", [L, D, 2 * DFF]), din("ffn2_wi", [L, D, 2 * DFF])]
        self.d_ffn_wo = [din("ffn1_wo", [L, DFF, D]), din("ffn2_wo", [L, DFF, D])]
        self.d_win = din("w_in", [L, D, INP])
        self.d_wout = din("w_out", [L, D, D])
        self.d_xwq = din("x_wq", [L, D, D])
        self.d_xwkv = din("x_wkv", [L, D, 2 * D])
        self.d_xwo = din("x_wo", [L, D, D])
        self.d_out = dt("outT", [D, T], F32, kind="ExternalOutput").ap()
        self.d_vf = dt("vfirst", [3, 128, T], F32).ap()

        self.XT = S.sbuf("XT", [128, KC, T], F32)
        self.WS = S.sbuf("WS", [128, NSLOT, SLOT_ELEMS], BF16)
        self.slot_i = 0
        self.SCR = S.sbuf("SCR", [128, SCR_UNITS], F32)
        self.CST = S.sbuf("CST", [128, C_N], F32)
        self.PV = S.sbuf("LYR", [128, PV_N], F32)
        self.ones_bf = S.sbuf("ones_bf", [128, 128], BF16)
        self.bo_bf = S.sbuf("bo_bf", [128, 128], BF16)
        self.HS = S.sbuf("HS", [128, 3, 64], F32)
        self.CARRY = S.sbuf("CARRY", [128, 10], F32)
        self.UHIST = S.sbuf("UHIST", [128, 2, 16], F32)
        self.P = [S.psum("P%d" % i, [128, 512], F32) for i in range(8)]
        self.cur_layer = -1

        S.dma("sp", self.CST[:], self.d_cst, writes=["cst"])
        S.op("dve", lambda e: e.memset(self.ones_bf[:], 1.0), writes=["ones_bf"])
        S.op("dve", lambda e: e.tensor_copy(self.bo_bf[:], self.CST[:, C_BO:C_BO + 128]), reads=["cst"], writes=["bo_bf"])
        for kc in range(KC):
            for t8 in range(0, 8, 2):
                S.dma("sp", self.XT[:, kc, t8 * 256:(t8 + 2) * 256],
                      self.d_xT[kc * 128:(kc + 1) * 128, t8 * 256:(t8 + 2) * 256],
                      writes=[("x", kc, t8), ("x", kc, t8 + 1)], semkey=("xin", t8))
        for t8 in range(0, 8, 2):
            last = S.lastw[("x", KC - 1, t8)]
            for kc in range(KC):
                S.lastw[("x", kc, t8)] = last
                S.lastw[("x", kc, t8 + 1)] = last

    def xk(self, kc, t0, n):
        return [("x", kc, t) for t in range(t0 // 256, (t0 + n + 255) // 256)]

    def hk(self, kc, off, n):
        return [("hn", kc, t) for t in range(off // 256, (off + n + 255) // 256)]

    def slot(self):
        g = getattr(self, "slot_group", None)
        if g is not None:
            self.slot_gi = getattr(self, "slot_gi", {})
            k = self.slot_gi.get(g, 0)
            self.slot_gi[g] = k + 1
            return g[k % len(g)]
        i = self.slot_i
        self.slot_i = (self.slot_i + 1) % NSLOT
        return i

    def carve_reset(self):
        self.S.barrier()
        self.co = 0

    def carve(self, n, dtype=F32, inner=None):
        ap = self.SCR[:, self.co:self.co + n]
        self.co += n
        assert self.co <= SCR_UNITS, self.co
        if dtype == BF16:
            ap = ap.bitcast(BF16)
        if inner:
            ap = ap.rearrange("p (a b) -> p a b", b=inner)
        return ap

    def load_layer(self, l):
        if self.cur_layer != l:
            self.S.dma("sp", self.PV[:], self.d_lyr[l], writes=["pv"])
            self.cur_layer = l

    def mm(self, out, lhsT, rhs, start, stop, reads, writes, inc=None, after=None):
        bank = int(writes[0][1:])
        meta = (lhsT.base_partition(), lhsT.shape[0], bank, start, stop)
        return self.S.op("pe", lambda e: e.matmul(out, lhsT, rhs, start=start, stop=stop),
                         reads=reads, writes=writes, inc=(stop if inc is None else inc), after=after, pe_meta=meta)

    def load_w_piece(self, wsrc, c0, ncols, nk=KC):
        si = self.slot()
        wsl = self.WS[:, si, 0:nk * ncols].rearrange("p (kc c) -> p kc c", c=ncols)
        self.S.dma("pool", wsl, wsrc[:, :, c0:c0 + ncols], writes=[("ws", si)], semkey=("ws", si))
        return wsl, si

    def rmsnorm(self, pvcol, t0, n, HN, hoff, scr):
        S = self.S
        ps = self.P[6]
        for kc in range(KC):
            b = kc % 2
            sq = scr["sq"][b][:, 0:n]
            S.op("act", lambda e, sq=sq, kc=kc: e.activation(sq, self.XT[:, kc, t0:t0 + n], AF.Square),
                 reads=self.xk(kc, t0, n), writes=[("sq", b)])
            self.mm(ps[:, 0:n], self.ones_bf[:], sq, kc == 0, kc == KC - 1,
                    reads=[("sq", b), "ones_bf"], writes=["P6"], inc=True)
        rt = scr["rt"][:, 0:n]
        S.op("act", lambda e: e.activation(rt, ps[:, 0:n], AF.Sqrt, bias=RMS_EPS, scale=1.0 / D),
             reads=["P6"], writes=["rt"])
        S.op("dve", lambda e: e.reciprocal(rt, rt), reads=["rt"], writes=["rt"])
        for kc in range(KC):
            S.op("dve", lambda e, kc=kc: e.scalar_tensor_tensor(
                HN[:, kc, hoff:hoff + n], self.XT[:, kc, t0:t0 + n],
                self.PV[:, pvcol + kc:pvcol + kc + 1], rt, ALU.mult, ALU.mult),
                reads=self.xk(kc, t0, n) + ["pv", "rt"], writes=self.hk(kc, hoff, n))

    def ffn(self, l, which):
        S = self.S
        self.carve_reset()
        self.load_layer(l)
        wi = self.d_ffn_wi[which][l].rearrange("(kc p) c -> p kc c", p=128)
        wo = self.d_ffn_wo[which][l].rearrange("(kc p) c -> p kc c", p=128)
        pvcol = PV_FFN1 if which == 0 else PV_FFN2
        H = self.carve(11264, BF16, 1024)
        HN = self.carve(4096, BF16, 1024)
        sqb = self.carve(512, BF16)
        scr = {"sq": [sqb[:, 0:512], sqb[:, 512:1024]], "rt": self.carve(512)}
        sg = [self.carve(512), self.carve(512)]
        for hb in range(2):
            for t2 in range(2):
                self.rmsnorm(pvcol, hb * 1024 + t2 * 512, 512, HN, t2 * 512, scr)
            for j in range(JC):
                si = self.slot()
                wsl = self.WS[:, si, 0:2048].rearrange("p (g kc c) -> p g kc c", g=2, kc=KC)
                S.dma("pool", wsl[:, 0], wi[:, :, j * 128:(j + 1) * 128], writes=[("ws", si)], semkey=("ws", si))
                S.dma("pool", wsl[:, 1], wi[:, :, DFF + j * 128:DFF + (j + 1) * 128], writes=[("ws", si)], semkey=("ws", si))
                for t2 in range(2):
                    b = (j * 2 + t2) % 2
                    pg, pu = self.P[b], self.P[2 + b]
                    for kc in range(KC):
                        self.mm(pg[:], wsl[:, 0, kc, :], HN[:, kc, t2 * 512:(t2 + 1) * 512], kc == 0, kc == KC - 1,
                                reads=[("ws", si)] + self.hk(kc, t2 * 512, 512), writes=["P%d" % b])
                    for kc in range(KC):
                        self.mm(pu[:], wsl[:, 1, kc, :], HN[:, kc, t2 * 512:(t2 + 1) * 512], kc == 0, kc == KC - 1,
                                reads=[("ws", si)] + self.hk(kc, t2 * 512, 512), writes=["P%d" % (2 + b)])
                    S.op("act", lambda e, b=b, pg=pg: e.activation(sg[b], pg[:], AF.Silu),
                         reads=["P%d" % b], writes=[("sg", b)])
                    S.op("dve", lambda e, b=b, pu=pu, j=j, t2=t2: e.tensor_tensor(
                        H[:, j, t2 * 512:(t2 + 1) * 512], sg[b], pu[:], ALU.mult),
                        reads=[("sg", b), "P%d" % (2 + b)], writes=[("H", j, t2)])
            for m in range(KC):
                wsl, si = self.load_w_piece(wo, m * 128, 128, nk=JC)
                for t2 in range(2):
                    b = (m * 2 + t2) % 2
                    po = self.P[4 + b]
                    t0 = hb * 1024 + t2 * 512
                    for j in range(JC):
                        self.mm(po[:], wsl[:, j, :], H[:, j, t2 * 512:(t2 + 1) * 512], j == 0, j == JC - 1,
                                reads=[("ws", si), ("H", j, t2)], writes=["P%d" % (4 + b)])
                    S.op("dve", lambda e, po=po, m=m, t0=t0: e.scalar_tensor_tensor(
                        self.XT[:, m, t0:t0 + 512], po[:], 0.5, self.XT[:, m, t0:t0 + 512], ALU.mult, ALU.add),
                        reads=["P%d" % (4 + b)] + self.xk(m, t0, 512), writes=self.xk(m, t0, 512))

    def headnorm(self, src, dst, chunks, n, gcols, scr, srckeys, dstkeys):
        S = self.S
        ps = self.P[6]
        nc_ = len(chunks)
        for i, c in enumerate(chunks):
            b = i % 2
            sq = scr["sq"][b][:, 0:n]
            S.op("act", lambda e, sq=sq, c=c: e.activation(sq, src[:, c, 0:n], AF.Square),
                 reads=[srckeys[i]], writes=[("sq", b)])
            self.mm(ps[:, 0:n], self.ones_bf[:], sq, i == 0, i == nc_ - 1,
                    reads=[("sq", b), "ones_bf"], writes=["P6"], inc=True)
        rt = scr["rt"][:, 0:n]
        S.op("act", lambda e: e.activation(rt, ps[:, 0:n], AF.Sqrt, bias=RMS_EPS, scale=1.0 / (128 * nc_)),
             reads=["P6"], writes=["rt"])
        S.op("dve", lambda e: e.reciprocal(rt, rt), reads=["rt"], writes=["rt"])
        for i, c in enumerate(chunks):
            S.op("dve", lambda e, i=i, c=c: e.scalar_tensor_tensor(
                dst[:, c, 0:n], src[:, c, 0:n], self.PV[:, gcols + i:gcols + i + 1], rt, ALU.mult, ALU.mult),
                reads=[srckeys[i], "pv", "rt"], writes=[dstkeys[i]])

    def cross(self, l):
        S = self.S
        self.carve_reset()
        self.load_layer(l)
        wq = self.d_xwq[l].rearrange("(kc p) c -> p kc c", p=128)
        wkv = self.d_xwkv[l].rearrange("(kc p) c -> p kc c", p=128)
        wo = self.d_xwo[l].rearrange("(kc p) c -> p kc c", p=128)
        KT = self.carve(1024, BF16, 256)
        V = self.carve(1024, BF16, 1024)
        memn = self.carve(1024, BF16, 256)
        save = self.co
        Kf = self.carve(2048, F32, 256)
        MEMT = self.carve(2048, F32, 256)
        self.co = save
        qf = self.carve(4096, F32, 512)
        qn = self.carve(2048, BF16, 512)
        E = self.carve(1024, BF16, 512)
        ob = self.carve(2048, BF16, 512)
        HN = self.carve(2048, BF16, 512)
        sqb = self.carve(512, BF16)
        scr = {"sq": [sqb[:, 0:512], sqb[:, 512:1024]], "rt": self.carve(512)}
        rden = [self.carve(512), self.carve(512)]
        S.dma("sp", MEMT, self.d_memT.rearrange("(kc p) m -> p kc m", p=128), writes=["memT"])
        ps = self.P[6]
        for kc in range(KC):
            b = kc % 2
            sq = scr["sq"][b][:, 0:NMEM]
            S.op("act", lambda e, sq=sq, kc=kc: e.activation(sq, MEMT[:, kc, :], AF.Square),
                 reads=["memT"], writes=[("sq", b)])
            self.mm(ps[:, 0:NMEM], self.ones_bf[:], sq, kc == 0, kc == KC - 1,
                    reads=[("sq", b), "ones_bf"], writes=["P6"], inc=True)
        rt = scr["rt"][:, 0:NMEM]
        S.op("act", lambda e: e.activation(rt, ps[:, 0:NMEM], AF.Sqrt, bias=RMS_EPS, scale=1.0 / D),
             reads=["P6"], writes=["rt"])
        S.op("dve", lambda e: e.reciprocal(rt, rt), reads=["rt"], writes=["rt"])
        for kc in range(KC):
            S.op("dve", lambda e, kc=kc: e.scalar_tensor_tensor(
                memn[:, kc, :], MEMT[:, kc, :], self.PV[:, PV_MEM + kc:PV_MEM + kc + 1], rt, ALU.mult, ALU.mult),
                reads=["memT", "pv", "rt"], writes=[("memn", kc)])
        for pc in range(4):
            wsl, si = self.load_w_piece(wkv, pc * 256, 256)
            for mi in range(2):
                m = pc * 2 + mi
                pp = self.P[m % 2]
                for kc in range(KC):
                    self.mm(pp[:, 0:NMEM], wsl[:, kc, mi * 128:(mi + 1) * 128], memn[:, kc, :], kc == 0, kc == KC - 1,
                            reads=[("ws", si), ("memn", kc)], writes=["P%d" % (m % 2)])
                S.op("act", lambda e, m=m, pp=pp: e.copy(Kf[:, m, :], pp[:, 0:NMEM]), reads=["P%d" % (m % 2)], writes=[("Kf", m)])
        for h in range(4):
            self.headnorm(Kf, KT, [2 * h, 2 * h + 1], NMEM, PV_XK, scr,
                          [("Kf", 2 * h), ("Kf", 2 * h + 1)], [("KT", 2 * h), ("KT", 2 * h + 1)])
        for pc in range(4):
            wsl, si = self.load_w_piece(wkv, D + pc * 256, 256)
            for mt in range(2):
                pp = self.P[(pc * 2 + mt) % 2]
                for kc in range(KC):
                    self.mm(pp[:, 0:256], memn[:, kc, mt * 128:(mt + 1) * 128], wsl[:, kc, :], kc == 0, kc == KC - 1,
                            reads=[("ws", si), ("memn", kc)], writes=["P%d" % ((pc * 2 + mt) % 2)])
                S.op("act", lambda e, pp=pp, mt=mt, pc=pc: e.copy(V[:, mt, pc * 256:(pc + 1) * 256], pp[:, 0:256]),
                     reads=["P%d" % ((pc * 2 + mt) % 2)], writes=[("V", mt, pc)])
        qf_guard = [("Kf", m) for m in range(8)] + ["memT"]
        for tb in range(4):
            t0 = tb * 512
            self.rmsnorm(PV_CROSS, t0, 512, HN, 0, scr)
            for pc in range(4):
                wsl, si = self.load_w_piece(wq, pc * 256, 256)
                for mi in range(2):
                    m = pc * 2 + mi
                    pp = self.P[m % 2]
                    for kc in range(KC):
                        self.mm(pp[:], wsl[:, kc, mi * 128:(mi + 1) * 128], HN[:, kc, 0:512], kc == 0, kc == KC - 1,
                                reads=[("ws", si)] + self.hk(kc, 0, 512), writes=["P%d" % (m % 2)])
                    S.op("act", lambda e, m=m, pp=pp: e.copy(qf[:, m, :], pp[:]), reads=["P%d" % (m % 2)],
                         writes=[("qf", m)] + (qf_guard if tb == 0 else []))
            for h in range(4):
                self.headnorm(qf, qn, [2 * h, 2 * h + 1], 512, PV_XQ, scr,
                              [("qf", 2 * h), ("qf", 2 * h + 1)], [("qn", 2 * h), ("qn", 2 * h + 1)])
            for h in range(4):
                eb = h % 2
                for mt in range(2):
                    pp = self.P[2 + mt]
                    for c in range(2):
                        self.mm(pp[:], KT[:, 2 * h + c, mt * 128:(mt + 1) * 128], qn[:, 2 * h + c, :], c == 0, c == 1,
                                reads=[("KT", 2 * h + c), ("qn", 2 * h + c)], writes=["P%d" % (2 + mt)])
                    S.op("act", lambda e, pp=pp, eb=eb, mt=mt: e.activation(E[:, eb * 2 + mt, :], pp[:], AF.Exp, scale=1.0 / 16.0),
                         reads=["P%d" % (2 + mt)], writes=[("E", eb, mt)])
                for c in range(2):
                    pp = self.P[4 + c]
                    for mt in range(2):
                        self.mm(pp[:], V[:, mt, h * 256 + c * 128:h * 256 + (c + 1) * 128], E[:, eb * 2 + mt, :], mt == 0, mt == 1,
                                reads=[("V", mt, h), ("E", eb, mt)], writes=["P%d" % (4 + c)])
                pd = self.P[7]
                for mt in range(2):
                    self.mm(pd[:], self.ones_bf[:], E[:, eb * 2 + mt, :], mt == 0, mt == 1,
                            reads=["ones_bf", ("E", eb, mt)], writes=["P7"])
                S.op("dve", lambda e, eb=eb: e.reciprocal(rden[eb], pd[:]), reads=["P7"], writes=[("rden", eb)])
                for c in range(2):
                    pp = self.P[4 + c]
                    S.op("dve", lambda e, pp=pp, c=c, h=h, eb=eb: e.tensor_tensor(ob[:, 2 * h + c, :], pp[:], rden[eb], ALU.mult),
                         reads=["P%d" % (4 + c), ("rden", eb)], writes=[("ob", 2 * h + c)])
            for pc in range(4):
                wsl, si = self.load_w_piece(wo, pc * 256, 256)
                for mi in range(2):
                    m = pc * 2 + mi
                    pp = self.P[m % 2]
                    for kc in range(KC):
                        self.mm(pp[:], wsl[:, kc, mi * 128:(mi + 1) * 128], ob[:, kc, :], kc == 0, kc == KC - 1,
                                reads=[("ws", si), ("ob", kc)], writes=["P%d" % (m % 2)])
                    S.op("dve", lambda e, m=m, pp=pp, t0=t0: e.tensor_tensor(
                        self.XT[:, m, t0:t0 + 512], pp[:], self.XT[:, m, t0:t0 + 512], ALU.add),
                        reads=["P%d" % (m % 2)] + self.xk(m, t0, 512), writes=self.xk(m, t0, 512))

    def mixer(self, l):
        S = self.S
        mixers = self.cfg.get("mixers", "abc")
        self.carve_reset()
        self.load_layer(l)
        c = self.carve
        M = self.M = {}
        M["HN"] = c(1024, BF16, TB)
        M["KB"] = c(1152, BF16, 768)
        M["VB"] = c(1152, BF16, 384)
        M["BIAS"] = c(1920, BF16, 640)
        M["MIX"] = c(1024, BF16, TB)
        sqb = c(256, BF16)
        M["scr"] = {"sq": [sqb[:, 0:TB], sqb[:, TB:2 * TB]], "rt": c(256)}
        u0 = self.co
        raw = c(260)
        M["RAW"] = [raw, raw]
        M["TMP"] = c(TB)
        M["TMP2"] = c(TB)
        M["XS"] = c(9 * TB, F32, TB)
        for nm in ("TG", "CUM", "A", "KK", "B", "EX1", "EX2", "YS", "YC", "VD", "VF", "PT1", "PT2"):
            M[nm] = c(TB)
        for nm in ("G", "BON"):
            M[nm] = [c(TB), c(TB)]
        for nm in ("AR", "BK", "BKH", "VV"):
            M[nm] = [c(NCH * 128).rearrange("p (c s t) -> p c s t", s=2, t=64) for _ in range(2)]
        M["GT"] = [c(NCH * 256).rearrange("p (h c x) -> p h c x", h=2, x=128) for _ in range(2)]
        M["X"] = [c(NCH * 64, BF16, 64), c(NCH * 64, BF16, 64)]
        M["XTb"] = [c(NCH * 64, BF16, 64), c(NCH * 64, BF16, 64)]
        M["XT0b"] = c(NCH * 64, BF16, 64)
        M["TTb"] = c(NCH * 64, BF16, 64)
        M["TT"] = [c(NCH * 128, F32, 64), c(NCH * 128, F32, 64)]
        M["BKHT"] = c(NCH * 128, F32, 64)
        M["UV"] = c(NCH * 128, F32, 64)
        M["Z"] = c(128)
        M["WC"] = [c(16), c(16)]
        M["QF"] = c(TB)
        M["QN"] = c(3 * TB // 2, BF16, TB)
        sbt = c(640)
        M["SBt"] = [sbt, sbt]
        M["PT"] = [c(320, BF16), c(320, BF16)]
        M["rden"] = [c(128), c(128)]
        W = TB + 16
        M["LV"] = [c(2 * W, F32, W) for _ in range(3)]
        M["POOLED"] = M["LV"][0][:, :, 16:W]
        btmp = self.SCR[:, u0:u0 + 3840].rearrange("p (h x) -> p h x", x=640)
        S.dma("sp", btmp, self.d_bias[l].rearrange("p (h x) -> p h x", x=640), writes=["btmp"])
        S.op("act", lambda e: e.copy(M["BIAS"], btmp), reads=["btmp"], writes=["bias"])
        S.barrier()
        S.op("dve", lambda e: e.memset(self.HS[:], 0.0), reads=[], writes=[("HS", 0), ("HS", 1), ("HS", 2)])
        S.op("dve", lambda e: e.memset(self.CARRY[:], 0.0), writes=["carry"])
        S.op("dve", lambda e: e.memset(self.UHIST[:], 0.0), writes=["uhist"])
        S.op("dve", lambda e: e.memset(M["MIX"][:], 0.0), writes=[("mix", k) for k in range(8)])
        winT = self.d_win[l].rearrange("(kc p) c -> p kc c", p=128)
        woutT = self.d_wout[l].rearrange("(kc p) c -> p kc c", p=128)
        ro_prev = []
        nfill = self.cfg.get("pe_fill", 2)
        if nfill:
            S.pe_filler = (nfill, lambda e: e.matmul(self.P[7][:, 0:128], self.ones_bf[:, 0:128], self.ones_bf[:, 0:128], start=True, stop=True))
        for tb in range(self.cfg.get("nblocks", T // TB)):
            t0 = tb * TB
            self.rmsnorm(PV_MIX, t0, TB, M["HN"], 0, M["scr"])
            ra, rb = [], []
            if "a" in mixers:
                self.slot_group = (0, 1)
                S.rec_begin()
                self.rwkv_block(l, tb, winT)
                ra = S.rec_end()
            self.slot_group = (2, 3)
            S.rec_begin()
            if "c" in mixers:
                self.pool_block(l, tb, winT)
            if "b" in mixers:
                self.attn_block(l, tb, winT)
            rb = S.rec_end()
            self.slot_group = None
            ra_h, ra_t = split_at_mix_write(ra)
            rb_h, rb_t = split_at_mix_write(rb)
            merged = merge(ra_h, ro_prev + rb_h) + merge(ra_t, rb_t)
            for it in merged:
                S.replay(it)
            self.slot_group = (2, 3)
            S.rec_begin()
            for pc in range(4):
                wsl, si = self.load_w_piece(woutT, pc * 256, 256)
                for mi in range(2):
                    m = pc * 2 + mi
                    bk = (6, 1)[m % 2]
                    pp = self.P[bk]
                    for kc in range(KC):
                        self.mm(pp[:, 0:TB], wsl[:, kc, mi * 128:(mi + 1) * 128], M["MIX"][:, kc, :], kc == 0, kc == KC - 1,
                                reads=[("ws", si), ("mix", kc)], writes=["P%d" % bk])
                    S.op("dve", lambda e, m=m, pp=pp, t0=t0: e.tensor_tensor(
                        self.XT[:, m, t0:t0 + TB], pp[:, 0:TB], self.XT[:, m, t0:t0 + TB], ALU.add),
                        reads=["P%d" % bk] + self.xk(m, t0, TB), writes=self.xk(m, t0, TB))
            ro_prev = S.rec_end()
            self.slot_group = None
        for it in ro_prev:
            S.replay(it)
        S.pe_filler = None

    def proj_fm(self, wsl, si, coff, bank, n=TB):
        pp = self.P[bank]
        for kc in range(KC):
            self.mm(pp[:, 0:n], wsl[:, kc, coff:coff + 128], self.M["HN"][:, kc, 0:n], kc == 0, kc == KC - 1,
                    reads=[("ws", si)] + self.hk(kc, 0, n), writes=["P%d" % bank])
        return pp

    def rwkv_block(self, l, tb, winT):
        S, M, PV = self.S, self.M, self.PV
        t0 = tb * TB
        XS, TMP = M["XS"], M["TMP"]
        pieces = [(0, 384), (384, 384), (768, 384), (1152, 128)]
        cur = None
        for q in range(10):
            pi, coff = (q // 3, (q % 3) * 128) if q < 9 else (3, 0)
            if cur is None or cur[0] != pi:
                wsl, si = self.load_w_piece(winT, pieces[pi][0], pieces[pi][1])
                cur = (pi, wsl, si)
            bank = q % 2
            pp = self.proj_fm(cur[1], cur[2], coff, bank)
            RAW = M["RAW"][q % 2]
            rk = ("raw", 0)
            S.op("dve", lambda e, RAW=RAW, q=q: e.tensor_copy(RAW[:, 0:1], self.CARRY[:, q:q + 1]), reads=["carry"], writes=[rk])
            S.op("act", lambda e, RAW=RAW, pp=pp: e.copy(RAW[:, 1:TB + 1], pp[:, 0:TB]), reads=["P%d" % bank], writes=[rk])
            S.op("dve", lambda e, RAW=RAW, q=q: e.tensor_copy(self.CARRY[:, q:q + 1], RAW[:, TB:TB + 1]), reads=[rk], writes=["carry"])
            S.op("dve", lambda e, RAW=RAW: e.tensor_tensor(TMP, RAW[:, 0:TB], RAW[:, 1:TB + 1], ALU.subtract), reads=[rk], writes=["TMP"])
            dst = XS[:, q, :] if q < 9 else M["TG"]
            S.op("dve", lambda e, RAW=RAW, q=q, dst=dst: e.scalar_tensor_tensor(
                dst, TMP, PV[:, PV_MU + q:PV_MU + q + 1], RAW[:, 1:TB + 1], ALU.mult, ALU.add),
                reads=["TMP", rk, "pv"], writes=[("xs", q) if q < 9 else "TG"])
        TG = M["TG"]
        S.op("act", lambda e: e.activation(TG[0:32, :], TG[0:32, :], AF.Tanh), reads=["TG"], writes=["TG"])
        S.op("act", lambda e: e.activation(TG[64:128, :], TG[64:128, :], AF.Sigmoid), reads=["TG"], writes=["TG"])
        if l == 0:
            for hp in range(3):
                S.dma("sp", self.d_vf[hp][:, t0:t0 + TB], XS[:, 6 + hp, :], reads=[("xs", 6 + hp)], writes=[("vf", hp, tb)], semkey=("vfst", hp))
        else:
            pvd = self.P[5]
            for hp in range(3):
                self.mm(pvd[0:32, 0:TB], PV[:, PV_VDOWN + hp * 32:PV_VDOWN + (hp + 1) * 32], XS[:, 6 + hp, :], hp == 0, hp == 2,
                        reads=["pv", ("xs", 6 + hp)], writes=["P5"])
            VD, VF = M["VD"], M["VF"]
            S.op("act", lambda e: e.copy(VD[0:32, :], pvd[0:32, 0:TB]), reads=["P5"], writes=["VD"])
            for hp in range(3):
                self.mm(pvd[:, 0:TB], PV[0:32, PV_VUP + hp * 128:PV_VUP + (hp + 1) * 128], VD[0:32, :], True, True,
                        reads=["pv", "VD"], writes=["P5"])
                S.op("act", lambda e, hp=hp: e.activation(TMP, pvd[:, 0:TB], AF.Sigmoid, bias=PV[:, PV_V0 + hp:PV_V0 + hp + 1]),
                     reads=["P5", "pv"], writes=["TMP"])
                S.dma("sp", VF, self.d_vf[hp][:, t0:t0 + TB], reads=[("vf", hp, tb)], writes=["VF"])
                S.op("dve", lambda e, hp=hp: e.tensor_tensor(VF, VF, XS[:, 6 + hp, :], ALU.subtract), reads=["VF", ("xs", 6 + hp)], writes=["VF"])
                S.op("dve", lambda e: e.tensor_tensor(VF, VF, TMP, ALU.mult), reads=["VF", "TMP"], writes=["VF"])
                S.op("dve", lambda e, hp=hp: e.tensor_tensor(XS[:, 6 + hp, :], XS[:, 6 + hp, :], VF, ALU.add),
                     reads=["VF", ("xs", 6 + hp)], writes=[("xs", 6 + hp)])
        def rec(fn, *a):
            S.rec_begin()
            fn(*a)
            return S.rec_end()
        for it in rec(self.rwkv_prep, l, tb, 0) + rec(self.rwkv_P, l, tb, 0):
            S.replay(it)
        for hp in range(3):
            main = rec(self.rwkv_TS, l, tb, hp) + rec(self.rwkv_post, l, tb, hp)
            nxt = (rec(self.rwkv_prep, l, tb, hp + 1) + rec(self.rwkv_P, l, tb, hp + 1)) if hp < 2 else []
            for it in merge(main, nxt):
                S.replay(it)

    def rwkv_prep(self, l, tb, hp):
        S, M, PV, CST = self.S, self.M, self.PV, self.CST
        pb = hp % 2
        XS, TMP, TMP2, TG = M["XS"], M["TMP"], M["TMP2"], M["TG"]
        CUM, A_, KK, B_, EX1, EX2 = (M[k] for k in ("CUM", "A", "KK", "B", "EX1", "EX2"))
        G, BON, AR, BK, BKH, VV, WC = (M[k][pb] for k in ("G", "BON", "AR", "BK", "BKH", "VV", "WC"))
        kG, kBON, kAR, kBK, kBKH, kVV, kWC = (("G", pb), ("BON", pb), ("AR", pb), ("BK", pb), ("BKH", pb), ("VV", pb), ("WC", pb))
        bo = CST[:, C_BO:C_BO + 128]
        r, k, v = XS[:, hp, :], XS[:, 3 + hp, :], XS[:, 6 + hp, :]
        rk_, kk_, vk_ = ("xs", hp), ("xs", 3 + hp), ("xs", 6 + hp)
        lo = PV_LORA + hp * 128
        P6 = self.P[6][:, 0:TB]

        def pcol(base):
            return PV[:, base + hp:base + hp + 1]

        def c4(ap):
            return ap.rearrange("p (c t) -> p c t", t=64)
        self.mm(P6, PV[0:32, lo:lo + 128], TG[0:32, :], True, True, reads=["pv", "TG"], writes=["P6"])
        S.op("act", lambda e: e.activation(CUM, P6, AF.Sigmoid, bias=pcol(PV_W0)), reads=["P6", "pv"], writes=["E"])
        self.mm(P6, PV[32:64, lo:lo + 128], TG[32:64, :], True, True, reads=["pv", "TG"], writes=["P6"])
        S.op("act", lambda e: e.activation(A_, P6, AF.Sigmoid, bias=pcol(PV_A0)), reads=["P6", "pv"], writes=["A"])
        self.mm(P6, PV[64:128, lo:lo + 128], TG[64:128, :], True, True, reads=["pv", "TG"], writes=["P6"])
        S.op("act", lambda e: e.copy(G, P6), reads=["P6"], writes=[kG])
        S.op("act", lambda e: e.activation(TMP, k, AF.Square, scale=pcol(PV_KK)), reads=[kk_, "pv"], writes=["TMP"])
        self.mm(P6, bo, TMP, True, True, reads=["cst", "TMP"], writes=["P6"])
        S.op("act", lambda e: e.activation(TMP2, P6, AF.Sqrt), reads=["P6"], writes=["TMP2"])
        S.op("dve", lambda e: e.tensor_scalar(TMP2, TMP2, 1e-12, None, ALU.max), reads=["TMP2"], writes=["TMP2"])
        S.op("dve", lambda e: e.reciprocal(TMP2, TMP2), reads=["TMP2"], writes=["TMP2"])
        S.op("dve", lambda e: e.scalar_tensor_tensor(KK, k, pcol(PV_KK), TMP2, ALU.mult, ALU.mult), reads=[kk_, "pv", "TMP2"], writes=["KK"])
        S.op("dve", lambda e: e.tensor_scalar(TMP, A_, -1.0, pcol(PV_KA), ALU.add, ALU.mult), reads=["A", "pv"], writes=["TMP"])
        S.op("dve", lambda e: e.scalar_tensor_tensor(k, TMP, 1.0, k, ALU.add, ALU.mult), reads=["TMP", kk_], writes=[kk_])
        S.op("dve", lambda e: e.tensor_tensor(B_, KK, A_, ALU.mult), reads=["KK", "A"], writes=["B"])
        S.op("dve", lambda e: e.scalar_tensor_tensor(TMP, r, pcol(PV_RK), k, ALU.mult, ALU.mult), reads=[rk_, kk_, "pv"], writes=["TMP"])
        self.mm(P6, bo, TMP, True, True, reads=["cst", "TMP"], writes=["P6"])
        S.op("dve", lambda e: e.tensor_tensor(BON, P6, v, ALU.mult), reads=["P6", vk_], writes=[kBON])
        S.op("act", lambda e: e.copy(TMP2, CUM), reads=["E"], writes=["TMP2"])
        smask = CST[:, C_SM:C_SM + TB]
        S.op("dve", lambda e: e.tensor_tensor_scan(CUM, smask, TMP2, 0.0, ALU.mult, ALU.add),
             reads=["TMP2", "cst", "E"], writes=["E"])
        S.op("act", lambda e: e.activation(EX1, CUM, AF.Exp, scale=-SDEC), reads=["E"], writes=["EX1"])
        S.op("dve", lambda e: e.tensor_tensor(AR[:, :, 1, :], c4(r), c4(EX1), ALU.mult), reads=[rk_, "EX1"], writes=[kAR])
        S.op("act", lambda e: e.activation(EX2, CUM, AF.Exp, scale=SDEC), reads=["E"], writes=["EX2"])
        S.op("dve", lambda e: e.tensor_tensor(BK[:, :, 0, :], c4(B_), c4(EX2), ALU.mult), reads=["B", "EX2"], writes=[kBK])
        S.op("dve", lambda e: e.tensor_tensor(BK[:, :, 1, :], c4(k), c4(EX2), ALU.mult), reads=[kk_, "EX2"], writes=[kBK])
        S.op("dve", lambda e: e.tensor_tensor(TMP, CUM, TMP2, ALU.subtract), reads=["E", "TMP2"], writes=["TMP"])
        S.op("act", lambda e: e.activation(EX1, TMP, AF.Exp, scale=-SDEC), reads=["TMP"], writes=["EX1"])
        S.op("dve", lambda e: e.scalar_tensor_tensor(AR[:, :, 0, :], c4(KK), -1.0, c4(EX1), ALU.mult, ALU.mult), reads=["KK", "EX1"], writes=[kAR])
        S.op("dve", lambda e: e.tensor_tensor(c4(TMP), c4(CUM), c4(CUM)[:, :, 63:64].to_broadcast([128, NCH, 64]), ALU.subtract),
             reads=["E", "EX1"], writes=["TMP"])
        S.op("act", lambda e: e.activation(EX2, TMP, AF.Exp, scale=SDEC), reads=["TMP"], writes=["EX2"])
        S.op("dve", lambda e: e.tensor_tensor(BKH[:, :, 0, :], c4(B_), c4(EX2), ALU.mult), reads=["B", "EX2"], writes=[kBKH])
        S.op("dve", lambda e: e.tensor_tensor(BKH[:, :, 1, :], c4(k), c4(EX2), ALU.mult), reads=[kk_, "EX2"], writes=[kBKH])
        S.op("act", lambda e: e.activation(WC[:, 0:NCH], c4(CUM)[:, :, 63], AF.Exp, scale=-SDEC), reads=["E"], writes=[kWC])
        S.op("dve", lambda e: e.memset(VV[:, :, 0, :], 0.0), writes=[kVV])
        S.op("act", lambda e: e.copy(VV[:, :, 1, :], c4(v)), reads=[vk_], writes=[kVV])

    def rwkv_P(self, l, tb, hp):
        S, M, CST = self.S, self.M, self.CST
        pb = hp % 2
        AR, BK, BKH, VV, WC = (M[k][pb] for k in ("AR", "BK", "BKH", "VV", "WC"))
        kAR, kBK, kBKH, kVV, kWC = (("AR", pb), ("BK", pb), ("BKH", pb), ("VV", pb), ("WC", pb))
        GT, TT = M["GT"][pb], M["TT"][pb]
        kGT, kTT = ("GT", pb), ("TT", pb)
        BKHT, UV, Z = (M[k] for k in ("BKHT", "UV", "Z"))
        ident = CST[:, C_ID:C_ID + 128]

        def f2(ap):
            return ap.rearrange("p a b -> p (a b)")
        for hh in range(2):
            hs = slice(hh * 64, hh * 64 + 64)
            for c in range(NCH):
                self.mm(self.P[2 + hh][:, c * 128:(c + 1) * 128], f2(BK[hs, c]), f2(AR[hs, c]), True, True,
                        reads=[kBK, kAR], writes=["P%d" % (2 + hh)])
        mg = CST[:, C_MG:C_MG + 128].unsqueeze(1).to_broadcast([128, NCH, 128])
        for hh in range(2):
            S.op("dve", lambda e, hh=hh: e.tensor_tensor(
                GT[:, hh], self.P[2 + hh][:, 0:NCH * 128].rearrange("p (c x) -> p c x", x=128), mg, ALU.mult),
                reads=["P%d" % (2 + hh), "cst"], writes=[kGT])
        X, XTb = M["X"], M["XTb"]
        NI = NCH * 2
        for hh in range(2):
            hs = slice(hh * 64, hh * 64 + 64)
            for c in range(NCH):
                self.mm(self.P[4 + hh][0:64, c * 64:c * 64 + 64], AR[hs, c, 0, :], BK[hs, c, 0, :], True, True,
                        reads=[kAR, kBK], writes=["P%d" % (4 + hh)])
        ml = CST[0:64, C_ML:C_ML + 64].unsqueeze(1).to_broadcast([64, NCH, 64])

        def p3(bank, n=NI):
            return self.P[bank][0:64, 0:n * 64].rearrange("p (a b) -> p a b", b=64)
        for hh in range(2):
            S.op("dve", lambda e, hh=hh: e.tensor_tensor(X[0][0:64, hh * NCH:(hh + 1) * NCH, :], p3(4 + hh, NCH), ml, ALU.mult),
                 reads=["P%d" % (4 + hh), "cst"], writes=[("X", 0)])
        xt0 = GT[0:64, :, :, 0:64].rearrange("p h c x -> p (h c) x")
        id3 = ident[0:64, 0:64].unsqueeze(1).to_broadcast([64, NI, 64])
        S.op("dve", lambda e: e.tensor_tensor(TT[0:64], xt0, id3, ALU.add), reads=[kGT, "cst"], writes=[kTT])
        XT0b, TTb = M["XT0b"], M["TTb"]
        S.op("act", lambda e: e.copy(XT0b[0:64], xt0), reads=[kGT], writes=["XT0b"])
        S.op("act", lambda e: e.copy(TTb[0:64], TT[0:64]), reads=[kTT], writes=["TTb"])
        xc, xtc, xck, xtck = X[0], XT0b, ("X", 0), "XT0b"
        for j in range(1, 6):
            xn, xnk = X[j % 2], ("X", j % 2)
            xtn, xtnk = XTb[j % 2], ("XT", j % 2)
            for i in range(NI):
                self.mm(self.P[4][0:64, i * 64:i * 64 + 64], xtc[0:64, i, :], xc[0:64, i, :], True, True,
                        reads=[xck, xtck], writes=["P4"])
            if j < 5:
                for i in range(NI):
                    self.mm(self.P[5][0:64, i * 64:i * 64 + 64], xc[0:64, i, :], xtc[0:64, i, :], True, True,
                            reads=[xck, xtck], writes=["P5"])
            S.op("act", lambda e, xn=xn: e.copy(xn[0:64], p3(4)), reads=["P4"], writes=[xnk])
            if j < 5:
                S.op("dve", lambda e, xtn=xtn: e.tensor_copy(xtn[0:64], p3(5)), reads=["P5"], writes=[xtnk])
            for i in range(NI):
                self.mm(self.P[2][0:64, i * 64:i * 64 + 64], xn[0:64, i, :], TTb[0:64, i, :], True, True,
                        reads=[xnk, "TTb"], writes=["P2"])
            S.op("dve", lambda e: e.tensor_tensor(TT[0:64], p3(2), TT[0:64], ALU.add), reads=["P2", kTT], writes=[kTT])
            if j < 5:
                S.op("act", lambda e: e.copy(TTb[0:64], TT[0:64]), reads=[kTT], writes=["TTb"])
            xc, xtc, xck, xtck = xn, xtn, xnk, xtnk

    def rwkv_TS(self, l, tb, hp):
        S, M, CST = self.S, self.M, self.CST
        pb = hp % 2
        AR, BK, BKH, VV, WC = (M[k][pb] for k in ("AR", "BK", "BKH", "VV", "WC"))
        kAR, kBK, kBKH, kVV, kWC = (("AR", pb), ("BK", pb), ("BKH", pb), ("VV", pb), ("WC", pb))
        GT, TT = M["GT"][pb], M["TT"][pb]
        kGT, kTT = ("GT", pb), ("TT", pb)
        BKHT, UV, Z = (M[k] for k in ("BKHT", "UV", "Z"))
        ident = CST[:, C_ID:C_ID + 128]

        def f2(ap):
            return ap.rearrange("p a b -> p (a b)")
        for hh in range(2):
            hs = slice(hh * 64, hh * 64 + 64)
            for c in range(NCH):
                S.op("pe", lambda e, c=c, hs=hs, hh=hh: e.transpose(self.P[2 + hh][:, c * 64:c * 64 + 64], f2(BKH[hs, c]), ident[hs, hs]),
                     reads=[kBKH, "cst"], writes=["P%d" % (2 + hh)], pe_meta=(hh * 64, 64, 2 + hh))
                S.op("pe", lambda e, c=c, hs=hs, hh=hh: e.transpose(self.P[4 + hh][:, c * 64:c * 64 + 64], f2(VV[hs, c]), ident[hs, hs]),
                     reads=[kVV, "cst"], writes=["P%d" % (4 + hh)], pe_meta=(hh * 64, 64, 4 + hh))
        for hh in range(2):
            S.op("dve", lambda e, hh=hh: e.tensor_copy(BKHT[:, hh * NCH:(hh + 1) * NCH, :],
                                                       self.P[2 + hh][:, 0:NCH * 64].rearrange("p (a b) -> p a b", b=64)),
                 reads=["P%d" % (2 + hh)], writes=["BKHT"])
            S.op("act", lambda e, hh=hh: e.copy(UV[64:128, hh * NCH:(hh + 1) * NCH, :],
                                                self.P[4 + hh][64:128, 0:NCH * 64].rearrange("p (a b) -> p a b", b=64)),
                 reads=["P%d" % (4 + hh)], writes=["UVv"])
        HSk = ("HS", hp)
        pZ, pU, pY, pH = self.P[1], self.P[3], self.P[0], self.P[5]
        UVh = UV[0:64].rearrange("p (h c) v -> p h c v", h=2)
        for c in range(NCH):
            for hh in range(2):
                hs = slice(hh * 64, hh * 64 + 64)
                i = hh * NCH + c
                t1 = self.mm(pZ[0:64, hh * 64:hh * 64 + 64], AR[hs, c, 0, :], self.HS[hs, hp, :], True, False,
                             reads=[kAR, HSk], writes=["P1"], inc=True)
                self.mm(pZ[0:64, hh * 64:hh * 64 + 64], GT[64:128, hh, c, 0:64], UV[64:128, i, :], False, True,
                        reads=[kGT, "UVv"], writes=["P1"], after=(t1 if hh == 0 else None))
            S.op("act", lambda e: e.copy(Z[0:64, :], pZ[0:64, 0:128]), reads=["P1"], writes=["Z"])
            for hh in range(2):
                i = hh * NCH + c
                self.mm(pU[0:64, hh * 64:hh * 64 + 64], TT[0:64, i, :], Z[0:64, hh * 64:hh * 64 + 64], True, True,
                        reads=[kTT, "Z"], writes=["P3"])
            S.op("dve", lambda e, c=c: e.tensor_copy(UVh[:, :, c, :], pU[0:64, 0:128].rearrange("p (a b) -> p a b", b=64)),
                 reads=["P3"], writes=[("UVu", c)])
            for hh in range(2):
                hs = slice(hh * 64, hh * 64 + 64)
                i = hh * NCH + c
                self.mm(pY[hs, c * 64:c * 64 + 64], self.HS[hs, hp, :], AR[hs, c, 1, :], True, False,
                        reads=[HSk, kAR], writes=["P0"])
                self.mm(pY[hs, c * 64:c * 64 + 64], UV[:, i, :], GT[:, hh, c, 64:128], False, True,
                        reads=[("UVu", c), "UVv", kGT], writes=["P0"])
            for hh in range(2):
                hs = slice(hh * 64, hh * 64 + 64)
                i = hh * NCH + c
                self.mm(pH[hs, 0:64], BKHT[:, i, :], UV[:, i, :], True, True,
                        reads=["BKHT", ("UVu", c), "UVv"], writes=["P5"])
            S.op("dve", lambda e, c=c: e.scalar_tensor_tensor(self.HS[:, hp, :], self.HS[:, hp, :], WC[:, c:c + 1], pH[:, 0:64], ALU.mult, ALU.add),
                 reads=["P5", kWC, HSk], writes=[HSk])

    def rwkv_post(self, l, tb, hp):
        S, M, PV, CST = self.S, self.M, self.PV, self.CST
        pb = hp % 2
        YS, YC, T1, T2 = M["YS"], M["YC"], M["PT1"], M["PT2"]
        G, BON = M["G"][pb], M["BON"][pb]
        bo = CST[:, C_BO:C_BO + 128]
        pY = self.P[0]
        P6 = self.P[6][:, 0:TB]

        def pcol(base):
            return PV[:, base + hp:base + hp + 1]
        S.op("act", lambda e: e.copy(YS, pY[:, 0:TB]), reads=["P0"], writes=["YS"])
        self.mm(P6, bo, YS, True, True, reads=["cst", "YS"], writes=["P6"])
        S.op("dve", lambda e: e.scalar_tensor_tensor(YC, P6, -1.0 / 64.0, YS, ALU.mult, ALU.add), reads=["P6", "YS"], writes=["YC"])
        S.op("act", lambda e: e.activation(T1, YC, AF.Square), reads=["YC"], writes=["PT1"])
        self.mm(P6, bo, T1, True, True, reads=["cst", "PT1"], writes=["P6"])
        S.op("act", lambda e: e.activation(T2, P6, AF.Sqrt, bias=GN_EPS, scale=1.0 / 64.0), reads=["P6"], writes=["PT2"])
        S.op("dve", lambda e: e.reciprocal(T2, T2), reads=["PT2"], writes=["PT2"])
        S.op("dve", lambda e: e.tensor_tensor(YC, YC, T2, ALU.mult), reads=["YC", "PT2"], writes=["YC"])
        S.op("act", lambda e: e.activation(YC, YC, AF.Identity, bias=pcol(PV_GNB), scale=pcol(PV_GNG)), reads=["YC", "pv"], writes=["YC"])
        S.op("dve", lambda e: e.tensor_tensor(YC, YC, BON, ALU.add), reads=["YC", ("BON", pb)], writes=["YC"])
        S.op("dve", lambda e: e.tensor_tensor(M["MIX"][:, hp, :], YC, G, ALU.mult), reads=["YC", ("G", pb)], writes=[("mix", hp)])

    def attn_block(self, l, tb, winT):
        S, M, PV = self.S, self.M, self.PV
        t0 = tb * TB
        QF, QN, KB, VB, BIAS, MIX = M["QF"], M["QN"], M["KB"], M["VB"], M["BIAS"], M["MIX"]
        scr = M["scr"]

        def hnorm(dst, gcol, dkey):
            sq = scr["sq"][0][:, 0:TB]
            rt = scr["rt"][:, 0:TB]
            S.op("act", lambda e: e.activation(sq, QF, AF.Square), reads=["QF"], writes=[("sq", 0)])
            self.mm(self.P[6][:, 0:TB], self.bo_bf[:], sq, True, True, reads=["bo_bf", ("sq", 0)], writes=["P6"])
            S.op("act", lambda e: e.activation(rt, self.P[6][:, 0:TB], AF.Sqrt, bias=RMS_EPS, scale=1.0 / 64.0), reads=["P6"], writes=["rt"])
            S.op("dve", lambda e: e.reciprocal(rt, rt), reads=["rt"], writes=["rt"])
            S.op("dve", lambda e: e.scalar_tensor_tensor(dst, QF, PV[:, gcol:gcol + 1], rt, ALU.mult, ALU.mult),
                 reads=["QF", "pv", "rt"], writes=[dkey])
        for part, c0 in (("q", 1280), ("k", 1664)):
            wsl, si = self.load_w_piece(winT, c0, 384)
            for hp in range(3):
                bank = (1, 6)[hp % 2]
                pp = self.proj_fm(wsl, si, hp * 128, bank)
                S.op("act", lambda e, pp=pp: e.copy(QF, pp[:, 0:TB]), reads=["P%d" % bank], writes=["QF"])
                if part == "q":
                    hnorm(QN[:, hp, :], PV_BQG, ("qn", hp))
                else:
                    hnorm(KB[:, hp, t0 % 768:t0 % 768 + TB], PV_BKG, ("kb", hp))
        wsl, si = self.load_w_piece(winT, 2048, 384)
        for i in range(NQT):
            qt = tb * NQT + i
            vbk = (1, 6)[i % 2]
            pp = self.P[vbk]
            for kc in range(KC):
                self.mm(pp[:, 0:384], M["HN"][:, kc, i * 128:(i + 1) * 128], wsl[:, kc, :], kc == 0, kc == KC - 1,
                        reads=[("ws", si)] + self.hk(kc, 0, TB), writes=["P%d" % vbk])
            S.op("act", lambda e, pp=pp, qt=qt: e.copy(VB[:, qt % 6, :], pp[:, 0:384]), reads=["P%d" % vbk], writes=["vb"])
        for i in range(NQT):
            qt = tb * NQT + i
            r0 = max(0, 4 - qt)
            for hp in range(3):
                ob = 4
                pO, pD = self.P[ob], self.P[ob + 1]
                for hh in range(2):
                    h = 2 * hp + hh
                    hs = slice(hh * 64, hh * 64 + 64)
                    b = hh
                    SBt, PT = M["SBt"][b], M["PT"][b]
                    sbk = ("sbt", 0)
                    for r in range(r0, 5):
                        kt = qt - 4 + r
                        sb = 6 if hh == 1 else 2
                        sb1 = 1 if hh == 1 else 3
                        if r < 4:
                            out, bk = self.P[sb][:, r * 128:(r + 1) * 128], "P%d" % sb
                        else:
                            out, bk = self.P[sb1][:, 0:128], "P%d" % sb1
                        self.mm(out, KB[hs, hp, (kt % 6) * 128:(kt % 6 + 1) * 128], QN[hs, hp, i * 128:(i + 1) * 128], True, True,
                                reads=[("kb", hp), ("qn", hp)], writes=[bk])
                    if r0 < 4:
                        S.op("dve", lambda e, SBt=SBt, h=h, r0=r0, sb=sb: e.scalar_tensor_tensor(
                            SBt[:, r0 * 128:512], self.P[sb][:, r0 * 128:512], 0.125, BIAS[:, h, r0 * 128:512], ALU.mult, ALU.add),
                            reads=["P%d" % sb, "bias"], writes=[sbk])
                    S.op("dve", lambda e, SBt=SBt, h=h, sb1=sb1: e.scalar_tensor_tensor(
                        SBt[:, 512:640], self.P[sb1][:, 0:128], 0.125, BIAS[:, h, 512:640], ALU.mult, ALU.add),
                        reads=["P%d" % sb1, "bias"], writes=[sbk])
                    S.op("act", lambda e, SBt=SBt, PT=PT, r0=r0: e.activation(PT[:, r0 * 128:640], SBt[:, r0 * 128:640], AF.Exp),
                         reads=[sbk], writes=[("pt", b)])
                    for r in range(r0, 5):
                        kt = qt - 4 + r
                        self.mm(pO[hs, 0:128], VB[:, kt % 6, h * 64:(h + 1) * 64], PT[:, r * 128:(r + 1) * 128], r == r0, r == 4,
                                reads=["vb", ("pt", b)], writes=["P%d" % ob])
                    for r in range(r0, 5):
                        self.mm(pD[hs, 0:128], self.ones_bf[:, 0:64], PT[:, r * 128:(r + 1) * 128], r == r0, r == 4,
                                reads=["ones_bf", ("pt", b)], writes=["P%d" % (ob + 1)])
                rd = M["rden"][hp % 2]
                S.op("dve", lambda e, rd=rd, pD=pD: e.reciprocal(rd, pD[:, 0:128]), reads=["P%d" % (ob + 1)], writes=[("rden", hp % 2)])
                S.op("dve", lambda e, rd=rd, hp=hp, i=i, pO=pO: e.tensor_tensor(MIX[:, 3 + hp, i * 128:(i + 1) * 128], pO[:, 0:128], rd, ALU.mult),
                     reads=["P%d" % ob, ("rden", hp % 2)], writes=[("mix", 3 + hp)])

    def pool_block(self, l, tb, winT):
        S, M, PV, CST = self.S, self.M, self.PV, self.CST
        LV0, LV1, LV2 = M["LV"]
        POOLED, MIX = M["POOLED"], M["MIX"]
        W = TB + 16
        wsl, si = self.load_w_piece(winT, 2432, 256)
        for ch in range(2):
            pbk = (1, 6)[ch]
            pp = self.proj_fm(wsl, si, ch * 128, pbk)
            S.op("act", lambda e, pp=pp, ch=ch: e.copy(LV0[:, ch, 16:W], pp[:, 0:TB]), reads=["P%d" % pbk], writes=["lv0"])
        S.op("dve", lambda e: e.tensor_copy(LV0[:, :, 0:16], self.UHIST[:]), reads=["uhist"], writes=["lv0"])
        S.op("dve", lambda e: e.tensor_copy(self.UHIST[:], LV0[:, :, TB:W]), reads=["lv0"], writes=["uhist"])
        S.op("dve", lambda e: e.tensor_tensor(LV1[:, :, 1:W], LV0[:, :, 1:W], LV0[:, :, 0:W - 1], ALU.add), reads=["lv0"], writes=["lv1"])
        S.op("dve", lambda e: e.tensor_tensor(LV2[:, :, 3:W], LV1[:, :, 3:W], LV1[:, :, 1:W - 2], ALU.add), reads=["lv1"], writes=["lv2"])
        S.op("dve", lambda e: e.tensor_tensor(LV1[:, 1, 7:W], LV2[:, 1, 7:W], LV2[:, 1, 3:W - 4], ALU.add), reads=["lv2", "lv1"], writes=["lv1"])
        S.op("dve", lambda e: e.tensor_tensor(LV2[64:128, 1, 15:W], LV1[64:128, 1, 15:W], LV1[64:128, 1, 7:W - 8], ALU.add),
             reads=["lv1", "lv2"], writes=["lv2"])
        if tb == 0:
            pf = CST[:, C_PF:C_PF + 32].rearrange("p (a b) -> p a b", b=16)
            S.op("dve", lambda e: e.tensor_tensor(LV1[0:64, :, 16:32], LV1[0:64, :, 16:32], pf[0:64], ALU.mult), reads=["lv1", "cst"], writes=["lv1"])
            S.op("dve", lambda e: e.tensor_tensor(LV2[64:128, :, 16:32], LV2[64:128, :, 16:32], pf[64:128], ALU.mult), reads=["lv2", "cst"], writes=["lv2"])
        for ch in range(2):
            iw = CST[:, C_IW + ch:C_IW + ch + 1]
            S.op("dve", lambda e, ch=ch, iw=iw: e.scalar_tensor_tensor(
                POOLED[0:64, ch, :], LV1[0:64, ch, 16:W], iw[0:64], LV0[0:64, ch, 16:W], ALU.mult, ALU.subtract),
                reads=["lv1", "lv0", "cst"], writes=[("pooled", ch), "lv0"])
            S.op("dve", lambda e, ch=ch, iw=iw: e.scalar_tensor_tensor(
                POOLED[64:128, ch, :], LV2[64:128, ch, 16:W], iw[64:128], LV0[64:128, ch, 16:W], ALU.mult, ALU.subtract),
                reads=["lv2", "lv0", "cst"], writes=[("pooled", ch), "lv0"])
        for ch in range(2):
            pk = (6, 1)[ch]
            pp = self.P[pk]
            self.mm(pp[:, 0:TB], PV[:, PV_PW + ch * 128:PV_PW + (ch + 1) * 128], POOLED[:, ch, :], True, True,
                    reads=["pv", ("pooled", ch)], writes=["P%d" % pk])
            S.op("dve", lambda e, ch=ch, pp=pp: e.tensor_scalar(MIX[:, 6 + ch, :], pp[:, 0:TB], PV[:, PV_PSC + ch:PV_PSC + ch + 1], None, ALU.mult),
                 reads=["P%d" % pk, "pv"], writes=[("mix", 6 + ch)])

    def store_out(self):
        S = self.S
        keys = []
        for kc in range(KC):
            for t8 in range(0, 8, 2):
                k = ("out", kc, t8)
                S.dma("sp", self.d_out[kc * 128:(kc + 1) * 128, t8 * 256:(t8 + 2) * 256],
                      self.XT[:, kc, t8 * 256:(t8 + 2) * 256], reads=self.xk(kc, t8 * 256, 512), writes=[k], semkey="out")
                keys.append(k)
        last = S.lastw[keys[-1]]
        for k in keys:
            S.lastw[k] = last
        S.wait_keys("sp", keys)


def build_program(cfg):
    nc = bass.Bass("TRN2", target_bir_lowering=False)
    with ExitStack() as st:
        mk = MK(nc, st, cfg)
        for (l, stage) in cfg["stages"]:
            if stage == "ffn1":
                mk.ffn(l, 0)
            elif stage == "ffn2":
                mk.ffn(l, 1)
            elif stage == "cross":
                mk.cross(l)
            elif stage.startswith("mix"):
                if len(stage) > 3:
                    mk.cfg["mixers"] = stage[3:]
                mk.mixer(l)
        mk.S.barrier()
        mk.store_out()
        mk.S.emit()
    return nc


def host_consts():
    c = np.zeros((128, C_N), np.float32)
    c[:, C_ID:C_ID + 128] = np.eye(128, dtype=np.float32)
    bo = np.zeros((128, 128), np.float32)
    bo[:64, :64] = 1.0
    bo[64:, 64:] = 1.0
    c[:, C_BO:C_BO + 128] = bo
    si = np.arange(64)[:, None]
    ti = np.arange(64)[None, :]
    mg = np.zeros((128, 128), np.float32)
    mg[:64, :64] = si < ti
    mg[:64, 64:] = si <= ti
    mg[64:, :64] = si < ti
    mg[64:, 64:] = si <= ti
    c[:, C_MG:C_MG + 128] = mg
    c[:, C_MG + 128:C_MG + 256] = mg
    ml = (si > ti).astype(np.float32)
    c[:64, C_ML:C_ML + 64] = ml
    c[64:, C_ML:C_ML + 64] = ml
    c[:, C_SM:C_SM + TB] = 1.0
    c[:, C_SM:C_SM + TB:64] = 0.0
    wins = {(0, 0): 2, (0, 1): 4, (1, 0): 8, (1, 1): 16}
    for (ch, half), win in wins.items():
        rows = slice(half * 64, half * 64 + 64)
        tt = np.arange(16)
        c[rows, C_PF + ch * 16:C_PF + (ch + 1) * 16] = (win / np.minimum(tt + 1, win)).astype(np.float32)[None, :]
        c[rows, C_IW + ch] = 1.0 / win
    return c


def host_layer_params(inp):
    pv = np.zeros((L, 128, PV_N), np.float32)

    def put(l, col, vec):
        n = vec.shape[0] // 128
        pv[l, :, col:col + n] = vec.reshape(n, 128).T
    for l in range(L):
        put(l, PV_FFN1, inp["norm_ffn1"][l])
        put(l, PV_MIX, inp["norm_mix"][l])
        put(l, PV_CROSS, inp["norm_cross"][l])
        put(l, PV_MEM, inp["norm_mem"][l])
        put(l, PV_FFN2, inp["norm_ffn2"][l])
        put(l, PV_XQ, inp["x_q_gain"][l])
        put(l, PV_XK, inp["x_k_gain"][l])
        put(l, PV_MU, inp["a_mu"][l])
        put(l, PV_W0, inp["a_w0"][l])
        put(l, PV_A0, inp["a_a0"][l])
        put(l, PV_KK, inp["a_k_k"][l])
        put(l, PV_KA, inp["a_k_a"][l])
        put(l, PV_RK, inp["a_r_k"][l].reshape(-1))
        put(l, PV_GNG, inp["a_gn_g"][l])
        put(l, PV_GNB, inp["a_gn_b"][l])
        if l >= 1:
            put(l, PV_V0, inp["a_v0"][l - 1])
            pv[l, 0:32, PV_VUP:PV_VUP + 384] = inp["a_v_up"][l - 1]
            pv[l, :, PV_VDOWN:PV_VDOWN + 96] = inp["a_v_down"][l - 1].reshape(3, 128, 32).transpose(1, 0, 2).reshape(128, 96)
        pv[l, :, PV_BQG] = np.tile(inp["b_q_gain"][l], 2)
        pv[l, :, PV_BKG] = np.tile(inp["b_k_gain"][l], 2)
        put(l, PV_PSC, inp["c_pool_scale"][l])
        pv[l, 0:32, PV_LORA:PV_LORA + 384] = inp["a_w_up"][l]
        pv[l, 32:64, PV_LORA:PV_LORA + 384] = inp["a_a_up"][l]
        pv[l, 64:128, PV_LORA:PV_LORA + 384] = inp["a_g_up"][l]
        for ch in range(2):
            for half in range(2):
                g = ch * 2 + half
                rows = slice(half * 64, half * 64 + 64)
                pv[l, rows, PV_PW + ch * 128 + half * 64:PV_PW + ch * 128 + half * 64 + 64] = inp["c_pool_w"][l, g]
    return pv


def host_bias(inp):
    j = np.arange(128)[:, None, None]
    r = np.arange(5)[None, :, None]
    i = np.arange(128)[None, None, :]
    dist = (4 - r) * 128 + i - j
    idx = np.clip(dist, -63, 256) + 63
    dchunk = (r - 4) * 2 + (j // 64) - (i // 64)
    valid = (dchunk >= -8) & (dchunk <= 0)
    out = np.zeros((L, 128, 6, 5, 128), np.float32)
    for l in range(L):
        for h in range(6):
            g = inp["b_rel_bias"][l, h][idx]
            out[l, :, h] = np.where(valid, g, np.float32(NEG))
    return out.reshape(L, 128, 6 * 640)


WEIGHT_KEYS = ("ffn1_wi", "ffn1_wo", "ffn2_wi", "ffn2_wo", "w_in", "w_out", "x_wq", "x_wkv", "x_wo")


def make_in_maps(inp, cores):
    shared = {"cst": host_consts(), "lyr": host_layer_params(inp), "biasT": host_bias(inp)}
    for k in WEIGHT_KEYS:
        shared[k] = np.ascontiguousarray(inp[k], dtype=np.float32)
    maps = []
    for b in cores:
        m = dict(shared)
        m["xT"] = np.ascontiguousarray(inp["x"][b].T)
        m["memT"] = np.ascontiguousarray(inp["mem"][b].T)
        maps.append(m)
    return maps


FULL_STAGES = [(l, s) for l in range(L) for s in ("ffn1", "mix", "cross", "ffn2")]
_CACHE = {}


def kernel(**inputs):
    inp = {k: np.asarray(v) for k, v in inputs.items()}
    if "nc" not in _CACHE:
        _CACHE["nc"] = build_program({"stages": FULL_STAGES})
    nc = _CACHE["nc"]
    maps = make_in_maps(inp, list(range(8)))
    res = run_bass_kernel_spmd(nc, maps, core_ids=list(range(8)))
    out = np.stack([np.ascontiguousarray(r["outT"].T) for r in res.results], axis=0)
    return out.astype(np.float32)
```

```python
import math
import numpy as np
import concourse.bass as bass
import concourse.mybir as mybir
from concourse.bass_utils import run_bass_kernel_spmd
from contextlib import ExitStack

F32 = mybir.dt.float32
BF16 = mybir.dt.bfloat16
AF = mybir.ActivationFunctionType
ALU = mybir.AluOpType

D = 1024
T = 2048
L = 2
NMEM = 256
DFF = 2816
KC = D // 128
JC = DFF // 128
INP = 2688
RMS_EPS = 1e-6
GN_EPS = 64e-5
SLOT_ELEMS = 3072
NSLOT = 4
SCR_UNITS = 27920
TB = 256
NCH = TB // 64
NQT = TB // 128
SDEC = math.exp(-0.5)
NEG = -30000.0

C_ID, C_BO, C_MG, C_ML, C_PF, C_IW, C_SM, C_N = 0, 128, 256, 512, 640, 672, 704, 960
PV_FFN1, PV_MIX, PV_CROSS, PV_MEM, PV_FFN2, PV_XQ, PV_XK = 0, 8, 16, 24, 32, 40, 42
PV_MU, PV_W0, PV_A0, PV_KK, PV_KA, PV_RK, PV_GNG, PV_GNB, PV_V0, PV_BQG, PV_BKG, PV_PSC = 44, 54, 57, 60, 63, 66, 69, 72, 75, 78, 79, 80
PV_LORA, PV_VUP, PV_VDOWN, PV_PW, PV_N = 96, 480, 864, 960, 1216


class Ref:
    tok = None


def _units(ops):
    units, cur, open_, live = [], [], False, set()
    for it in ops:
        kind, args, _ = it
        cur.append(it)
        if kind == "op":
            if args[0] == "pe":
                meta = args[6]
                if meta is not None:
                    if len(meta) > 3:
                        open_ = not meta[4]
                    if meta[2] != 0:
                        live.add("P%d" % meta[2])
            else:
                for k in args[2]:
                    live.discard(k)
        if not open_ and not live:
            units.append(cur)
            cur = []
    if cur:
        units.append(cur)
    return units


def split_at_mix_write(ops):
    head, tail, hit = [], [], False
    for u in _units(ops):
        if not hit and any(k[0] == "mix" for it in u for k in it[1][3] if isinstance(k, tuple)):
            hit = True
        (tail if hit else head).extend(u)
    return head, tail


def merge(a, b):
    return interleave(a, b) if len(a) >= len(b) else interleave(b, a)


def _is_bank(k):
    return isinstance(k, str) and len(k) == 2 and k[0] == "P" and k[1].isdigit()


def split_for_next_head(tail, head_ops, frac=0.35):
    wk = set(k for it in head_ops for k in it[1][3 if it[0] == "op" else 4] if not _is_bank(k))
    units = _units(tail)
    last = -1
    for ui, u in enumerate(units):
        for it in u:
            a = it[1]
            rd, wr = (a[2], a[3]) if it[0] == "op" else (a[3], a[4])
            if any((k in wk) for k in rd) or any((k in wk) for k in wr):
                last = ui
    cut = max(last + 1, int(len(units) * frac))
    t1 = [it for u in units[:cut] for it in u]
    t2 = [it for u in units[cut:] for it in u]
    return t1, t2


def interleave(a, b):
    ua, ub = _units(a), _units(b)
    if not ub:
        return list(a)
    out, step, nb = [], max(1, len(ua) // (len(ub) + 1)), 0
    for i, x in enumerate(ua):
        out.extend(x)
        if (i + 1) % step == 0 and nb < len(ub):
            out.extend(ub[nb])
            nb += 1
    for x in ub[nb:]:
        out.extend(x)
    return out


class Sched:
    ENG = ("pe", "act", "dve", "pool", "sp")

    def __init__(self, nc, stack, same_engine_sync=True):
        self.nc = nc
        self.stack = stack
        self.sem = {e: stack.enter_context(nc.semaphore("sem_" + e)) for e in self.ENG}
        self.count = {e: 0 for e in self.ENG}
        self.prog = {e: [] for e in self.ENG}
        self.waited = {e: {} for e in self.ENG}
        self.lastw = {}
        self.readers = {}
        self.dma_sems = {}
        self.same_engine_sync = same_engine_sync
        self.n_sem = 0
        self._rec = None
        self._rec_stack = []
        self._pe_recent = []

    def sbuf(self, name, shape, dtype=F32):
        return self.stack.enter_context(self.nc.sbuf_tensor(name, list(shape), dtype))

    def psum(self, name, shape, dtype=F32):
        return self.stack.enter_context(self.nc.psum_tensor(name, list(shape), dtype))

    def _deps(self, reads, writes):
        deps = []
        for k in reads:
            t = self.lastw.get(k)
            if t is not None:
                deps.append(t)
        for k in writes:
            t = self.lastw.get(k)
            if t is not None:
                deps.append(t)
            r = self.readers.get(k)
            if r:
                deps.extend(r.values())
        return deps

    def _emit_waits(self, eng, deps):
        need = {}
        for t in deps:
            sem, val, owner = t
            if owner == eng and (eng == "pe" or not self.same_engine_sync):
                continue
            if owner is not None and val > self.count[owner]:
                raise RuntimeError("wait on a not-yet-incrementing instruction (%s waits %s>=%d)" % (eng, owner, val))
            key = sem.name
            if key not in need or need[key][1] < val:
                need[key] = (sem, val)
        w = self.waited[eng]
        for key, (sem, val) in need.items():
            if w.get(key, 0) >= val:
                continue
            w[key] = val
            self.prog[eng].append(lambda e, sem=sem, val=val: e.wait_ge(sem, val))

    def _record(self, tok, reads, writes):
        key = tok[0].name
        for k in reads:
            r = self.readers.setdefault(k, {})
            if key not in r or r[key][1] < tok[1]:
                r[key] = tok
        for k in writes:
            self.lastw[k] = tok
            self.readers[k] = {}

    def rec_begin(self):
        self._rec_stack.append(self._rec)
        self._rec = []

    def rec_end(self):
        r, self._rec = self._rec, self._rec_stack.pop()
        return r

    def _pe_guard(self, meta):
        base, n, bank = meta[0], meta[1], meta[2]
        G = frozenset(range(base // 32, (base + n + 31) // 32))
        if len(G) == 4:
            self._pe_recent = []
            return [], None
        waits = [t for (g, b, t) in self._pe_recent if not (g & G) and b == bank]
        self._pe_recent = [(g, b, t) for (g, b, t) in self._pe_recent if not (g & G)]
        return waits, G

    def replay(self, item):
        kind, args, ref = item
        ref.tok = (self.op if kind == "op" else self.dma)(*args)

    def op(self, eng, fn, reads=(), writes=(), inc=True, after=None, pe_meta=None):
        if self._rec is not None:
            ref = Ref()
            self._rec.append(("op", (eng, fn, tuple(reads), tuple(writes), inc, after, pe_meta), ref))
            return ref
        while isinstance(after, Ref):
            after = after.tok
        forced = [after] if after is not None else []
        G = None
        if pe_meta is not None:
            w, G = self._pe_guard(pe_meta)
            forced += w
            if G is not None:
                inc = True
        self._emit_waits(eng, self._deps(reads, writes))
        for (sem_a, val_a, _) in forced:
            if self.waited[eng].get(sem_a.name, 0) < val_a:
                self.waited[eng][sem_a.name] = val_a
                self.prog[eng].append(lambda e, sem_a=sem_a, val_a=val_a: e.wait_ge(sem_a, val_a))
        sem = self.sem[eng]
        if inc:
            self.count[eng] += 1
            val = self.count[eng]
            self.prog[eng].append(lambda e, fn=fn, sem=sem: fn(e).then_inc(sem, 1))
        else:
            val = self.count[eng] + 1
            self.prog[eng].append(lambda e, fn=fn: fn(e))
        tok = (sem, val, eng)
        if G is not None:
            self._pe_recent.append((G, pe_meta[2], tok))
        self._record(tok, reads, writes)
        return tok

    def dma(self, q, out, in_, reads=(), writes=(), semkey=None):
        if self._rec is not None:
            ref = Ref()
            self._rec.append(("dma", (q, out, in_, tuple(reads), tuple(writes), semkey), ref))
            return ref
        self._emit_waits(q, self._deps(reads, writes))
        if semkey is None:
            semkey = writes[0]
        if semkey not in self.dma_sems:
            self.n_sem += 1
            s = self.stack.enter_context(self.nc.semaphore("dsem%d" % self.n_sem))
            self.dma_sems[semkey] = [s, 0]
        ent = self.dma_sems[semkey]
        ent[1] += 16
        sem, val = ent[0], ent[1]
        self.prog[q].append(lambda e, out=out, in_=in_, sem=sem: e.dma_start(out=out, in_=in_).then_inc(sem, 16))
        tok = (sem, val, None)
        self._record(tok, reads, writes)
        return tok

    def wait_keys(self, eng, keys):
        self._emit_waits(eng, self._deps(keys, ()))

    def barrier(self):
        toks = [(self.sem[e], self.count[e], e) for e in self.ENG if self.count[e] > 0]
        toks += [(s, v, None) for (s, v) in self.dma_sems.values()]
        for e in self.ENG:
            if e == "pool":
                continue
            self._emit_waits(e, [t for t in toks if t[2] != e])

    def emit(self):
        with self.nc.Block() as block:
            @block.sync
            def _(e):
                for f in self.prog["sp"]:
                    f(e)

            @block.gpsimd
            def _(e):
                for f in self.prog["pool"]:
                    f(e)

            @block.scalar
            def _(e):
                for f in self.prog["act"]:
                    f(e)

            @block.vector
            def _(e):
                for f in self.prog["dve"]:
                    f(e)

            @block.tensor
            def _(e):
                for f in self.prog["pe"]:
                    f(e)


class MK:
    def __init__(self, nc, st, cfg):
        self.nc = nc
        self.cfg = cfg
        S = self.S = Sched(nc, st, same_engine_sync=cfg.get("same_engine_sync", True))
        dt = nc.dram_tensor

        def din(name, shape):
            return dt(name, list(shape), F32, kind="ExternalInput").ap()

        self.d_xT = din("xT", [D, T])
        self.d_memT = din("memT", [D, NMEM])
        self.d_cst = din("cst", [128, C_N])
        self.d_lyr = din("lyr", [L, 128, PV_N])
        self.d_bias = din("biasT", [L, 128, 6 * 640])
        self.d_ffn_wi = [din("ffn1_wi", [L, D, 2 * DFF]), din("ffn2_wi", [L, D, 2 * DFF])]
        self.d_ffn_wo = [din("ffn1_wo", [L, DFF, D]), din("ffn2_wo", [L, DFF, D])]
        self.d_win = din("w_in", [L, D, INP])
        self.d_wout = din("w_out", [L, D, D])
        self.d_xwq = din("x_wq", [L, D, D])
        self.d_xwkv = din("x_wkv", [L, D, 2 * D])
        self.d_xwo = din("x_wo", [L, D, D])
        self.d_out = dt("outT", [D, T], F32, kind="ExternalOutput").ap()
        self.d_vf = dt("vfirst", [3, 128, T], F32).ap()

        self.XT = S.sbuf("XT", [128, KC, T], F32)
        self.WS = S.sbuf("WS", [128, NSLOT, SLOT_ELEMS], BF16)
        self.slot_i = 0
        self.SCR = S.sbuf("SCR", [128, SCR_UNITS], F32)
        self.CST = S.sbuf("CST", [128, C_N], F32)
        self.PV = S.sbuf("LYR", [128, PV_N], F32)
        self.ones_bf = S.sbuf("ones_bf", [128, 128], BF16)
        self.bo_bf = S.sbuf("bo_bf", [128, 128], BF16)
        self.HS = S.sbuf("HS", [128, 3, 64], F32)
        self.CARRY = S.sbuf("CARRY", [128, 10], F32)
        self.UHIST = S.sbuf("UHIST", [128, 2, 16], F32)
        self.P = [S.psum("P%d" % i, [128, 512], F32) for i in range(8)]
        self.cur_layer = -1

        S.dma("sp", self.CST[:], self.d_cst, writes=["cst"])
        S.op("dve", lambda e: e.memset(self.ones_bf[:], 1.0), writes=["ones_bf"])
        S.op("dve", lambda e: e.tensor_copy(self.bo_bf[:], self.CST[:, C_BO:C_BO + 128]), reads=["cst"], writes=["bo_bf"])
        for kc in range(KC):
            for t8 in range(0, 8, 2):
                S.dma("sp", self.XT[:, kc, t8 * 256:(t8 + 2) * 256],
                      self.d_xT[kc * 128:(kc + 1) * 128, t8 * 256:(t8 + 2) * 256],
                      writes=[("x", kc, t8), ("x", kc, t8 + 1)], semkey=("xin", t8))
        for t8 in range(0, 8, 2):
            last = S.lastw[("x", KC - 1, t8)]
            for kc in range(KC):
                S.lastw[("x", kc, t8)] = last
                S.lastw[("x", kc, t8 + 1)] = last

    def xk(self, kc, t0, n):
        return [("x", kc, t) for t in range(t0 // 256, (t0 + n + 255) // 256)]

    def hk(self, kc, off, n):
        return [("hn", kc, t) for t in range(off // 256, (off + n + 255) // 256)]

    def slot(self):
        g = getattr(self, "slot_group", None)
        if g is not None:
            self.slot_gi = getattr(self, "slot_gi", {})
            k = self.slot_gi.get(g, 0)
            self.slot_gi[g] = k + 1
            return g[k % len(g)]
        i = self.slot_i
        self.slot_i = (self.slot_i + 1) % NSLOT
        return i

    def carve_reset(self):
        self.S.barrier()
        self.co = 0

    def carve(self, n, dtype=F32, inner=None):
        ap = self.SCR[:, self.co:self.co + n]
        self.co += n
        assert self.co <= SCR_UNITS, self.co
        if dtype == BF16:
            ap = ap.bitcast(BF16)
        if inner:
            ap = ap.rearrange("p (a b) -> p a b", b=inner)
        return ap

    def load_layer(self, l):
        if self.cur_layer != l:
            self.S.dma("sp", self.PV[:], self.d_lyr[l], writes=["pv"])
            self.cur_layer = l

    def mm(self, out, lhsT, rhs, start, stop, reads, writes, inc=None, after=None):
        bank = int(writes[0][1:])
        meta = (lhsT.base_partition(), lhsT.shape[0], bank, start, stop)
        return self.S.op("pe", lambda e: e.matmul(out, lhsT, rhs, start=start, stop=stop),
                         reads=reads, writes=writes, inc=(stop if inc is None else inc), after=after, pe_meta=meta)

    def load_w_piece(self, wsrc, c0, ncols, nk=KC):
        si = self.slot()
        wsl = self.WS[:, si, 0:nk * ncols].rearrange("p (kc c) -> p kc c", c=ncols)
        self.S.dma("pool", wsl, wsrc[:, :, c0:c0 + ncols], writes=[("ws", si)], semkey=("ws", si))
        return wsl, si

    def rmsnorm(self, pvcol, t0, n, HN, hoff, scr):
        S = self.S
        ps = self.P[6]
        for kc in range(KC):
            b = kc % 2
            sq = scr["sq"][b][:, 0:n]
            S.op("act", lambda e, sq=sq, kc=kc: e.activation(sq, self.XT[:, kc, t0:t0 + n], AF.Square),
                 reads=self.xk(kc, t0, n), writes=[("sq", b)])
            self.mm(ps[:, 0:n], self.ones_bf[:], sq, kc == 0, kc == KC - 1,
                    reads=[("sq", b), "ones_bf"], writes=["P6"], inc=True)
        rt = scr["rt"][:, 0:n]
        S.op("act", lambda e: e.activation(rt, ps[:, 0:n], AF.Sqrt, bias=RMS_EPS, scale=1.0 / D),
             reads=["P6"], writes=["rt"])
        S.op("dve", lambda e: e.reciprocal(rt, rt), reads=["rt"], writes=["rt"])
        for kc in range(KC):
            S.op("dve", lambda e, kc=kc: e.scalar_tensor_tensor(
                HN[:, kc, hoff:hoff + n], self.XT[:, kc, t0:t0 + n],
                self.PV[:, pvcol + kc:pvcol + kc + 1], rt, ALU.mult, ALU.mult),
                reads=self.xk(kc, t0, n) + ["pv", "rt"], writes=self.hk(kc, hoff, n))

    def ffn(self, l, which):
        S = self.S
        self.carve_reset()
        self.load_layer(l)
        wi = self.d_ffn_wi[which][l].rearrange("(kc p) c -> p kc c", p=128)
        wo = self.d_ffn_wo[which][l].rearrange("(kc p) c -> p kc c", p=128)
        pvcol = PV_FFN1 if which == 0 else PV_FFN2
        H = self.carve(11264, BF16, 1024)
        HN = self.carve(4096, BF16, 1024)
        sqb = self.carve(512, BF16)
        scr = {"sq": [sqb[:, 0:512], sqb[:, 512:1024]], "rt": self.carve(512)}
        sg = [self.carve(512), self.carve(512)]
        for hb in range(2):
            for t2 in range(2):
                self.rmsnorm(pvcol, hb * 1024 + t2 * 512, 512, HN, t2 * 512, scr)
            for j in range(JC):
                si = self.slot()
                wsl = self.WS[:, si, 0:2048].rearrange("p (g kc c) -> p g kc c", g=2, kc=KC)
                S.dma("pool", wsl[:, 0], wi[:, :, j * 128:(j + 1) * 128], writes=[("ws", si)], semkey=("ws", si))
                S.dma("pool", wsl[:, 1], wi[:, :, DFF + j * 128:DFF + (j + 1) * 128], writes=[("ws", si)], semkey=("ws", si))
                for t2 in range(2):
                    b = (j * 2 + t2) % 2
                    pg, pu = self.P[b], self.P[2 + b]
                    for kc in range(KC):
                        self.mm(pg[:], wsl[:, 0, kc, :], HN[:, kc, t2 * 512:(t2 + 1) * 512], kc == 0, kc == KC - 1,
                                reads=[("ws", si)] + self.hk(kc, t2 * 512, 512), writes=["P%d" % b])
                    for kc in range(KC):
                        self.mm(pu[:], wsl[:, 1, kc, :], HN[:, kc, t2 * 512:(t2 + 1) * 512], kc == 0, kc == KC - 1,
                                reads=[("ws", si)] + self.hk(kc, t2 * 512, 512), writes=["P%d" % (2 + b)])
                    S.op("act", lambda e, b=b, pg=pg: e.activation(sg[b], pg[:], AF.Silu),
                         reads=["P%d" % b], writes=[("sg", b)])
                    S.op("dve", lambda e, b=b, pu=pu, j=j, t2=t2: e.tensor_tensor(
                        H[:, j, t2 * 512:(t2 + 1) * 512], sg[b], pu[:], ALU.mult),
                        reads=[("sg", b), "P%d" % (2 + b)], writes=[("H", j, t2)])
            for m in range(KC):
                wsl, si = self.load_w_piece(wo, m * 128, 128, nk=JC)
                for t2 in range(2):
                    b = (m * 2 + t2) % 2
                    po = self.P[4 + b]
                    t0 = hb * 1024 + t2 * 512
                    for j in range(JC):
                        self.mm(po[:], wsl[:, j, :], H[:, j, t2 * 512:(t2 + 1) * 512], j == 0, j == JC - 1,
                                reads=[("ws", si), ("H", j, t2)], writes=["P%d" % (4 + b)])
                    S.op("dve", lambda e, po=po, m=m, t0=t0: e.scalar_tensor_tensor(
                        self.XT[:, m, t0:t0 + 512], po[:], 0.5, self.XT[:, m, t0:t0 + 512], ALU.mult, ALU.add),
                        reads=["P%d" % (4 + b)] + self.xk(m, t0, 512), writes=self.xk(m, t0, 512))

    def headnorm(self, src, dst, chunks, n, gcols, scr, srckeys, dstkeys):
        S = self.S
        ps = self.P[6]
        nc_ = len(chunks)
        for i, c in enumerate(chunks):
            b = i % 2
            sq = scr["sq"][b][:, 0:n]
            S.op("act", lambda e, sq=sq, c=c: e.activation(sq, src[:, c, 0:n], AF.Square),
                 reads=[srckeys[i]], writes=[("sq", b)])
            self.mm(ps[:, 0:n], self.ones_bf[:], sq, i == 0, i == nc_ - 1,
                    reads=[("sq", b), "ones_bf"], writes=["P6"], inc=True)
        rt = scr["rt"][:, 0:n]
        S.op("act", lambda e: e.activation(rt, ps[:, 0:n], AF.Sqrt, bias=RMS_EPS, scale=1.0 / (128 * nc_)),
             reads=["P6"], writes=["rt"])
        S.op("dve", lambda e: e.reciprocal(rt, rt), reads=["rt"], writes=["rt"])
        for i, c in enumerate(chunks):
            S.op("dve", lambda e, i=i, c=c: e.scalar_tensor_tensor(
                dst[:, c, 0:n], src[:, c, 0:n], self.PV[:, gcols + i:gcols + i + 1], rt, ALU.mult, ALU.mult),
                reads=[srckeys[i], "pv", "rt"], writes=[dstkeys[i]])

    def cross(self, l):
        S = self.S
        self.carve_reset()
        self.load_layer(l)
        wq = self.d_xwq[l].rearrange("(kc p) c -> p kc c", p=128)
        wkv = self.d_xwkv[l].rearrange("(kc p) c -> p kc c", p=128)
        wo = self.d_xwo[l].rearrange("(kc p) c -> p kc c", p=128)
        KT = self.carve(1024, BF16, 256)
        V = self.carve(1024, BF16, 1024)
        memn = self.carve(1024, BF16, 256)
        save = self.co
        Kf = self.carve(2048, F32, 256)
        MEMT = self.carve(2048, F32, 256)
        self.co = save
        qf = self.carve(4096, F32, 512)
        qn = self.carve(2048, BF16, 512)
        E = self.carve(1024, BF16, 512)
        ob = self.carve(2048, BF16, 512)
        HN = self.carve(2048, BF16, 512)
        sqb = self.carve(512, BF16)
        scr = {"sq": [sqb[:, 0:512], sqb[:, 512:1024]], "rt": self.carve(512)}
        rden = [self.carve(512), self.carve(512)]
        S.dma("sp", MEMT, self.d_memT.rearrange("(kc p) m -> p kc m", p=128), writes=["memT"])
        ps = self.P[6]
        for kc in range(KC):
            b = kc % 2
            sq = scr["sq"][b][:, 0:NMEM]
            S.op("act", lambda e, sq=sq, kc=kc: e.activation(sq, MEMT[:, kc, :], AF.Square),
                 reads=["memT"], writes=[("sq", b)])
            self.mm(ps[:, 0:NMEM], self.ones_bf[:], sq, kc == 0, kc == KC - 1,
                    reads=[("sq", b), "ones_bf"], writes=["P6"], inc=True)
        rt = scr["rt"][:, 0:NMEM]
        S.op("act", lambda e: e.activation(rt, ps[:, 0:NMEM], AF.Sqrt, bias=RMS_EPS, scale=1.0 / D),
             reads=["P6"], writes=["rt"])
        S.op("dve", lambda e: e.reciprocal(rt, rt), reads=["rt"], writes=["rt"])
        for kc in range(KC):
            S.op("dve", lambda e, kc=kc: e.scalar_tensor_tensor(
                memn[:, kc, :], MEMT[:, kc, :], self.PV[:, PV_MEM + kc:PV_MEM + kc + 1], rt, ALU.mult, ALU.mult),
                reads=["memT", "pv", "rt"], writes=[("memn", kc)])
        for pc in range(4):
            wsl, si = self.load_w_piece(wkv, pc * 256, 256)
            for mi in range(2):
                m = pc * 2 + mi
                pp = self.P[m % 2]
                for kc in range(KC):
                    self.mm(pp[:, 0:NMEM], wsl[:, kc, mi * 128:(mi + 1) * 128], memn[:, kc, :], kc == 0, kc == KC - 1,
                            reads=[("ws", si), ("memn", kc)], writes=["P%d" % (m % 2)])
                S.op("act", lambda e, m=m, pp=pp: e.copy(Kf[:, m, :], pp[:, 0:NMEM]), reads=["P%d" % (m % 2)], writes=[("Kf", m)])
        for h in range(4):
            self.headnorm(Kf, KT, [2 * h, 2 * h + 1], NMEM, PV_XK, scr,
                          [("Kf", 2 * h), ("Kf", 2 * h + 1)], [("KT", 2 * h), ("KT", 2 * h + 1)])
        for pc in range(4):
            wsl, si = self.load_w_piece(wkv, D + pc * 256, 256)
            for mt in range(2):
                pp = self.P[(pc * 2 + mt) % 2]
                for kc in range(KC):
                    self.mm(pp[:, 0:256], memn[:, kc, mt * 128:(mt + 1) * 128], wsl[:, kc, :], kc == 0, kc == KC - 1,
                            reads=[("ws", si), ("memn", kc)], writes=["P%d" % ((pc * 2 + mt) % 2)])
                S.op("act", lambda e, pp=pp, mt=mt, pc=pc: e.copy(V[:, mt, pc * 256:(pc + 1) * 256], pp[:, 0:256]),
                     reads=["P%d" % ((pc * 2 + mt) % 2)], writes=[("V", mt, pc)])
        qf_guard = [("Kf", m) for m in range(8)] + ["memT"]
        for tb in range(4):
            t0 = tb * 512
            self.rmsnorm(PV_CROSS, t0, 512, HN, 0, scr)
            for pc in range(4):
                wsl, si = self.load_w_piece(wq, pc * 256, 256)
                for mi in range(2):
                    m = pc * 2 + mi
                    pp = self.P[m % 2]
                    for kc in range(KC):
                        self.mm(pp[:], wsl[:, kc, mi * 128:(mi + 1) * 128], HN[:, kc, 0:512], kc == 0, kc == KC - 1,
                                reads=[("ws", si)] + self.hk(kc, 0, 512), writes=["P%d" % (m % 2)])
                    S.op("act", lambda e, m=m, pp=pp: e.copy(qf[:, m, :], pp[:]), reads=["P%d" % (m % 2)],
                         writes=[("qf", m)] + (qf_guard if tb == 0 else []))
            for h in range(4):
                self.headnorm(qf, qn, [2 * h, 2 * h + 1], 512, PV_XQ, scr,
                              [("qf", 2 * h), ("qf", 2 * h + 1)], [("qn", 2 * h), ("qn", 2 * h + 1)])
            for h in range(4):
                eb = h % 2
                for mt in range(2):
                    pp = self.P[2 + mt]
                    for c in range(2):
                        self.mm(pp[:], KT[:, 2 * h + c, mt * 128:(mt + 1) * 128], qn[:, 2 * h + c, :], c == 0, c == 1,
                                reads=[("KT", 2 * h + c), ("qn", 2 * h + c)], writes=["P%d" % (2 + mt)])
                    S.op("act", lambda e, pp=pp, eb=eb, mt=mt: e.activation(E[:, eb * 2 + mt, :], pp[:], AF.Exp, scale=1.0 / 16.0),
                         reads=["P%d" % (2 + mt)], writes=[("E", eb, mt)])
                for c in range(2):
                    pp = self.P[4 + c]
                    for mt in range(2):
                        self.mm(pp[:], V[:, mt, h * 256 + c * 128:h * 256 + (c + 1) * 128], E[:, eb * 2 + mt, :], mt == 0, mt == 1,
                                reads=[("V", mt, h), ("E", eb, mt)], writes=["P%d" % (4 + c)])
                pd = self.P[7]
                for mt in range(2):
                    self.mm(pd[:], self.ones_bf[:], E[:, eb * 2 + mt, :], mt == 0, mt == 1,
                            reads=["ones_bf", ("E", eb, mt)], writes=["P7"])
                S.op("dve", lambda e, eb=eb: e.reciprocal(rden[eb], pd[:]), reads=["P7"], writes=[("rden", eb)])
                for c in range(2):
                    pp = self.P[4 + c]
                    S.op("dve", lambda e, pp=pp, c=c, h=h, eb=eb: e.tensor_tensor(ob[:, 2 * h + c, :], pp[:], rden[eb], ALU.mult),
                         reads=["P%d" % (4 + c), ("rden", eb)], writes=[("ob", 2 * h + c)])
            for pc in range(4):
                wsl, si = self.load_w_piece(wo, pc * 256, 256)
                for mi in range(2):
                    m = pc * 2 + mi
                    pp = self.P[m % 2]
                    for kc in range(KC):
                        self.mm(pp[:], wsl[:, kc, mi * 128:(mi + 1) * 128], ob[:, kc, :], kc == 0, kc == KC - 1,
                                reads=[("ws", si), ("ob", kc)], writes=["P%d" % (m % 2)])
                    S.op("dve", lambda e, m=m, pp=pp, t0=t0: e.tensor_tensor(
                        self.XT[:, m, t0:t0 + 512], pp[:], self.XT[:, m, t0:t0 + 512], ALU.add),
                        reads=["P%d" % (m % 2)] + self.xk(m, t0, 512), writes=self.xk(m, t0, 512))

    def mixer(self, l):
        S = self.S
        mixers = self.cfg.get("mixers", "abc")
        self.carve_reset()
        self.load_layer(l)
        c = self.carve
        M = self.M = {}
        M["HN"] = c(1024, BF16, TB)
        M["KB"] = c(1152, BF16, 768)
        M["VB"] = c(1152, BF16, 384)
        M["BIAS"] = c(1920, BF16, 640)
        M["MIX"] = c(1024, BF16, TB)
        sqb = c(256, BF16)
        M["scr"] = {"sq": [sqb[:, 0:TB], sqb[:, TB:2 * TB]], "rt": c(256)}
        u0 = self.co
        raw = c(260)
        M["RAW"] = [raw, raw]
        M["TMP"] = c(TB)
        M["TMP2"] = c(TB)
        M["XS"] = c(9 * TB, F32, TB)
        for nm in ("TG", "CUM", "A", "KK", "B", "EX1", "EX2", "YS", "YC", "VD", "VF", "PT1", "PT2"):
            M[nm] = c(TB)
        for nm in ("G", "BON"):
            M[nm] = [c(TB), c(TB)]
        for nm in ("AR", "BK", "BKH", "VV"):
            M[nm] = [c(NCH * 128).rearrange("p (c s t) -> p c s t", s=2, t=64) for _ in range(2)]
        M["GT"] = [c(NCH * 256).rearrange("p (h c x) -> p h c x", h=2, x=128) for _ in range(2)]
        M["X"] = [c(NCH * 64, BF16, 64), c(NCH * 64, BF16, 64)]
        M["XTb"] = [c(NCH * 64, BF16, 64), c(NCH * 64, BF16, 64)]
        M["XT0b"] = c(NCH * 64, BF16, 64)
        M["TTb"] = c(NCH * 64, BF16, 64)
        M["TT"] = [c(NCH * 128, F32, 64), c(NCH * 128, F32, 64)]
        M["BKHT"] = c(NCH * 128, F32, 64)
        M["UV"] = c(NCH * 128, F32, 64)
        M["Z"] = c(128)
        M["WC"] = [c(16), c(16)]
        M["QF"] = c(TB)
        M["QN"] = c(3 * TB // 2, BF16, TB)
        sbt = c(640)
        M["SBt"] = [sbt, sbt]
        M["PT"] = [c(320, BF16), c(320, BF16)]
        M["rden"] = [c(128), c(128)]
        W = TB + 16
        M["LV"] = [c(2 * W, F32, W) for _ in range(3)]
        M["POOLED"] = M["LV"][0][:, :, 16:W]
        btmp = self.SCR[:, u0:u0 + 3840].rearrange("p (h x) -> p h x", x=640)
        S.dma("sp", btmp, self.d_bias[l].rearrange("p (h x) -> p h x", x=640), writes=["btmp"])
        S.op("act", lambda e: e.copy(M["BIAS"], btmp), reads=["btmp"], writes=["bias"])
        S.barrier()
        S.op("dve", lambda e: e.memset(self.HS[:], 0.0), reads=[], writes=[("HS", 0), ("HS", 1), ("HS", 2)])
        S.op("dve", lambda e: e.memset(self.CARRY[:], 0.0), writes=["carry"])
        S.op("dve", lambda e: e.memset(self.UHIST[:], 0.0), writes=["uhist"])
        S.op("dve", lambda e: e.memset(M["MIX"][:], 0.0), writes=[("mix", k) for k in range(8)])
        winT = self.d_win[l].rearrange("(kc p) c -> p kc c", p=128)
        woutT = self.d_wout[l].rearrange("(kc p) c -> p kc c", p=128)
        ro_prev = []
        nb = self.cfg.get("nblocks", T // TB)

        def rec_head(tb):
            self.slot_group = (0, 1)
            S.rec_begin()
            self.rmsnorm(PV_MIX, tb * TB, TB, M["HN"], 0, M["scr"])
            if "a" in mixers:
                self.rwkv_block(l, tb, winT)
            r = S.rec_end()
            self.slot_group = None
            return r
        for it in rec_head(0):
            S.replay(it)
        for tb in range(nb):
            t0 = tb * TB
            ra = []
            if "a" in mixers:
                S.rec_begin()
                self.rwkv_main(l, tb)
                ra = S.rec_end()
            self.slot_group = (2, 3)
            S.rec_begin()
            if "c" in mixers:
                self.pool_block(l, tb, winT)
            if "b" in mixers:
                self.attn_block(l, tb, winT)
            rb = S.rec_end()
            self.slot_group = None
            nxt_head = rec_head(tb + 1) if tb + 1 < nb else []
            ra_h, ra_t = split_at_mix_write(ra)
            rb_h, rb_t = split_at_mix_write(rb)
            tail = merge(ra_t, rb_t)
            t1, t2 = split_for_next_head(tail, nxt_head) if nxt_head else (tail, [])
            merged = merge(ra_h, ro_prev + rb_h) + t1 + merge(t2, nxt_head)
            for it in merged:
                S.replay(it)
            self.slot_group = (2, 3)
            S.rec_begin()
            for pc in range(4):
                wsl, si = self.load_w_piece(woutT, pc * 256, 256)
                for mi in range(2):
                    m = pc * 2 + mi
                    bk = 6 + m % 2
                    pp = self.P[bk]
                    for kc in range(KC):
                        self.mm(pp[:, 0:TB], wsl[:, kc, mi * 128:(mi + 1) * 128], M["MIX"][:, kc, :], kc == 0, kc == KC - 1,
                                reads=[("ws", si), ("mix", kc)], writes=["P%d" % bk])
                    S.op("dve", lambda e, m=m, pp=pp, t0=t0: e.tensor_tensor(
                        self.XT[:, m, t0:t0 + TB], pp[:, 0:TB], self.XT[:, m, t0:t0 + TB], ALU.add),
                        reads=["P%d" % bk] + self.xk(m, t0, TB), writes=self.xk(m, t0, TB))
            ro_prev = S.rec_end()
            self.slot_group = None
        for it in ro_prev:
            S.replay(it)

    def proj_fm(self, wsl, si, coff, bank, n=TB):
        pp = self.P[bank]
        for kc in range(KC):
            self.mm(pp[:, 0:n], wsl[:, kc, coff:coff + 128], self.M["HN"][:, kc, 0:n], kc == 0, kc == KC - 1,
                    reads=[("ws", si)] + self.hk(kc, 0, n), writes=["P%d" % bank])
        return pp

    def rwkv_block(self, l, tb, winT):
        S, M, PV = self.S, self.M, self.PV
        t0 = tb * TB
        XS, TMP = M["XS"], M["TMP"]
        pieces = [(0, 384), (384, 384), (768, 384), (1152, 128)]
        cur = None
        for q in range(10):
            pi, coff = (q // 3, (q % 3) * 128) if q < 9 else (3, 0)
            if cur is None or cur[0] != pi:
                wsl, si = self.load_w_piece(winT, pieces[pi][0], pieces[pi][1])
                cur = (pi, wsl, si)
            bank = (1, 6)[q % 2]
            pp = self.proj_fm(cur[1], cur[2], coff, bank)
            RAW = M["RAW"][q % 2]
            rk = ("raw", 0)
            S.op("dve", lambda e, RAW=RAW, q=q: e.tensor_copy(RAW[:, 0:1], self.CARRY[:, q:q + 1]), reads=["carry"], writes=[rk])
            S.op("act", lambda e, RAW=RAW, pp=pp: e.copy(RAW[:, 1:TB + 1], pp[:, 0:TB]), reads=["P%d" % bank], writes=[rk])
            S.op("dve", lambda e, RAW=RAW, q=q: e.tensor_copy(self.CARRY[:, q:q + 1], RAW[:, TB:TB + 1]), reads=[rk], writes=["carry"])
            S.op("dve", lambda e, RAW=RAW: e.tensor_tensor(TMP, RAW[:, 0:TB], RAW[:, 1:TB + 1], ALU.subtract), reads=[rk], writes=["TMP"])
            dst = XS[:, q, :] if q < 9 else M["TG"]
            S.op("dve", lambda e, RAW=RAW, q=q, dst=dst: e.scalar_tensor_tensor(
                dst, TMP, PV[:, PV_MU + q:PV_MU + q + 1], RAW[:, 1:TB + 1], ALU.mult, ALU.add),
                reads=["TMP", rk, "pv"], writes=[("xs", q) if q < 9 else "TG"])
        TG = M["TG"]
        S.op("act", lambda e: e.activation(TG[0:32, :], TG[0:32, :], AF.Tanh), reads=["TG"], writes=["TG"])
        S.op("act", lambda e: e.activation(TG[64:128, :], TG[64:128, :], AF.Sigmoid), reads=["TG"], writes=["TG"])
        if l == 0:
            for hp in range(3):
                S.dma("sp", self.d_vf[hp][:, t0:t0 + TB], XS[:, 6 + hp, :], reads=[("xs", 6 + hp)], writes=[("vf", hp, tb)], semkey=("vfst", hp))
        else:
            pvd = self.P[5]
            for hp in range(3):
                self.mm(pvd[0:32, 0:TB], PV[:, PV_VDOWN + hp * 32:PV_VDOWN + (hp + 1) * 32], XS[:, 6 + hp, :], hp == 0, hp == 2,
                        reads=["pv", ("xs", 6 + hp)], writes=["P5"])
            VD, VF = M["VD"], M["VF"]
            S.op("act", lambda e: e.copy(VD[0:32, :], pvd[0:32, 0:TB]), reads=["P5"], writes=["VD"])
            for hp in range(3):
                self.mm(pvd[:, 0:TB], PV[0:32, PV_VUP + hp * 128:PV_VUP + (hp + 1) * 128], VD[0:32, :], True, True,
                        reads=["pv", "VD"], writes=["P5"])
                S.op("act", lambda e, hp=hp: e.activation(TMP, pvd[:, 0:TB], AF.Sigmoid, bias=PV[:, PV_V0 + hp:PV_V0 + hp + 1]),
                     reads=["P5", "pv"], writes=["TMP"])
                S.dma("sp", VF, self.d_vf[hp][:, t0:t0 + TB], reads=[("vf", hp, tb)], writes=["VF"])
                S.op("dve", lambda e, hp=hp: e.tensor_tensor(VF, VF, XS[:, 6 + hp, :], ALU.subtract), reads=["VF", ("xs", 6 + hp)], writes=["VF"])
                S.op("dve", lambda e: e.tensor_tensor(VF, VF, TMP, ALU.mult), reads=["VF", "TMP"], writes=["VF"])
                S.op("dve", lambda e, hp=hp: e.tensor_tensor(XS[:, 6 + hp, :], XS[:, 6 + hp, :], VF, ALU.add),
                     reads=["VF", ("xs", 6 + hp)], writes=[("xs", 6 + hp)])
        self.rwkv_prep(l, tb, 0)
        self.rwkv_P(l, tb, 0)

    def rwkv_main(self, l, tb):
        S = self.S

        def rec(fn, *a):
            S.rec_begin()
            fn(*a)
            return S.rec_end()
        for hp in range(3):
            main = rec(self.rwkv_TS, l, tb, hp) + rec(self.rwkv_post, l, tb, hp)
            nxt = (rec(self.rwkv_prep, l, tb, hp + 1) + rec(self.rwkv_P, l, tb, hp + 1)) if hp < 2 else []
            for it in merge(main, nxt):
                S.replay(it)

    def rwkv_prep(self, l, tb, hp):
        S, M, PV, CST = self.S, self.M, self.PV, self.CST
        pb = (3 * tb + hp) % 2
        XS, TMP, TMP2, TG = M["XS"], M["TMP"], M["TMP2"], M["TG"]
        CUM, A_, KK, B_, EX1, EX2 = (M[k] for k in ("CUM", "A", "KK", "B", "EX1", "EX2"))
        G, BON, AR, BK, BKH, VV, WC = (M[k][pb] for k in ("G", "BON", "AR", "BK", "BKH", "VV", "WC"))
        kG, kBON, kAR, kBK, kBKH, kVV, kWC = (("G", pb), ("BON", pb), ("AR", pb), ("BK", pb), ("BKH", pb), ("VV", pb), ("WC", pb))
        bo = CST[:, C_BO:C_BO + 128]
        r, k, v = XS[:, hp, :], XS[:, 3 + hp, :], XS[:, 6 + hp, :]
        rk_, kk_, vk_ = ("xs", hp), ("xs", 3 + hp), ("xs", 6 + hp)
        lo = PV_LORA + hp * 128
        P6 = self.P[6][:, 0:TB]

        def pcol(base):
            return PV[:, base + hp:base + hp + 1]

        def c4(ap):
            return ap.rearrange("p (c t) -> p c t", t=64)
        self.mm(P6, PV[0:32, lo:lo + 128], TG[0:32, :], True, True, reads=["pv", "TG"], writes=["P6"])
        S.op("act", lambda e: e.activation(CUM, P6, AF.Sigmoid, bias=pcol(PV_W0)), reads=["P6", "pv"], writes=["E"])
        self.mm(P6, PV[32:64, lo:lo + 128], TG[32:64, :], True, True, reads=["pv", "TG"], writes=["P6"])
        S.op("act", lambda e: e.activation(A_, P6, AF.Sigmoid, bias=pcol(PV_A0)), reads=["P6", "pv"], writes=["A"])
        self.mm(P6, PV[64:128, lo:lo + 128], TG[64:128, :], True, True, reads=["pv", "TG"], writes=["P6"])
        S.op("act", lambda e: e.copy(G, P6), reads=["P6"], writes=[kG])
        S.op("act", lambda e: e.activation(TMP, k, AF.Square, scale=pcol(PV_KK)), reads=[kk_, "pv"], writes=["TMP"])
        self.mm(P6, bo, TMP, True, True, reads=["cst", "TMP"], writes=["P6"])
        S.op("act", lambda e: e.activation(TMP2, P6, AF.Sqrt), reads=["P6"], writes=["TMP2"])
        S.op("dve", lambda e: e.tensor_scalar(TMP2, TMP2, 1e-12, None, ALU.max), reads=["TMP2"], writes=["TMP2"])
        S.op("dve", lambda e: e.reciprocal(TMP2, TMP2), reads=["TMP2"], writes=["TMP2"])
        S.op("dve", lambda e: e.scalar_tensor_tensor(KK, k, pcol(PV_KK), TMP2, ALU.mult, ALU.mult), reads=[kk_, "pv", "TMP2"], writes=["KK"])
        S.op("dve", lambda e: e.tensor_scalar(TMP, A_, -1.0, pcol(PV_KA), ALU.add, ALU.mult), reads=["A", "pv"], writes=["TMP"])
        S.op("dve", lambda e: e.scalar_tensor_tensor(k, TMP, 1.0, k, ALU.add, ALU.mult), reads=["TMP", kk_], writes=[kk_])
        S.op("dve", lambda e: e.tensor_tensor(B_, KK, A_, ALU.mult), reads=["KK", "A"], writes=["B"])
        S.op("dve", lambda e: e.scalar_tensor_tensor(TMP, r, pcol(PV_RK), k, ALU.mult, ALU.mult), reads=[rk_, kk_, "pv"], writes=["TMP"])
        self.mm(P6, bo, TMP, True, True, reads=["cst", "TMP"], writes=["P6"])
        S.op("dve", lambda e: e.tensor_tensor(BON, P6, v, ALU.mult), reads=["P6", vk_], writes=[kBON])
        S.op("act", lambda e: e.copy(TMP2, CUM), reads=["E"], writes=["TMP2"])
        smask = CST[:, C_SM:C_SM + TB]
        S.op("dve", lambda e: e.tensor_tensor_scan(CUM, smask, TMP2, 0.0, ALU.mult, ALU.add),
             reads=["TMP2", "cst", "E"], writes=["E"])
        S.op("act", lambda e: e.activation(EX1, CUM, AF.Exp, scale=-SDEC), reads=["E"], writes=["EX1"])
        S.op("dve", lambda e: e.tensor_tensor(AR[:, :, 1, :], c4(r), c4(EX1), ALU.mult), reads=[rk_, "EX1"], writes=[kAR])
        S.op("act", lambda e: e.activation(EX2, CUM, AF.Exp, scale=SDEC), reads=["E"], writes=["EX2"])
        S.op("dve", lambda e: e.tensor_tensor(BK[:, :, 0, :], c4(B_), c4(EX2), ALU.mult), reads=["B", "EX2"], writes=[kBK])
        S.op("dve", lambda e: e.tensor_tensor(BK[:, :, 1, :], c4(k), c4(EX2), ALU.mult), reads=[kk_, "EX2"], writes=[kBK])
        S.op("dve", lambda e: e.tensor_tensor(TMP, CUM, TMP2, ALU.subtract), reads=["E", "TMP2"], writes=["TMP"])
        S.op("act", lambda e: e.activation(EX1, TMP, AF.Exp, scale=-SDEC), reads=["TMP"], writes=["EX1"])
        S.op("dve", lambda e: e.scalar_tensor_tensor(AR[:, :, 0, :], c4(KK), -1.0, c4(EX1), ALU.mult, ALU.mult), reads=["KK", "EX1"], writes=[kAR])
        S.op("dve", lambda e: e.tensor_tensor(c4(TMP), c4(CUM), c4(CUM)[:, :, 63:64].to_broadcast([128, NCH, 64]), ALU.subtract),
             reads=["E", "EX1"], writes=["TMP"])
        S.op("act", lambda e: e.activation(EX2, TMP, AF.Exp, scale=SDEC), reads=["TMP"], writes=["EX2"])
        S.op("dve", lambda e: e.tensor_tensor(BKH[:, :, 0, :], c4(B_), c4(EX2), ALU.mult), reads=["B", "EX2"], writes=[kBKH])
        S.op("dve", lambda e: e.tensor_tensor(BKH[:, :, 1, :], c4(k), c4(EX2), ALU.mult), reads=[kk_, "EX2"], writes=[kBKH])
        S.op("act", lambda e: e.activation(WC[:, 0:NCH], c4(CUM)[:, :, 63], AF.Exp, scale=-SDEC), reads=["E"], writes=[kWC])
        S.op("dve", lambda e: e.memset(VV[:, :, 0, :], 0.0), writes=[kVV])
        S.op("act", lambda e: e.copy(VV[:, :, 1, :], c4(v)), reads=[vk_], writes=[kVV])

    def rwkv_P(self, l, tb, hp):
        S, M, CST = self.S, self.M, self.CST
        pb = (3 * tb + hp) % 2
        AR, BK, BKH, VV, WC = (M[k][pb] for k in ("AR", "BK", "BKH", "VV", "WC"))
        kAR, kBK, kBKH, kVV, kWC = (("AR", pb), ("BK", pb), ("BKH", pb), ("VV", pb), ("WC", pb))
        GT, TT = M["GT"][pb], M["TT"][pb]
        kGT, kTT = ("GT", pb), ("TT", pb)
        BKHT, UV, Z = (M[k] for k in ("BKHT", "UV", "Z"))
        ident = CST[:, C_ID:C_ID + 128]

        def f2(ap):
            return ap.rearrange("p a b -> p (a b)")
        for hh in range(2):
            hs = slice(hh * 64, hh * 64 + 64)
            for c in range(NCH):
                self.mm(self.P[2 + hh][:, c * 128:(c + 1) * 128], f2(BK[hs, c]), f2(AR[hs, c]), True, True,
                        reads=[kBK, kAR], writes=["P%d" % (2 + hh)])
        mg = CST[:, C_MG:C_MG + 128].unsqueeze(1).to_broadcast([128, NCH, 128])
        for hh in range(2):
            S.op("dve", lambda e, hh=hh: e.tensor_tensor(
                GT[:, hh], self.P[2 + hh][:, 0:NCH * 128].rearrange("p (c x) -> p c x", x=128), mg, ALU.mult),
                reads=["P%d" % (2 + hh), "cst"], writes=[kGT])
        X, XTb = M["X"], M["XTb"]
        NI = NCH * 2
        for hh in range(2):
            hs = slice(hh * 64, hh * 64 + 64)
            for c in range(NCH):
                self.mm(self.P[4 + hh][0:64, c * 64:c * 64 + 64], AR[hs, c, 0, :], BK[hs, c, 0, :], True, True,
                        reads=[kAR, kBK], writes=["P%d" % (4 + hh)])
        ml = CST[0:64, C_ML:C_ML + 64].unsqueeze(1).to_broadcast([64, NCH, 64])

        def p3(bank, n=NI):
            return self.P[bank][0:64, 0:n * 64].rearrange("p (a b) -> p a b", b=64)
        for hh in range(2):
            S.op("dve", lambda e, hh=hh: e.tensor_tensor(X[0][0:64, hh * NCH:(hh + 1) * NCH, :], p3(4 + hh, NCH), ml, ALU.mult),
                 reads=["P%d" % (4 + hh), "cst"], writes=[("X", 0)])
        xt0 = GT[0:64, :, :, 0:64].rearrange("p h c x -> p (h c) x")
        id3 = ident[0:64, 0:64].unsqueeze(1).to_broadcast([64, NI, 64])
        S.op("dve", lambda e: e.tensor_tensor(TT[0:64], xt0, id3, ALU.add), reads=[kGT, "cst"], writes=[kTT])
        XT0b, TTb = M["XT0b"], M["TTb"]
        S.op("act", lambda e: e.copy(XT0b[0:64], xt0), reads=[kGT], writes=["XT0b"])
        S.op("act", lambda e: e.copy(TTb[0:64], TT[0:64]), reads=[kTT], writes=["TTb"])
        xc, xtc, xck, xtck = X[0], XT0b, ("X", 0), "XT0b"
        for j in range(1, 6):
            xn, xnk = X[j % 2], ("X", j % 2)
            xtn, xtnk = XTb[j % 2], ("XT", j % 2)
            for i in range(NI):
                self.mm(self.P[4][0:64, i * 64:i * 64 + 64], xtc[0:64, i, :], xc[0:64, i, :], True, True,
                        reads=[xck, xtck], writes=["P4"])
            if j < 5:
                for i in range(NI):
                    self.mm(self.P[5][0:64, i * 64:i * 64 + 64], xc[0:64, i, :], xtc[0:64, i, :], True, True,
                            reads=[xck, xtck], writes=["P5"])
            S.op("act", lambda e, xn=xn: e.copy(xn[0:64], p3(4)), reads=["P4"], writes=[xnk])
            if j < 5:
                S.op("dve", lambda e, xtn=xtn: e.tensor_copy(xtn[0:64], p3(5)), reads=["P5"], writes=[xtnk])
            for i in range(NI):
                self.mm(self.P[7][0:64, i * 64:i * 64 + 64], xn[0:64, i, :], TTb[0:64, i, :], True, True,
                        reads=[xnk, "TTb"], writes=["P7"])
            S.op("dve", lambda e: e.tensor_tensor(TT[0:64], p3(7), TT[0:64], ALU.add), reads=["P7", kTT], writes=[kTT])
            if j < 5:
                S.op("act", lambda e: e.copy(TTb[0:64], TT[0:64]), reads=[kTT], writes=["TTb"])
            xc, xtc, xck, xtck = xn, xtn, xnk, xtnk

    def rwkv_TS(self, l, tb, hp):
        S, M, CST = self.S, self.M, self.CST
        pb = (3 * tb + hp) % 2
        AR, BK, BKH, VV, WC = (M[k][pb] for k in ("AR", "BK", "BKH", "VV", "WC"))
        kAR, kBK, kBKH, kVV, kWC = (("AR", pb), ("BK", pb), ("BKH", pb), ("VV", pb), ("WC", pb))
        GT, TT = M["GT"][pb], M["TT"][pb]
        kGT, kTT = ("GT", pb), ("TT", pb)
        BKHT, UV, Z = (M[k] for k in ("BKHT", "UV", "Z"))
        ident = CST[:, C_ID:C_ID + 128]

        def f2(ap):
            return ap.rearrange("p a b -> p (a b)")
        for hh in range(2):
            hs = slice(hh * 64, hh * 64 + 64)
            for c in range(NCH):
                S.op("pe", lambda e, c=c, hs=hs, hh=hh: e.transpose(self.P[2 + hh][:, c * 64:c * 64 + 64], f2(BKH[hs, c]), ident[hs, hs]),
                     reads=[kBKH, "cst"], writes=["P%d" % (2 + hh)], pe_meta=(hh * 64, 64, 2 + hh))
                S.op("pe", lambda e, c=c, hs=hs, hh=hh: e.transpose(self.P[4 + hh][:, c * 64:c * 64 + 64], f2(VV[hs, c]), ident[hs, hs]),
                     reads=[kVV, "cst"], writes=["P%d" % (4 + hh)], pe_meta=(hh * 64, 64, 4 + hh))
        for hh in range(2):
            S.op("dve", lambda e, hh=hh: e.tensor_copy(BKHT[:, hh * NCH:(hh + 1) * NCH, :],
                                                       self.P[2 + hh][:, 0:NCH * 64].rearrange("p (a b) -> p a b", b=64)),
                 reads=["P%d" % (2 + hh)], writes=["BKHT"])
            S.op("act", lambda e, hh=hh: e.copy(UV[64:128, hh * NCH:(hh + 1) * NCH, :],
                                                self.P[4 + hh][64:128, 0:NCH * 64].rearrange("p (a b) -> p a b", b=64)),
                 reads=["P%d" % (4 + hh)], writes=["UVv"])
        HSk = ("HS", hp)
        pZ, pU, pY, pH = self.P[1], self.P[7], self.P[0], self.P[5]
        UVh = UV[0:64].rearrange("p (h c) v -> p h c v", h=2)
        for c in range(NCH):
            for hh in range(2):
                hs = slice(hh * 64, hh * 64 + 64)
                i = hh * NCH + c
                t1 = self.mm(pZ[0:64, hh * 64:hh * 64 + 64], AR[hs, c, 0, :], self.HS[hs, hp, :], True, False,
                             reads=[kAR, HSk], writes=["P1"], inc=True)
                self.mm(pZ[0:64, hh * 64:hh * 64 + 64], GT[64:128, hh, c, 0:64], UV[64:128, i, :], False, True,
                        reads=[kGT, "UVv"], writes=["P1"], after=(t1 if hh == 0 else None))
            S.op("act", lambda e: e.copy(Z[0:64, :], pZ[0:64, 0:128]), reads=["P1"], writes=["Z"])
            for hh in range(2):
                i = hh * NCH + c
                self.mm(pU[0:64, hh * 64:hh * 64 + 64], TT[0:64, i, :], Z[0:64, hh * 64:hh * 64 + 64], True, True,
                        reads=[kTT, "Z"], writes=["P7"])
            S.op("dve", lambda e, c=c: e.tensor_copy(UVh[:, :, c, :], pU[0:64, 0:128].rearrange("p (a b) -> p a b", b=64)),
                 reads=["P7"], writes=[("UVu", c)])
            for hh in range(2):
                hs = slice(hh * 64, hh * 64 + 64)
                i = hh * NCH + c
                self.mm(pY[hs, c * 64:c * 64 + 64], self.HS[hs, hp, :], AR[hs, c, 1, :], True, False,
                        reads=[HSk, kAR], writes=["P0"])
                self.mm(pY[hs, c * 64:c * 64 + 64], UV[:, i, :], GT[:, hh, c, 64:128], False, True,
                        reads=[("UVu", c), "UVv", kGT], writes=["P0"])
            for hh in range(2):
                hs = slice(hh * 64, hh * 64 + 64)
                i = hh * NCH + c
                self.mm(pH[hs, 0:64], BKHT[:, i, :], UV[:, i, :], True, True,
                        reads=["BKHT", ("UVu", c), "UVv"], writes=["P5"])
            S.op("dve", lambda e, c=c: e.scalar_tensor_tensor(self.HS[:, hp, :], self.HS[:, hp, :], WC[:, c:c + 1], pH[:, 0:64], ALU.mult, ALU.add),
                 reads=["P5", kWC, HSk], writes=[HSk])

    def rwkv_post(self, l, tb, hp):
        S, M, PV, CST = self.S, self.M, self.PV, self.CST
        pb = (3 * tb + hp) % 2
        YS, YC, T1, T2 = M["YS"], M["YC"], M["PT1"], M["PT2"]
        G, BON = M["G"][pb], M["BON"][pb]
        bo = CST[:, C_BO:C_BO + 128]
        pY = self.P[0]
        P6 = self.P[6][:, 0:TB]

        def pcol(base):
            return PV[:, base + hp:base + hp + 1]
        S.op("act", lambda e: e.copy(YS, pY[:, 0:TB]), reads=["P0"], writes=["YS"])
        self.mm(P6, bo, YS, True, True, reads=["cst", "YS"], writes=["P6"])
        S.op("dve", lambda e: e.scalar_tensor_tensor(YC, P6, -1.0 / 64.0, YS, ALU.mult, ALU.add), reads=["P6", "YS"], writes=["YC"])
        S.op("act", lambda e: e.activation(T1, YC, AF.Square), reads=["YC"], writes=["PT1"])
        self.mm(P6, bo, T1, True, True, reads=["cst", "PT1"], writes=["P6"])
        S.op("act", lambda e: e.activation(T2, P6, AF.Sqrt, bias=GN_EPS, scale=1.0 / 64.0), reads=["P6"], writes=["PT2"])
        S.op("dve", lambda e: e.reciprocal(T2, T2), reads=["PT2"], writes=["PT2"])
        S.op("dve", lambda e: e.tensor_tensor(YC, YC, T2, ALU.mult), reads=["YC", "PT2"], writes=["YC"])
        S.op("act", lambda e: e.activation(YC, YC, AF.Identity, bias=pcol(PV_GNB), scale=pcol(PV_GNG)), reads=["YC", "pv"], writes=["YC"])
        S.op("dve", lambda e: e.tensor_tensor(YC, YC, BON, ALU.add), reads=["YC", ("BON", pb)], writes=["YC"])
        S.op("dve", lambda e: e.tensor_tensor(M["MIX"][:, hp, :], YC, G, ALU.mult), reads=["YC", ("G", pb)], writes=[("mix", hp)])

    def attn_block(self, l, tb, winT):
        S, M, PV = self.S, self.M, self.PV
        t0 = tb * TB
        QF, QN, KB, VB, BIAS, MIX = M["QF"], M["QN"], M["KB"], M["VB"], M["BIAS"], M["MIX"]
        scr = M["scr"]

        def hnorm(dst, gcol, dkey):
            sq = scr["sq"][0][:, 0:TB]
            rt = scr["rt"][:, 0:TB]
            S.op("act", lambda e: e.activation(sq, QF, AF.Square), reads=["QF"], writes=[("sq", 0)])
            self.mm(self.P[6][:, 0:TB], self.bo_bf[:], sq, True, True, reads=["bo_bf", ("sq", 0)], writes=["P6"])
            S.op("act", lambda e: e.activation(rt, self.P[6][:, 0:TB], AF.Sqrt, bias=RMS_EPS, scale=1.0 / 64.0), reads=["P6"], writes=["rt"])
            S.op("dve", lambda e: e.reciprocal(rt, rt), reads=["rt"], writes=["rt"])
            S.op("dve", lambda e: e.scalar_tensor_tensor(dst, QF, PV[:, gcol:gcol + 1], rt, ALU.mult, ALU.mult),
                 reads=["QF", "pv", "rt"], writes=[dkey])
        for part, c0 in (("q", 1280), ("k", 1664)):
            wsl, si = self.load_w_piece(winT, c0, 384)
            for hp in range(3):
                bank = (1, 7)[hp % 2]
                pp = self.proj_fm(wsl, si, hp * 128, bank)
                S.op("act", lambda e, pp=pp: e.copy(QF, pp[:, 0:TB]), reads=["P%d" % bank], writes=["QF"])
                if part == "q":
                    hnorm(QN[:, hp, :], PV_BQG, ("qn", hp))
                else:
                    hnorm(KB[:, hp, t0 % 768:t0 % 768 + TB], PV_BKG, ("kb", hp))
        wsl, si = self.load_w_piece(winT, 2048, 384)
        for i in range(NQT):
            qt = tb * NQT + i
            vbk = (1, 7)[i % 2]
            pp = self.P[vbk]
            for kc in range(KC):
                self.mm(pp[:, 0:384], M["HN"][:, kc, i * 128:(i + 1) * 128], wsl[:, kc, :], kc == 0, kc == KC - 1,
                        reads=[("ws", si)] + self.hk(kc, 0, TB), writes=["P%d" % vbk])
            S.op("act", lambda e, pp=pp, qt=qt: e.copy(VB[:, qt % 6, :], pp[:, 0:384]), reads=["P%d" % vbk], writes=["vb"])
        for i in range(NQT):
            qt = tb * NQT + i
            r0 = max(0, 4 - qt)
            for hp in range(3):
                ob = 4
                pO, pD = self.P[ob], self.P[ob + 1]
                for hh in range(2):
                    h = 2 * hp + hh
                    hs = slice(hh * 64, hh * 64 + 64)
                    b = hh
                    SBt, PT = M["SBt"][b], M["PT"][b]
                    sbk = ("sbt", 0)
                    for r in range(r0, 5):
                        kt = qt - 4 + r
                        sb = 6 if hh == 1 else 2
                        if r < 4:
                            out, bk = self.P[sb][:, r * 128:(r + 1) * 128], "P%d" % sb
                        else:
                            out, bk = self.P[sb + 1][:, 0:128], "P%d" % (sb + 1)
                        self.mm(out, KB[hs, hp, (kt % 6) * 128:(kt % 6 + 1) * 128], QN[hs, hp, i * 128:(i + 1) * 128], True, True,
                                reads=[("kb", hp), ("qn", hp)], writes=[bk])
                    if r0 < 4:
                        S.op("dve", lambda e, SBt=SBt, h=h, r0=r0, sb=sb: e.scalar_tensor_tensor(
                            SBt[:, r0 * 128:512], self.P[sb][:, r0 * 128:512], 0.125, BIAS[:, h, r0 * 128:512], ALU.mult, ALU.add),
                            reads=["P%d" % sb, "bias"], writes=[sbk])
                    S.op("dve", lambda e, SBt=SBt, h=h, sb=sb: e.scalar_tensor_tensor(
                        SBt[:, 512:640], self.P[sb + 1][:, 0:128], 0.125, BIAS[:, h, 512:640], ALU.mult, ALU.add),
                        reads=["P%d" % (sb + 1), "bias"], writes=[sbk])
                    S.op("act", lambda e, SBt=SBt, PT=PT, r0=r0: e.activation(PT[:, r0 * 128:640], SBt[:, r0 * 128:640], AF.Exp),
                         reads=[sbk], writes=[("pt", b)])
                    for r in range(r0, 5):
                        kt = qt - 4 + r
                        self.mm(pO[hs, 0:128], VB[:, kt % 6, h * 64:(h + 1) * 64], PT[:, r * 128:(r + 1) * 128], r == r0, r == 4,
                                reads=["vb", ("pt", b)], writes=["P%d" % ob])
                    for r in range(r0, 5):
                        self.mm(pD[hs, 0:128], self.ones_bf[:, 0:64], PT[:, r * 128:(r + 1) * 128], r == r0, r == 4,
                                reads=["ones_bf", ("pt", b)], writes=["P%d" % (ob + 1)])
                rd = M["rden"][hp % 2]
                S.op("dve", lambda e, rd=rd, pD=pD: e.reciprocal(rd, pD[:, 0:128]), reads=["P%d" % (ob + 1)], writes=[("rden", hp % 2)])
                S.op("dve", lambda e, rd=rd, hp=hp, i=i, pO=pO: e.tensor_tensor(MIX[:, 3 + hp, i * 128:(i + 1) * 128], pO[:, 0:128], rd, ALU.mult),
                     reads=["P%d" % ob, ("rden", hp % 2)], writes=[("mix", 3 + hp)])

    def pool_block(self, l, tb, winT):
        S, M, PV, CST = self.S, self.M, self.PV, self.CST
        LV0, LV1, LV2 = M["LV"]
        POOLED, MIX = M["POOLED"], M["MIX"]
        W = TB + 16
        wsl, si = self.load_w_piece(winT, 2432, 256)
        for ch in range(2):
            pbk = (1, 7)[ch]
            pp = self.proj_fm(wsl, si, ch * 128, pbk)
            S.op("act", lambda e, pp=pp, ch=ch: e.copy(LV0[:, ch, 16:W], pp[:, 0:TB]), reads=["P%d" % pbk], writes=["lv0"])
        S.op("dve", lambda e: e.tensor_copy(LV0[:, :, 0:16], self.UHIST[:]), reads=["uhist"], writes=["lv0"])
        S.op("dve", lambda e: e.tensor_copy(self.UHIST[:], LV0[:, :, TB:W]), reads=["lv0"], writes=["uhist"])
        S.op("dve", lambda e: e.tensor_tensor(LV1[:, :, 1:W], LV0[:, :, 1:W], LV0[:, :, 0:W - 1], ALU.add), reads=["lv0"], writes=["lv1"])
        S.op("dve", lambda e: e.tensor_tensor(LV2[:, :, 3:W], LV1[:, :, 3:W], LV1[:, :, 1:W - 2], ALU.add), reads=["lv1"], writes=["lv2"])
        S.op("dve", lambda e: e.tensor_tensor(LV1[:, 1, 7:W], LV2[:, 1, 7:W], LV2[:, 1, 3:W - 4], ALU.add), reads=["lv2", "lv1"], writes=["lv1"])
        S.op("dve", lambda e: e.tensor_tensor(LV2[64:128, 1, 15:W], LV1[64:128, 1, 15:W], LV1[64:128, 1, 7:W - 8], ALU.add),
             reads=["lv1", "lv2"], writes=["lv2"])
        if tb == 0:
            pf = CST[:, C_PF:C_PF + 32].rearrange("p (a b) -> p a b", b=16)
            S.op("dve", lambda e: e.tensor_tensor(LV1[0:64, :, 16:32], LV1[0:64, :, 16:32], pf[0:64], ALU.mult), reads=["lv1", "cst"], writes=["lv1"])
            S.op("dve", lambda e: e.tensor_tensor(LV2[64:128, :, 16:32], LV2[64:128, :, 16:32], pf[64:128], ALU.mult), reads=["lv2", "cst"], writes=["lv2"])
        for ch in range(2):
            iw = CST[:, C_IW + ch:C_IW + ch + 1]
            S.op("dve", lambda e, ch=ch, iw=iw: e.scalar_tensor_tensor(
                POOLED[0:64, ch, :], LV1[0:64, ch, 16:W], iw[0:64], LV0[0:64, ch, 16:W], ALU.mult, ALU.subtract),
                reads=["lv1", "lv0", "cst"], writes=[("pooled", ch), "lv0"])
            S.op("dve", lambda e, ch=ch, iw=iw: e.scalar_tensor_tensor(
                POOLED[64:128, ch, :], LV2[64:128, ch, 16:W], iw[64:128], LV0[64:128, ch, 16:W], ALU.mult, ALU.subtract),
                reads=["lv2", "lv0", "cst"], writes=[("pooled", ch), "lv0"])
        for ch in range(2):
            pp = self.P[6 + ch]
            self.mm(pp[:, 0:TB], PV[:, PV_PW + ch * 128:PV_PW + (ch + 1) * 128], POOLED[:, ch, :], True, True,
                    reads=["pv", ("pooled", ch)], writes=["P%d" % (6 + ch)])
            S.op("dve", lambda e, ch=ch, pp=pp: e.tensor_scalar(MIX[:, 6 + ch, :], pp[:, 0:TB], PV[:, PV_PSC + ch:PV_PSC + ch + 1], None, ALU.mult),
                 reads=["P%d" % (6 + ch), "pv"], writes=[("mix", 6 + ch)])

    def store_out(self):
        S = self.S
        keys = []
        for kc in range(KC):
            for t8 in range(0, 8, 2):
                k = ("out", kc, t8)
                S.dma("sp", self.d_out[kc * 128:(kc + 1) * 128, t8 * 256:(t8 + 2) * 256],
                      self.XT[:, kc, t8 * 256:(t8 + 2) * 256], reads=self.xk(kc, t8 * 256, 512), writes=[k], semkey="out")
                keys.append(k)
        last = S.lastw[keys[-1]]
        for k in keys:
            S.lastw[k] = last
        S.wait_keys("sp", keys)


def build_program(cfg):
    nc = bass.Bass("TRN2", target_bir_lowering=False)
    with ExitStack() as st:
        mk = MK(nc, st, cfg)
        for (l, stage) in cfg["stages"]:
            if stage == "ffn1":
                mk.ffn(l, 0)
            elif stage == "ffn2":
                mk.ffn(l, 1)
            elif stage == "cross":
                mk.cross(l)
            elif stage.startswith("mix"):
                if len(stage) > 3:
                    mk.cfg["mixers"] = stage[3:]
                mk.mixer(l)
        mk.S.barrier()
        mk.store_out()
        mk.S.emit()
    return nc


def host_consts():
    c = np.zeros((128, C_N), np.float32)
    c[:, C_ID:C_ID + 128] = np.eye(128, dtype=np.float32)
    bo = np.zeros((128, 128), np.float32)
    bo[:64, :64] = 1.0
    bo[64:, 64:] = 1.0
    c[:, C_BO:C_BO + 128] = bo
    si = np.arange(64)[:, None]
    ti = np.arange(64)[None, :]
    mg = np.zeros((128, 128), np.float32)
    mg[:64, :64] = si < ti
    mg[:64, 64:] = si <= ti
    mg[64:, :64] = si < ti
    mg[64:, 64:] = si <= ti
    c[:, C_MG:C_MG + 128] = mg
    c[:, C_MG + 128:C_MG + 256] = mg
    ml = (si > ti).astype(np.float32)
    c[:64, C_ML:C_ML + 64] = ml
    c[64:, C_ML:C_ML + 64] = ml
    c[:, C_SM:C_SM + TB] = 1.0
    c[:, C_SM:C_SM + TB:64] = 0.0
    wins = {(0, 0): 2, (0, 1): 4, (1, 0): 8, (1, 1): 16}
    for (ch, half), win in wins.items():
        rows = slice(half * 64, half * 64 + 64)
        tt = np.arange(16)
        c[rows, C_PF + ch * 16:C_PF + (ch + 1) * 16] = (win / np.minimum(tt + 1, win)).astype(np.float32)[None, :]
        c[rows, C_IW + ch] = 1.0 / win
    return c


def host_layer_params(inp):
    pv = np.zeros((L, 128, PV_N), np.float32)

    def put(l, col, vec):
        n = vec.shape[0] // 128
        pv[l, :, col:col + n] = vec.reshape(n, 128).T
    for l in range(L):
        put(l, PV_FFN1, inp["norm_ffn1"][l])
        put(l, PV_MIX, inp["norm_mix"][l])
        put(l, PV_CROSS, inp["norm_cross"][l])
        put(l, PV_MEM, inp["norm_mem"][l])
        put(l, PV_FFN2, inp["norm_ffn2"][l])
        put(l, PV_XQ, inp["x_q_gain"][l])
        put(l, PV_XK, inp["x_k_gain"][l])
        put(l, PV_MU, inp["a_mu"][l])
        put(l, PV_W0, inp["a_w0"][l])
        put(l, PV_A0, inp["a_a0"][l])
        put(l, PV_KK, inp["a_k_k"][l])
        put(l, PV_KA, inp["a_k_a"][l])
        put(l, PV_RK, inp["a_r_k"][l].reshape(-1))
        put(l, PV_GNG, inp["a_gn_g"][l])
        put(l, PV_GNB, inp["a_gn_b"][l])
        if l >= 1:
            put(l, PV_V0, inp["a_v0"][l - 1])
            pv[l, 0:32, PV_VUP:PV_VUP + 384] = inp["a_v_up"][l - 1]
            pv[l, :, PV_VDOWN:PV_VDOWN + 96] = inp["a_v_down"][l - 1].reshape(3, 128, 32).transpose(1, 0, 2).reshape(128, 96)
        pv[l, :, PV_BQG] = np.tile(inp["b_q_gain"][l], 2)
        pv[l, :, PV_BKG] = np.tile(inp["b_k_gain"][l], 2)
        put(l, PV_PSC, inp["c_pool_scale"][l])
        pv[l, 0:32, PV_LORA:PV_LORA + 384] = inp["a_w_up"][l]
        pv[l, 32:64, PV_LORA:PV_LORA + 384] = inp["a_a_up"][l]
        pv[l, 64:128, PV_LORA:PV_LORA + 384] = inp["a_g_up"][l]
        for ch in range(2):
            for half in range(2):
                g = ch * 2 + half
                rows = slice(half * 64, half * 64 + 64)
                pv[l, rows, PV_PW + ch * 128 + half * 64:PV_PW + ch * 128 + half * 64 + 64] = inp["c_pool_w"][l, g]
    return pv


def host_bias(inp):
    j = np.arange(128)[:, None, None]
    r = np.arange(5)[None, :, None]
    i = np.arange(128)[None, None, :]
    dist = (4 - r) * 128 + i - j
    idx = np.clip(dist, -63, 256) + 63
    dchunk = (r - 4) * 2 + (j // 64) - (i // 64)
    valid = (dchunk >= -8) & (dchunk <= 0)
    out = np.zeros((L, 128, 6, 5, 128), np.float32)
    for l in range(L):
        for h in range(6):
            g = inp["b_rel_bias"][l, h][idx]
            out[l, :, h] = np.where(valid, g, np.float32(NEG))
    return out.reshape(L, 128, 6 * 640)


WEIGHT_KEYS = ("ffn1_wi", "ffn1_wo", "ffn2_wi", "ffn2_wo", "w_in", "w_out", "x_wq", "x_wkv", "x_wo")


def make_in_maps(inp, cores):
    shared = {"cst": host_consts(), "lyr": host_layer_params(inp), "biasT": host_bias(inp)}
    for k in WEIGHT_KEYS:
        shared[k] = np.ascontiguousarray(inp[k], dtype=np.float32)
    maps = []
    for b in cores:
        m = dict(shared)
        m["xT"] = np.ascontiguousarray(inp["x"][b].T)
        m["memT"] = np.ascontiguousarray(inp["mem"][b].T)
        maps.append(m)
    return maps


FULL_STAGES = [(l, s) for l in range(L) for s in ("ffn1", "mix", "cross", "ffn2")]
_CACHE = {}


def kernel(**inputs):
    inp = {k: np.asarray(v) for k, v in inputs.items()}
    if "nc" not in _CACHE:
        _CACHE["nc"] = build_program({"stages": FULL_STAGES})
    nc = _CACHE["nc"]
    maps = make_in_maps(inp, list(range(8)))
    res = run_bass_kernel_spmd(nc, maps, core_ids=list(range(8)))
    out = np.stack([np.ascontiguousarray(r["outT"].T) for r in res.results], axis=0)
    return out.astype(np.float32)
```

```python
import math
import numpy as np
import concourse.bass as bass
import concourse.mybir as mybir
from concourse.bass_utils import run_bass_kernel_spmd
from contextlib import ExitStack

F32 = mybir.dt.float32
BF16 = mybir.dt.bfloat16
AF = mybir.ActivationFunctionType
ALU = mybir.AluOpType

D = 1024
T = 2048
L = 2
NMEM = 256
DFF = 2816
KC = D // 128
JC = DFF // 128
INP = 2688
RMS_EPS = 1e-6
GN_EPS = 64e-5
SLOT_ELEMS = 3072
NSLOT = 4
SCR_UNITS = 27920
TB = 256
NCH = TB // 64
NQT = TB // 128
SDEC = math.exp(-0.5)
NEG = -30000.0

C_ID, C_BO, C_MG, C_ML, C_PF, C_IW, C_SM, C_N = 0, 128, 256, 512, 640, 672, 704, 960
PV_FFN1, PV_MIX, PV_CROSS, PV_MEM, PV_FFN2, PV_XQ, PV_XK = 0, 8, 16, 24, 32, 40, 42
PV_MU, PV_W0, PV_A0, PV_KK, PV_KA, PV_RK, PV_GNG, PV_GNB, PV_V0, PV_BQG, PV_BKG, PV_PSC = 44, 54, 57, 60, 63, 66, 69, 72, 75, 78, 79, 80
PV_LORA, PV_VUP, PV_VDOWN, PV_PW, PV_N = 96, 480, 864, 960, 1216


class Ref:
    tok = None


def _units(ops):
    units, cur, open_, live = [], [], False, set()
    for it in ops:
        kind, args, _ = it
        cur.append(it)
        if kind == "op":
            if args[0] == "pe":
                meta = args[6]
                if meta is not None:
                    if len(meta) > 3:
                        open_ = not meta[4]
                    if meta[2] != 0:
                        live.add("P%d" % meta[2])
            else:
                for k in args[2]:
                    live.discard(k)
        if not open_ and not live:
            units.append(cur)
            cur = []
    if cur:
        units.append(cur)
    return units


def split_at_mix_write(ops):
    head, tail, hit = [], [], False
    for u in _units(ops):
        if not hit and any(k[0] == "mix" for it in u for k in it[1][3] if isinstance(k, tuple)):
            hit = True
        (tail if hit else head).extend(u)
    return head, tail


def merge(a, b):
    return interleave(a, b) if len(a) >= len(b) else interleave(b, a)


def _is_bank(k):
    return isinstance(k, str) and len(k) == 2 and k[0] == "P" and k[1].isdigit()


def split_for_next_head(tail, head_ops, frac=0.35):
    wk = set(k for it in head_ops for k in it[1][3 if it[0] == "op" else 4] if not _is_bank(k))
    units = _units(tail)
    last = -1
    for ui, u in enumerate(units):
        for it in u:
            a = it[1]
            rd, wr = (a[2], a[3]) if it[0] == "op" else (a[3], a[4])
            if any((k in wk) for k in rd) or any((k in wk) for k in wr):
                last = ui
    cut = max(last + 1, int(len(units) * frac))
    t1 = [it for u in units[:cut] for it in u]
    t2 = [it for u in units[cut:] for it in u]
    return t1, t2


def interleave(a, b):
    ua, ub = _units(a), _units(b)
    if not ub:
        return list(a)
    out, step, nb = [], max(1, len(ua) // (len(ub) + 1)), 0
    for i, x in enumerate(ua):
        out.extend(x)
        if (i + 1) % step == 0 and nb < len(ub):
            out.extend(ub[nb])
            nb += 1
    for x in ub[nb:]:
        out.extend(x)
    return out


class Sched:
    ENG = ("pe", "act", "dve", "pool", "sp")

    def __init__(self, nc, stack, same_engine_sync=True):
        self.nc = nc
        self.stack = stack
        self.sem = {e: stack.enter_context(nc.semaphore("sem_" + e)) for e in self.ENG}
        self.count = {e: 0 for e in self.ENG}
        self.prog = {e: [] for e in self.ENG}
        self.waited = {e: {} for e in self.ENG}
        self.lastw = {}
        self.readers = {}
        self.dma_sems = {}
        self.same_engine_sync = same_engine_sync
        self.n_sem = 0
        self._rec = None
        self._rec_stack = []
        self._pe_recent = []

    def sbuf(self, name, shape, dtype=F32):
        return self.stack.enter_context(self.nc.sbuf_tensor(name, list(shape), dtype))

    def psum(self, name, shape, dtype=F32):
        return self.stack.enter_context(self.nc.psum_tensor(name, list(shape), dtype))

    def _deps(self, reads, writes):
        deps = []
        for k in reads:
            t = self.lastw.get(k)
            if t is not None:
                deps.append(t)
        for k in writes:
            t = self.lastw.get(k)
            if t is not None:
                deps.append(t)
            r = self.readers.get(k)
            if r:
                deps.extend(r.values())
        return deps

    def _emit_waits(self, eng, deps):
        need = {}
        for t in deps:
            sem, val, owner = t
            if owner == eng and (eng == "pe" or not self.same_engine_sync):
                continue
            if owner is not None and val > self.count[owner]:
                raise RuntimeError("wait on a not-yet-incrementing instruction (%s waits %s>=%d)" % (eng, owner, val))
            key = sem.name
            if key not in need or need[key][1] < val:
                need[key] = (sem, val)
        w = self.waited[eng]
        for key, (sem, val) in need.items():
            if w.get(key, 0) >= val:
                continue
            w[key] = val
            self.prog[eng].append(lambda e, sem=sem, val=val: e.wait_ge(sem, val))

    def _record(self, tok, reads, writes):
        key = tok[0].name
        for k in reads:
            r = self.readers.setdefault(k, {})
            if key not in r or r[key][1] < tok[1]:
                r[key] = tok
        for k in writes:
            self.lastw[k] = tok
            self.readers[k] = {}

    def rec_begin(self):
        self._rec_stack.append(self._rec)
        self._rec = []

    def rec_end(self):
        r, self._rec = self._rec, self._rec_stack.pop()
        return r

    def _pe_guard(self, meta):
        base, n, bank = meta[0], meta[1], meta[2]
        G = frozenset(range(base // 32, (base + n + 31) // 32))
        if len(G) == 4:
            self._pe_recent = []
            return [], None
        waits = [t for (g, b, t) in self._pe_recent if not (g & G) and b == bank]
        self._pe_recent = [(g, b, t) for (g, b, t) in self._pe_recent if not (g & G)]
        return waits, G

    def replay(self, item):
        kind, args, ref = item
        ref.tok = (self.op if kind == "op" else self.dma)(*args)

    def op(self, eng, fn, reads=(), writes=(), inc=True, after=None, pe_meta=None):
        if self._rec is not None:
            ref = Ref()
            self._rec.append(("op", (eng, fn, tuple(reads), tuple(writes), inc, after, pe_meta), ref))
            return ref
        while isinstance(after, Ref):
            after = after.tok
        forced = [after] if after is not None else []
        G = None
        if pe_meta is not None:
            w, G = self._pe_guard(pe_meta)
            forced += w
            if G is not None:
                inc = True
        self._emit_waits(eng, self._deps(reads, writes))
        for (sem_a, val_a, _) in forced:
            if self.waited[eng].get(sem_a.name, 0) < val_a:
                self.waited[eng][sem_a.name] = val_a
                self.prog[eng].append(lambda e, sem_a=sem_a, val_a=val_a: e.wait_ge(sem_a, val_a))
        sem = self.sem[eng]
        if inc:
            self.count[eng] += 1
            val = self.count[eng]
            self.prog[eng].append(lambda e, fn=fn, sem=sem: fn(e).then_inc(sem, 1))
        else:
            val = self.count[eng] + 1
            self.prog[eng].append(lambda e, fn=fn: fn(e))
        tok = (sem, val, eng)
        if G is not None:
            self._pe_recent.append((G, pe_meta[2], tok))
        self._record(tok, reads, writes)
        return tok

    def dma(self, q, out, in_, reads=(), writes=(), semkey=None):
        if self._rec is not None:
            ref = Ref()
            self._rec.append(("dma", (q, out, in_, tuple(reads), tuple(writes), semkey), ref))
            return ref
        self._emit_waits(q, self._deps(reads, writes))
        if semkey is None:
            semkey = writes[0]
        if semkey not in self.dma_sems:
            self.n_sem += 1
            s = self.stack.enter_context(self.nc.semaphore("dsem%d" % self.n_sem))
            self.dma_sems[semkey] = [s, 0]
        ent = self.dma_sems[semkey]
        ent[1] += 16
        sem, val = ent[0], ent[1]
        self.prog[q].append(lambda e, out=out, in_=in_, sem=sem: e.dma_start(out=out, in_=in_).then_inc(sem, 16))
        tok = (sem, val, None)
        self._record(tok, reads, writes)
        return tok

    def wait_keys(self, eng, keys):
        self._emit_waits(eng, self._deps(keys, ()))

    def barrier(self):
        toks = [(self.sem[e], self.count[e], e) for e in self.ENG if self.count[e] > 0]
        toks += [(s, v, None) for (s, v) in self.dma_sems.values()]
        for e in self.ENG:
            if e == "pool":
                continue
            self._emit_waits(e, [t for t in toks if t[2] != e])

    def emit(self):
        with self.nc.Block() as block:
            @block.sync
            def _(e):
                for f in self.prog["sp"]:
                    f(e)

            @block.gpsimd
            def _(e):
                for f in self.prog["pool"]:
                    f(e)

            @block.scalar
            def _(e):
                for f in self.prog["act"]:
                    f(e)

            @block.vector
            def _(e):
                for f in self.prog["dve"]:
                    f(e)

            @block.tensor
            def _(e):
                for f in self.prog["pe"]:
                    f(e)


class MK:
    def __init__(self, nc, st, cfg):
        self.nc = nc
        self.cfg = cfg
        S = self.S = Sched(nc, st, same_engine_sync=cfg.get("same_engine_sync", True))
        dt = nc.dram_tensor

        def din(name, shape):
            return dt(name, list(shape), F32, kind="ExternalInput").ap()

        self.d_xT = din("xT", [D, T])
        self.d_memT = din("memT", [D, NMEM])
        self.d_cst = din("cst", [128, C_N])
        self.d_lyr = din("lyr", [L, 128, PV_N])
        self.d_bias = din("biasT", [L, 128, 6 * 640])
        self.d_ffn_wi = [din("ffn1_wi", [L, D, 2 * DFF]), din("ffn2_wi", [L, D, 2 * DFF])]
        self.d_ffn_wo = [din("ffn1_wo", [L, DFF, D]), din("ffn2_wo", [L, DFF, D])]
        self.d_win = din("w_in", [L, D, INP])
        self.d_wout = din("w_out", [L, D, D])
        self.d_xwq = din("x_wq", [L, D, D])
        self.d_xwkv = din("x_wkv", [L, D, 2 * D])
        self.d_xwo = din("x_wo", [L, D, D])
        self.d_out = dt("outT", [D, T], F32, kind="ExternalOutput").ap()
        self.d_vf = dt("vfirst", [3, 128, T], F32).ap()

        self.XT = S.sbuf("XT", [128, KC, T], F32)
        self.WS = S.sbuf("WS", [128, NSLOT, SLOT_ELEMS], BF16)
        self.slot_i = 0
        self.SCR = S.sbuf("SCR", [128, SCR_UNITS], F32)
        self.CST = S.sbuf("CST", [128, C_N], F32)
        self.PV = S.sbuf("LYR", [128, PV_N], F32)
        self.ones_bf = S.sbuf("ones_bf", [128, 128], BF16)
        self.bo_bf = S.sbuf("bo_bf", [128, 128], BF16)
        self.HS = S.sbuf("HS", [128, 3, 64], F32)
        self.CARRY = S.sbuf("CARRY", [128, 10], F32)
        self.UHIST = S.sbuf("UHIST", [128, 2, 16], F32)
        self.P = [S.psum("P%d" % i, [128, 512], F32) for i in range(8)]
        self.cur_layer = -1

        S.dma("sp", self.CST[:], self.d_cst, writes=["cst"])
        S.op("dve", lambda e: e.memset(self.ones_bf[:], 1.0), writes=["ones_bf"])
        S.op("dve", lambda e: e.tensor_copy(self.bo_bf[:], self.CST[:, C_BO:C_BO + 128]), reads=["cst"], writes=["bo_bf"])
        for kc in range(KC):
            for t8 in range(0, 8, 2):
                S.dma("sp", self.XT[:, kc, t8 * 256:(t8 + 2) * 256],
                      self.d_xT[kc * 128:(kc + 1) * 128, t8 * 256:(t8 + 2) * 256],
                      writes=[("x", kc, t8), ("x", kc, t8 + 1)], semkey=("xin", t8))
        for t8 in range(0, 8, 2):
            last = S.lastw[("x", KC - 1, t8)]
            for kc in range(KC):
                S.lastw[("x", kc, t8)] = last
                S.lastw[("x", kc, t8 + 1)] = last

    def xk(self, kc, t0, n):
        return [("x", kc, t) for t in range(t0 // 256, (t0 + n + 255) // 256)]

    def hk(self, kc, off, n):
        return [("hn", kc, t) for t in range(off // 256, (off + n + 255) // 256)]

    def slot(self):
        g = getattr(self, "slot_group", None)
        if g is not None:
            self.slot_gi = getattr(self, "slot_gi", {})
            k = self.slot_gi.get(g, 0)
            self.slot_gi[g] = k + 1
            return g[k % len(g)]
        i = self.slot_i
        self.slot_i = (self.slot_i + 1) % NSLOT
        return i

    def carve_reset(self):
        self.S.barrier()
        self.co = 0

    def carve(self, n, dtype=F32, inner=None):
        ap = self.SCR[:, self.co:self.co + n]
        self.co += n
        assert self.co <= SCR_UNITS, self.co
        if dtype == BF16:
            ap = ap.bitcast(BF16)
        if inner:
            ap = ap.rearrange("p (a b) -> p a b", b=inner)
        return ap

    def load_layer(self, l):
        if self.cur_layer != l:
            self.S.dma("sp", self.PV[:], self.d_lyr[l], writes=["pv"])
            self.cur_layer = l

    def mm(self, out, lhsT, rhs, start, stop, reads, writes, inc=None, after=None):
        bank = int(writes[0][1:])
        meta = (lhsT.base_partition(), lhsT.shape[0], bank, start, stop)
        return self.S.op("pe", lambda e: e.matmul(out, lhsT, rhs, start=start, stop=stop),
                         reads=reads, writes=writes, inc=(stop if inc is None else inc), after=after, pe_meta=meta)

    def load_w_piece(self, wsrc, c0, ncols, nk=KC):
        si = self.slot()
        wsl = self.WS[:, si, 0:nk * ncols].rearrange("p (kc c) -> p kc c", c=ncols)
        self.S.dma("pool", wsl, wsrc[:, :, c0:c0 + ncols], writes=[("ws", si)], semkey=("ws", si))
        return wsl, si

    def rmsnorm(self, pvcol, t0, n, HN, hoff, scr):
        S = self.S
        ps = self.P[6]
        for kc in range(KC):
            b = kc % 2
            sq = scr["sq"][b][:, 0:n]
            S.op("act", lambda e, sq=sq, kc=kc: e.activation(sq, self.XT[:, kc, t0:t0 + n], AF.Square),
                 reads=self.xk(kc, t0, n), writes=[("sq", b)])
            self.mm(ps[:, 0:n], self.ones_bf[:], sq, kc == 0, kc == KC - 1,
                    reads=[("sq", b), "ones_bf"], writes=["P6"], inc=True)
        rt = scr["rt"][:, 0:n]
        S.op("act", lambda e: e.activation(rt, ps[:, 0:n], AF.Ln, bias=RMS_EPS, scale=1.0 / D),
             reads=["P6"], writes=["rt"])
        S.op("act", lambda e: e.activation(rt, rt, AF.Exp, scale=-0.5), reads=["rt"], writes=["rt"])
        for kc in range(KC):
            S.op("dve", lambda e, kc=kc: e.scalar_tensor_tensor(
                HN[:, kc, hoff:hoff + n], self.XT[:, kc, t0:t0 + n],
                self.PV[:, pvcol + kc:pvcol + kc + 1], rt, ALU.mult, ALU.mult),
                reads=self.xk(kc, t0, n) + ["pv", "rt"], writes=self.hk(kc, hoff, n))

    def ffn(self, l, which):
        S = self.S
        self.carve_reset()
        self.load_layer(l)
        wi = self.d_ffn_wi[which][l].rearrange("(kc p) c -> p kc c", p=128)
        wo = self.d_ffn_wo[which][l].rearrange("(kc p) c -> p kc c", p=128)
        pvcol = PV_FFN1 if which == 0 else PV_FFN2
        H = self.carve(11264, BF16, 1024)
        HN = self.carve(4096, BF16, 1024)
        sqb = self.carve(512, BF16)
        scr = {"sq": [sqb[:, 0:512], sqb[:, 512:1024]], "rt": self.carve(512)}
        sg = [self.carve(512), self.carve(512)]
        for hb in range(2):
            for t2 in range(2):
                self.rmsnorm(pvcol, hb * 1024 + t2 * 512, 512, HN, t2 * 512, scr)
            for j in range(JC):
                si = self.slot()
                wsl = self.WS[:, si, 0:2048].rearrange("p (g kc c) -> p g kc c", g=2, kc=KC)
                S.dma("pool", wsl[:, 0], wi[:, :, j * 128:(j + 1) * 128], writes=[("ws", si)], semkey=("ws", si))
                S.dma("pool", wsl[:, 1], wi[:, :, DFF + j * 128:DFF + (j + 1) * 128], writes=[("ws", si)], semkey=("ws", si))
                for t2 in range(2):
                    b = (j * 2 + t2) % 2
                    pg, pu = self.P[b], self.P[2 + b]
                    for kc in range(KC):
                        self.mm(pg[:], wsl[:, 0, kc, :], HN[:, kc, t2 * 512:(t2 + 1) * 512], kc == 0, kc == KC - 1,
                                reads=[("ws", si)] + self.hk(kc, t2 * 512, 512), writes=["P%d" % b])
                    for kc in range(KC):
                        self.mm(pu[:], wsl[:, 1, kc, :], HN[:, kc, t2 * 512:(t2 + 1) * 512], kc == 0, kc == KC - 1,
                                reads=[("ws", si)] + self.hk(kc, t2 * 512, 512), writes=["P%d" % (2 + b)])
                    S.op("act", lambda e, b=b, pg=pg: e.activation(sg[b], pg[:], AF.Silu),
                         reads=["P%d" % b], writes=[("sg", b)])
                    S.op("dve", lambda e, b=b, pu=pu, j=j, t2=t2: e.tensor_tensor(
                        H[:, j, t2 * 512:(t2 + 1) * 512], sg[b], pu[:], ALU.mult),
                        reads=[("sg", b), "P%d" % (2 + b)], writes=[("H", j, t2)])
            for m in range(KC):
                wsl, si = self.load_w_piece(wo, m * 128, 128, nk=JC)
                for t2 in range(2):
                    b = (m * 2 + t2) % 2
                    po = self.P[4 + b]
                    t0 = hb * 1024 + t2 * 512
                    for j in range(JC):
                        self.mm(po[:], wsl[:, j, :], H[:, j, t2 * 512:(t2 + 1) * 512], j == 0, j == JC - 1,
                                reads=[("ws", si), ("H", j, t2)], writes=["P%d" % (4 + b)])
                    S.op("dve", lambda e, po=po, m=m, t0=t0: e.scalar_tensor_tensor(
                        self.XT[:, m, t0:t0 + 512], po[:], 0.5, self.XT[:, m, t0:t0 + 512], ALU.mult, ALU.add),
                        reads=["P%d" % (4 + b)] + self.xk(m, t0, 512), writes=self.xk(m, t0, 512))

    def headnorm(self, src, dst, chunks, n, gcols, scr, srckeys, dstkeys):
        S = self.S
        ps = self.P[6]
        nc_ = len(chunks)
        for i, c in enumerate(chunks):
            b = i % 2
            sq = scr["sq"][b][:, 0:n]
            S.op("act", lambda e, sq=sq, c=c: e.activation(sq, src[:, c, 0:n], AF.Square),
                 reads=[srckeys[i]], writes=[("sq", b)])
            self.mm(ps[:, 0:n], self.ones_bf[:], sq, i == 0, i == nc_ - 1,
                    reads=[("sq", b), "ones_bf"], writes=["P6"], inc=True)
        rt = scr["rt"][:, 0:n]
        S.op("act", lambda e: e.activation(rt, ps[:, 0:n], AF.Ln, bias=RMS_EPS, scale=1.0 / (128 * nc_)),
             reads=["P6"], writes=["rt"])
        S.op("act", lambda e: e.activation(rt, rt, AF.Exp, scale=-0.5), reads=["rt"], writes=["rt"])
        for i, c in enumerate(chunks):
            S.op("dve", lambda e, i=i, c=c: e.scalar_tensor_tensor(
                dst[:, c, 0:n], src[:, c, 0:n], self.PV[:, gcols + i:gcols + i + 1], rt, ALU.mult, ALU.mult),
                reads=[srckeys[i], "pv", "rt"], writes=[dstkeys[i]])

    def cross(self, l):
        S = self.S
        self.carve_reset()
        self.load_layer(l)
        wq = self.d_xwq[l].rearrange("(kc p) c -> p kc c", p=128)
        wkv = self.d_xwkv[l].rearrange("(kc p) c -> p kc c", p=128)
        wo = self.d_xwo[l].rearrange("(kc p) c -> p kc c", p=128)
        KT = self.carve(1024, BF16, 256)
        V = self.carve(1024, BF16, 1024)
        memn = self.carve(1024, BF16, 256)
        save = self.co
        Kf = self.carve(2048, F32, 256)
        MEMT = self.carve(2048, F32, 256)
        self.co = save
        qf = self.carve(4096, F32, 512)
        qn = self.carve(2048, BF16, 512)
        E = self.carve(1024, BF16, 512)
        ob = self.carve(2048, BF16, 512)
        HN = self.carve(2048, BF16, 512)
        sqb = self.carve(512, BF16)
        scr = {"sq": [sqb[:, 0:512], sqb[:, 512:1024]], "rt": self.carve(512)}
        rden = [self.carve(512), self.carve(512)]
        S.dma("sp", MEMT, self.d_memT.rearrange("(kc p) m -> p kc m", p=128), writes=["memT"])
        ps = self.P[6]
        for kc in range(KC):
            b = kc % 2
            sq = scr["sq"][b][:, 0:NMEM]
            S.op("act", lambda e, sq=sq, kc=kc: e.activation(sq, MEMT[:, kc, :], AF.Square),
                 reads=["memT"], writes=[("sq", b)])
            self.mm(ps[:, 0:NMEM], self.ones_bf[:], sq, kc == 0, kc == KC - 1,
                    reads=[("sq", b), "ones_bf"], writes=["P6"], inc=True)
        rt = scr["rt"][:, 0:NMEM]
        S.op("act", lambda e: e.activation(rt, ps[:, 0:NMEM], AF.Ln, bias=RMS_EPS, scale=1.0 / D),
             reads=["P6"], writes=["rt"])
        S.op("act", lambda e: e.activation(rt, rt, AF.Exp, scale=-0.5), reads=["rt"], writes=["rt"])
        for kc in range(KC):
            S.op("dve", lambda e, kc=kc: e.scalar_tensor_tensor(
                memn[:, kc, :], MEMT[:, kc, :], self.PV[:, PV_MEM + kc:PV_MEM + kc + 1], rt, ALU.mult, ALU.mult),
                reads=["memT", "pv", "rt"], writes=[("memn", kc)])
        for pc in range(4):
            wsl, si = self.load_w_piece(wkv, pc * 256, 256)
            for mi in range(2):
                m = pc * 2 + mi
                pp = self.P[m % 2]
                for kc in range(KC):
                    self.mm(pp[:, 0:NMEM], wsl[:, kc, mi * 128:(mi + 1) * 128], memn[:, kc, :], kc == 0, kc == KC - 1,
                            reads=[("ws", si), ("memn", kc)], writes=["P%d" % (m % 2)])
                S.op("act", lambda e, m=m, pp=pp: e.copy(Kf[:, m, :], pp[:, 0:NMEM]), reads=["P%d" % (m % 2)], writes=[("Kf", m)])
        for h in range(4):
            self.headnorm(Kf, KT, [2 * h, 2 * h + 1], NMEM, PV_XK, scr,
                          [("Kf", 2 * h), ("Kf", 2 * h + 1)], [("KT", 2 * h), ("KT", 2 * h + 1)])
        for pc in range(4):
            wsl, si = self.load_w_piece(wkv, D + pc * 256, 256)
            for mt in range(2):
                pp = self.P[(pc * 2 + mt) % 2]
                for kc in range(KC):
                    self.mm(pp[:, 0:256], memn[:, kc, mt * 128:(mt + 1) * 128], wsl[:, kc, :], kc == 0, kc == KC - 1,
                            reads=[("ws", si), ("memn", kc)], writes=["P%d" % ((pc * 2 + mt) % 2)])
                S.op("act", lambda e, pp=pp, mt=mt, pc=pc: e.copy(V[:, mt, pc * 256:(pc + 1) * 256], pp[:, 0:256]),
                     reads=["P%d" % ((pc * 2 + mt) % 2)], writes=[("V", mt, pc)])
        qf_guard = [("Kf", m) for m in range(8)] + ["memT"]
        for tb in range(4):
            t0 = tb * 512
            self.rmsnorm(PV_CROSS, t0, 512, HN, 0, scr)
            for pc in range(4):
                wsl, si = self.load_w_piece(wq, pc * 256, 256)
                for mi in range(2):
                    m = pc * 2 + mi
                    pp = self.P[m % 2]
                    for kc in range(KC):
                        self.mm(pp[:], wsl[:, kc, mi * 128:(mi + 1) * 128], HN[:, kc, 0:512], kc == 0, kc == KC - 1,
                                reads=[("ws", si)] + self.hk(kc, 0, 512), writes=["P%d" % (m % 2)])
                    S.op("act", lambda e, m=m, pp=pp: e.copy(qf[:, m, :], pp[:]), reads=["P%d" % (m % 2)],
                         writes=[("qf", m)] + (qf_guard if tb == 0 else []))
            for h in range(4):
                self.headnorm(qf, qn, [2 * h, 2 * h + 1], 512, PV_XQ, scr,
                              [("qf", 2 * h), ("qf", 2 * h + 1)], [("qn", 2 * h), ("qn", 2 * h + 1)])
            for h in range(4):
                eb = h % 2
                for mt in range(2):
                    pp = self.P[2 + mt]
                    for c in range(2):
                        self.mm(pp[:], KT[:, 2 * h + c, mt * 128:(mt + 1) * 128], qn[:, 2 * h + c, :], c == 0, c == 1,
                                reads=[("KT", 2 * h + c), ("qn", 2 * h + c)], writes=["P%d" % (2 + mt)])
                    S.op("act", lambda e, pp=pp, eb=eb, mt=mt: e.activation(E[:, eb * 2 + mt, :], pp[:], AF.Exp, scale=1.0 / 16.0),
                         reads=["P%d" % (2 + mt)], writes=[("E", eb, mt)])
                for c in range(2):
                    pp = self.P[4 + c]
                    for mt in range(2):
                        self.mm(pp[:], V[:, mt, h * 256 + c * 128:h * 256 + (c + 1) * 128], E[:, eb * 2 + mt, :], mt == 0, mt == 1,
                                reads=[("V", mt, h), ("E", eb, mt)], writes=["P%d" % (4 + c)])
                pd = self.P[7]
                for mt in range(2):
                    self.mm(pd[:], self.ones_bf[:], E[:, eb * 2 + mt, :], mt == 0, mt == 1,
                            reads=["ones_bf", ("E", eb, mt)], writes=["P7"])
                S.op("act", lambda e, eb=eb: e.activation(rden[eb], pd[:], AF.Ln), reads=["P7"], writes=[("rden", eb)])
                S.op("act", lambda e, eb=eb: e.activation(rden[eb], rden[eb], AF.Exp, scale=-1.0), reads=[("rden", eb)], writes=[("rden", eb)])
                for c in range(2):
                    pp = self.P[4 + c]
                    S.op("dve", lambda e, pp=pp, c=c, h=h, eb=eb: e.tensor_tensor(ob[:, 2 * h + c, :], pp[:], rden[eb], ALU.mult),
                         reads=["P%d" % (4 + c), ("rden", eb)], writes=[("ob", 2 * h + c)])
            for pc in range(4):
                wsl, si = self.load_w_piece(wo, pc * 256, 256)
                for mi in range(2):
                    m = pc * 2 + mi
                    pp = self.P[m % 2]
                    for kc in range(KC):
                        self.mm(pp[:], wsl[:, kc, mi * 128:(mi + 1) * 128], ob[:, kc, :], kc == 0, kc == KC - 1,
                                reads=[("ws", si), ("ob", kc)], writes=["P%d" % (m % 2)])
                    S.op("dve", lambda e, m=m, pp=pp, t0=t0: e.tensor_tensor(
                        self.XT[:, m, t0:t0 + 512], pp[:], self.XT[:, m, t0:t0 + 512], ALU.add),
                        reads=["P%d" % (m % 2)] + self.xk(m, t0, 512), writes=self.xk(m, t0, 512))

    def mixer(self, l):
        S = self.S
        mixers = self.cfg.get("mixers", "abc")
        self.carve_reset()
        self.load_layer(l)
        c = self.carve
        M = self.M = {}
        M["HN"] = c(1024, BF16, TB)
        M["KB"] = c(1152, BF16, 768)
        M["VB"] = c(1152, BF16, 384)
        M["BIAS"] = c(1920, BF16, 640)
        M["MIX"] = c(1024, BF16, TB)
        sqb = c(256, BF16)
        M["scr"] = {"sq": [sqb[:, 0:TB], sqb[:, TB:2 * TB]], "rt": c(256)}
        u0 = self.co
        raw = c(260)
        M["RAW"] = [raw, raw]
        M["TMP"] = c(TB)
        M["TMP2"] = c(TB)
        M["XS"] = c(9 * TB, F32, TB)
        for nm in ("TG", "CUM", "A", "KK", "B", "EX1", "EX2", "YS", "YC", "VD", "VF", "PT1", "PT2"):
            M[nm] = c(TB)
        for nm in ("G", "BON"):
            M[nm] = [c(TB), c(TB)]
        for nm in ("AR", "BK", "BKH", "VV"):
            M[nm] = [c(NCH * 128).rearrange("p (c s t) -> p c s t", s=2, t=64) for _ in range(2)]
        M["GT"] = [c(NCH * 256).rearrange("p (h c x) -> p h c x", h=2, x=128) for _ in range(2)]
        M["X"] = [c(NCH * 64, BF16, 64), c(NCH * 64, BF16, 64)]
        M["XTb"] = [c(NCH * 64, BF16, 64), c(NCH * 64, BF16, 64)]
        M["XT0b"] = c(NCH * 64, BF16, 64)
        M["TTb"] = c(NCH * 64, BF16, 64)
        M["TT"] = [c(NCH * 128, F32, 64), c(NCH * 128, F32, 64)]
        M["BKHT"] = c(NCH * 128, F32, 64)
        M["UV"] = c(NCH * 128, F32, 64)
        M["Z"] = c(128)
        M["WC"] = [c(16), c(16)]
        M["QF"] = c(TB)
        M["QN"] = c(3 * TB // 2, BF16, TB)
        sbt = c(640)
        M["SBt"] = [sbt, sbt]
        M["PT"] = [c(320, BF16), c(320, BF16)]
        M["rden"] = [c(128), c(128)]
        W = TB + 16
        M["LV"] = [c(2 * W, F32, W) for _ in range(3)]
        M["POOLED"] = M["LV"][0][:, :, 16:W]
        btmp = self.SCR[:, u0:u0 + 3840].rearrange("p (h x) -> p h x", x=640)
        S.dma("sp", btmp, self.d_bias[l].rearrange("p (h x) -> p h x", x=640), writes=["btmp"])
        S.op("act", lambda e: e.copy(M["BIAS"], btmp), reads=["btmp"], writes=["bias"])
        S.barrier()
        S.op("dve", lambda e: e.memset(self.HS[:], 0.0), reads=[], writes=[("HS", 0), ("HS", 1), ("HS", 2)])
        S.op("dve", lambda e: e.memset(self.CARRY[:], 0.0), writes=["carry"])
        S.op("dve", lambda e: e.memset(self.UHIST[:], 0.0), writes=["uhist"])
        S.op("dve", lambda e: e.memset(M["MIX"][:], 0.0), writes=[("mix", k) for k in range(8)])
        winT = self.d_win[l].rearrange("(kc p) c -> p kc c", p=128)
        woutT = self.d_wout[l].rearrange("(kc p) c -> p kc c", p=128)
        ro_prev = []
        nb = self.cfg.get("nblocks", T // TB)

        def rec_head(tb):
            self.slot_group = (0, 1)
            S.rec_begin()
            self.rmsnorm(PV_MIX, tb * TB, TB, M["HN"], 0, M["scr"])
            if "a" in mixers:
                self.rwkv_block(l, tb, winT)
            r = S.rec_end()
            self.slot_group = None
            return r
        for it in rec_head(0):
            S.replay(it)
        for tb in range(nb):
            t0 = tb * TB
            ra = []
            if "a" in mixers:
                S.rec_begin()
                self.rwkv_main(l, tb)
                ra = S.rec_end()
            self.slot_group = (2, 3)
            S.rec_begin()
            if "c" in mixers:
                self.pool_block(l, tb, winT)
            if "b" in mixers:
                self.attn_block(l, tb, winT)
            rb = S.rec_end()
            self.slot_group = None
            nxt_head = rec_head(tb + 1) if tb + 1 < nb else []
            ra_h, ra_t = split_at_mix_write(ra)
            rb_h, rb_t = split_at_mix_write(rb)
            tail = merge(ra_t, rb_t)
            t1, t2 = split_for_next_head(tail, nxt_head) if nxt_head else (tail, [])
            merged = merge(ra_h, ro_prev + rb_h) + t1 + merge(t2, nxt_head)
            for it in merged:
                S.replay(it)
            self.slot_group = (2, 3)
            S.rec_begin()
            for pc in range(4):
                wsl, si = self.load_w_piece(woutT, pc * 256, 256)
                for mi in range(2):
                    m = pc * 2 + mi
                    bk = 6 + m % 2
                    pp = self.P[bk]
                    for kc in range(KC):
                        self.mm(pp[:, 0:TB], wsl[:, kc, mi * 128:(mi + 1) * 128], M["MIX"][:, kc, :], kc == 0, kc == KC - 1,
                                reads=[("ws", si), ("mix", kc)], writes=["P%d" % bk])
                    S.op("dve", lambda e, m=m, pp=pp, t0=t0: e.tensor_tensor(
                        self.XT[:, m, t0:t0 + TB], pp[:, 0:TB], self.XT[:, m, t0:t0 + TB], ALU.add),
                        reads=["P%d" % bk] + self.xk(m, t0, TB), writes=self.xk(m, t0, TB))
            ro_prev = S.rec_end()
            self.slot_group = None
        for it in ro_prev:
            S.replay(it)

    def proj_fm(self, wsl, si, coff, bank, n=TB):
        pp = self.P[bank]
        for kc in range(KC):
            self.mm(pp[:, 0:n], wsl[:, kc, coff:coff + 128], self.M["HN"][:, kc, 0:n], kc == 0, kc == KC - 1,
                    reads=[("ws", si)] + self.hk(kc, 0, n), writes=["P%d" % bank])
        return pp

    def rwkv_block(self, l, tb, winT):
        S, M, PV = self.S, self.M, self.PV
        t0 = tb * TB
        XS, TMP = M["XS"], M["TMP"]
        pieces = [(0, 384), (384, 384), (768, 384), (1152, 128)]
        cur = None
        for q in range(10):
            pi, coff = (q // 3, (q % 3) * 128) if q < 9 else (3, 0)
            if cur is None or cur[0] != pi:
                wsl, si = self.load_w_piece(winT, pieces[pi][0], pieces[pi][1])
                cur = (pi, wsl, si)
            bank = (1, 6)[q % 2]
            pp = self.proj_fm(cur[1], cur[2], coff, bank)
            RAW = M["RAW"][q % 2]
            rk = ("raw", 0)
            S.op("dve", lambda e, RAW=RAW, q=q: e.tensor_copy(RAW[:, 0:1], self.CARRY[:, q:q + 1]), reads=["carry"], writes=[rk])
            S.op("act", lambda e, RAW=RAW, pp=pp: e.copy(RAW[:, 1:TB + 1], pp[:, 0:TB]), reads=["P%d" % bank], writes=[rk])
            S.op("dve", lambda e, RAW=RAW, q=q: e.tensor_copy(self.CARRY[:, q:q + 1], RAW[:, TB:TB + 1]), reads=[rk], writes=["carry"])
            S.op("dve", lambda e, RAW=RAW: e.tensor_tensor(TMP, RAW[:, 0:TB], RAW[:, 1:TB + 1], ALU.subtract), reads=[rk], writes=["TMP"])
            dst = XS[:, q, :] if q < 9 else M["TG"]
            S.op("dve", lambda e, RAW=RAW, q=q, dst=dst: e.scalar_tensor_tensor(
                dst, TMP, PV[:, PV_MU + q:PV_MU + q + 1], RAW[:, 1:TB + 1], ALU.mult, ALU.add),
                reads=["TMP", rk, "pv"], writes=[("xs", q) if q < 9 else "TG"])
        TG = M["TG"]
        S.op("act", lambda e: e.activation(TG[0:32, :], TG[0:32, :], AF.Tanh), reads=["TG"], writes=["TG"])
        S.op("act", lambda e: e.activation(TG[64:128, :], TG[64:128, :], AF.Sigmoid), reads=["TG"], writes=["TG"])
        if l == 0:
            for hp in range(3):
                S.dma("sp", self.d_vf[hp][:, t0:t0 + TB], XS[:, 6 + hp, :], reads=[("xs", 6 + hp)], writes=[("vf", hp, tb)], semkey=("vfst", hp))
        else:
            pvd = self.P[5]
            for hp in range(3):
                self.mm(pvd[0:32, 0:TB], PV[:, PV_VDOWN + hp * 32:PV_VDOWN + (hp + 1) * 32], XS[:, 6 + hp, :], hp == 0, hp == 2,
                        reads=["pv", ("xs", 6 + hp)], writes=["P5"])
            VD, VF = M["VD"], M["VF"]
            S.op("act", lambda e: e.copy(VD[0:32, :], pvd[0:32, 0:TB]), reads=["P5"], writes=["VD"])
            for hp in range(3):
                self.mm(pvd[:, 0:TB], PV[0:32, PV_VUP + hp * 128:PV_VUP + (hp + 1) * 128], VD[0:32, :], True, True,
                        reads=["pv", "VD"], writes=["P5"])
                S.op("act", lambda e, hp=hp: e.activation(TMP, pvd[:, 0:TB], AF.Sigmoid, bias=PV[:, PV_V0 + hp:PV_V0 + hp + 1]),
                     reads=["P5", "pv"], writes=["TMP"])
                S.dma("sp", VF, self.d_vf[hp][:, t0:t0 + TB], reads=[("vf", hp, tb)], writes=["VF"])
                S.op("dve", lambda e, hp=hp: e.tensor_tensor(VF, VF, XS[:, 6 + hp, :], ALU.subtract), reads=["VF", ("xs", 6 + hp)], writes=["VF"])
                S.op("dve", lambda e: e.tensor_tensor(VF, VF, TMP, ALU.mult), reads=["VF", "TMP"], writes=["VF"])
                S.op("dve", lambda e, hp=hp: e.tensor_tensor(XS[:, 6 + hp, :], XS[:, 6 + hp, :], VF, ALU.add),
                     reads=["VF", ("xs", 6 + hp)], writes=[("xs", 6 + hp)])
        self.rwkv_prep(l, tb, 0)
        self.rwkv_P(l, tb, 0)

    def rwkv_main(self, l, tb):
        S = self.S

        def rec(fn, *a):
            S.rec_begin()
            fn(*a)
            return S.rec_end()
        for hp in range(3):
            main = rec(self.rwkv_TS, l, tb, hp) + rec(self.rwkv_post, l, tb, hp)
            nxt = (rec(self.rwkv_prep, l, tb, hp + 1) + rec(self.rwkv_P, l, tb, hp + 1)) if hp < 2 else []
            for it in merge(main, nxt):
                S.replay(it)

    def rwkv_prep(self, l, tb, hp):
        S, M, PV, CST = self.S, self.M, self.PV, self.CST
        pb = (3 * tb + hp) % 2
        XS, TMP, TMP2, TG = M["XS"], M["TMP"], M["TMP2"], M["TG"]
        CUM, A_, KK, B_, EX1, EX2 = (M[k] for k in ("CUM", "A", "KK", "B", "EX1", "EX2"))
        G, BON, AR, BK, BKH, VV, WC = (M[k][pb] for k in ("G", "BON", "AR", "BK", "BKH", "VV", "WC"))
        kG, kBON, kAR, kBK, kBKH, kVV, kWC = (("G", pb), ("BON", pb), ("AR", pb), ("BK", pb), ("BKH", pb), ("VV", pb), ("WC", pb))
        bo = CST[:, C_BO:C_BO + 128]
        r, k, v = XS[:, hp, :], XS[:, 3 + hp, :], XS[:, 6 + hp, :]
        rk_, kk_, vk_ = ("xs", hp), ("xs", 3 + hp), ("xs", 6 + hp)
        lo = PV_LORA + hp * 128
        P6 = self.P[6][:, 0:TB]

        def pcol(base):
            return PV[:, base + hp:base + hp + 1]

        def c4(ap):
            return ap.rearrange("p (c t) -> p c t", t=64)
        self.mm(P6, PV[0:32, lo:lo + 128], TG[0:32, :], True, True, reads=["pv", "TG"], writes=["P6"])
        S.op("act", lambda e: e.activation(CUM, P6, AF.Sigmoid, bias=pcol(PV_W0)), reads=["P6", "pv"], writes=["E"])
        self.mm(P6, PV[32:64, lo:lo + 128], TG[32:64, :], True, True, reads=["pv", "TG"], writes=["P6"])
        S.op("act", lambda e: e.activation(A_, P6, AF.Sigmoid, bias=pcol(PV_A0)), reads=["P6", "pv"], writes=["A"])
        self.mm(P6, PV[64:128, lo:lo + 128], TG[64:128, :], True, True, reads=["pv", "TG"], writes=["P6"])
        S.op("act", lambda e: e.copy(G, P6), reads=["P6"], writes=[kG])
        S.op("act", lambda e: e.activation(TMP, k, AF.Square, scale=pcol(PV_KK)), reads=[kk_, "pv"], writes=["TMP"])
        self.mm(P6, bo, TMP, True, True, reads=["cst", "TMP"], writes=["P6"])
        S.op("dve", lambda e: e.tensor_scalar(TMP2, P6, 1e-24, None, ALU.max), reads=["P6"], writes=["TMP2"])
        S.op("act", lambda e: e.activation(TMP2, TMP2, AF.Ln), reads=["TMP2"], writes=["TMP2"])
        S.op("act", lambda e: e.activation(TMP2, TMP2, AF.Exp, scale=-0.5), reads=["TMP2"], writes=["TMP2"])
        S.op("dve", lambda e: e.scalar_tensor_tensor(KK, k, pcol(PV_KK), TMP2, ALU.mult, ALU.mult), reads=[kk_, "pv", "TMP2"], writes=["KK"])
        S.op("dve", lambda e: e.tensor_scalar(TMP, A_, -1.0, pcol(PV_KA), ALU.add, ALU.mult), reads=["A", "pv"], writes=["TMP"])
        S.op("dve", lambda e: e.scalar_tensor_tensor(k, TMP, 1.0, k, ALU.add, ALU.mult), reads=["TMP", kk_], writes=[kk_])
        S.op("dve", lambda e: e.tensor_tensor(B_, KK, A_, ALU.mult), reads=["KK", "A"], writes=["B"])
        S.op("dve", lambda e: e.scalar_tensor_tensor(TMP, r, pcol(PV_RK), k, ALU.mult, ALU.mult), reads=[rk_, kk_, "pv"], writes=["TMP"])
        self.mm(P6, bo, TMP, True, True, reads=["cst", "TMP"], writes=["P6"])
        S.op("dve", lambda e: e.tensor_tensor(BON, P6, v, ALU.mult), reads=["P6", vk_], writes=[kBON])
        S.op("act", lambda e: e.copy(TMP2, CUM), reads=["E"], writes=["TMP2"])
        smask = CST[:, C_SM:C_SM + TB]
        S.op("dve", lambda e: e.tensor_tensor_scan(CUM, smask, TMP2, 0.0, ALU.mult, ALU.add),
             reads=["TMP2", "cst", "E"], writes=["E"])
        S.op("act", lambda e: e.activation(EX1, CUM, AF.Exp, scale=-SDEC), reads=["E"], writes=["EX1"])
        S.op("dve", lambda e: e.tensor_tensor(AR[:, :, 1, :], c4(r), c4(EX1), ALU.mult), reads=[rk_, "EX1"], writes=[kAR])
        S.op("act", lambda e: e.activation(EX2, CUM, AF.Exp, scale=SDEC), reads=["E"], writes=["EX2"])
        S.op("dve", lambda e: e.tensor_tensor(BK[:, :, 0, :], c4(B_), c4(EX2), ALU.mult), reads=["B", "EX2"], writes=[kBK])
        S.op("dve", lambda e: e.tensor_tensor(BK[:, :, 1, :], c4(k), c4(EX2), ALU.mult), reads=[kk_, "EX2"], writes=[kBK])
        S.op("dve", lambda e: e.tensor_tensor(TMP, CUM, TMP2, ALU.subtract), reads=["E", "TMP2"], writes=["TMP"])
        S.op("act", lambda e: e.activation(EX1, TMP, AF.Exp, scale=-SDEC), reads=["TMP"], writes=["EX1"])
        S.op("dve", lambda e: e.scalar_tensor_tensor(AR[:, :, 0, :], c4(KK), -1.0, c4(EX1), ALU.mult, ALU.mult), reads=["KK", "EX1"], writes=[kAR])
        S.op("dve", lambda e: e.tensor_tensor(c4(TMP), c4(CUM), c4(CUM)[:, :, 63:64].to_broadcast([128, NCH, 64]), ALU.subtract),
             reads=["E", "EX1"], writes=["TMP"])
        S.op("act", lambda e: e.activation(EX2, TMP, AF.Exp, scale=SDEC), reads=["TMP"], writes=["EX2"])
        S.op("dve", lambda e: e.tensor_tensor(BKH[:, :, 0, :], c4(B_), c4(EX2), ALU.mult), reads=["B", "EX2"], writes=[kBKH])
        S.op("dve", lambda e: e.tensor_tensor(BKH[:, :, 1, :], c4(k), c4(EX2), ALU.mult), reads=[kk_, "EX2"], writes=[kBKH])
        S.op("act", lambda e: e.activation(WC[:, 0:NCH], c4(CUM)[:, :, 63], AF.Exp, scale=-SDEC), reads=["E"], writes=[kWC])
        S.op("dve", lambda e: e.memset(VV[:, :, 0, :], 0.0), writes=[kVV])
        S.op("act", lambda e: e.copy(VV[:, :, 1, :], c4(v)), reads=[vk_], writes=[kVV])

    def rwkv_P(self, l, tb, hp):
        S, M, CST = self.S, self.M, self.CST
        pb = (3 * tb + hp) % 2
        AR, BK, BKH, VV, WC = (M[k][pb] for k in ("AR", "BK", "BKH", "VV", "WC"))
        kAR, kBK, kBKH, kVV, kWC = (("AR", pb), ("BK", pb), ("BKH", pb), ("VV", pb), ("WC", pb))
        GT, TT = M["GT"][pb], M["TT"][pb]
        kGT, kTT = ("GT", pb), ("TT", pb)
        BKHT, UV, Z = (M[k] for k in ("BKHT", "UV", "Z"))
        ident = CST[:, C_ID:C_ID + 128]

        def f2(ap):
            return ap.rearrange("p a b -> p (a b)")
        for hh in range(2):
            hs = slice(hh * 64, hh * 64 + 64)
            for c in range(NCH):
                self.mm(self.P[2 + hh][:, c * 128:(c + 1) * 128], f2(BK[hs, c]), f2(AR[hs, c]), True, True,
                        reads=[kBK, kAR], writes=["P%d" % (2 + hh)])
        mg = CST[:, C_MG:C_MG + 128].unsqueeze(1).to_broadcast([128, NCH, 128])
        for hh in range(2):
            S.op("dve", lambda e, hh=hh: e.tensor_tensor(
                GT[:, hh], self.P[2 + hh][:, 0:NCH * 128].rearrange("p (c x) -> p c x", x=128), mg, ALU.mult),
                reads=["P%d" % (2 + hh), "cst"], writes=[kGT])
        X, XTb = M["X"], M["XTb"]
        NI = NCH * 2
        for hh in range(2):
            hs = slice(hh * 64, hh * 64 + 64)
            for c in range(NCH):
                self.mm(self.P[4 + hh][0:64, c * 64:c * 64 + 64], AR[hs, c, 0, :], BK[hs, c, 0, :], True, True,
                        reads=[kAR, kBK], writes=["P%d" % (4 + hh)])
        ml = CST[0:64, C_ML:C_ML + 64].unsqueeze(1).to_broadcast([64, NCH, 64])

        def p3(bank, n=NI):
            return self.P[bank][0:64, 0:n * 64].rearrange("p (a b) -> p a b", b=64)
        for hh in range(2):
            S.op("dve", lambda e, hh=hh: e.tensor_tensor(X[0][0:64, hh * NCH:(hh + 1) * NCH, :], p3(4 + hh, NCH), ml, ALU.mult),
                 reads=["P%d" % (4 + hh), "cst"], writes=[("X", 0)])
        xt0 = GT[0:64, :, :, 0:64].rearrange("p h c x -> p (h c) x")
        id3 = ident[0:64, 0:64].unsqueeze(1).to_broadcast([64, NI, 64])
        S.op("dve", lambda e: e.tensor_tensor(TT[0:64], xt0, id3, ALU.add), reads=[kGT, "cst"], writes=[kTT])
        XT0b, TTb = M["XT0b"], M["TTb"]
        S.op("act", lambda e: e.copy(XT0b[0:64], xt0), reads=[kGT], writes=["XT0b"])
        S.op("act", lambda e: e.copy(TTb[0:64], TT[0:64]), reads=[kTT], writes=["TTb"])
        xc, xtc, xck, xtck = X[0], XT0b, ("X", 0), "XT0b"
        for j in range(1, 6):
            xn, xnk = X[j % 2], ("X", j % 2)
            xtn, xtnk = XTb[j % 2], ("XT", j % 2)
            for i in range(NI):
                self.mm(self.P[4][0:64, i * 64:i * 64 + 64], xtc[0:64, i, :], xc[0:64, i, :], True, True,
                        reads=[xck, xtck], writes=["P4"])
            if j < 5:
                for i in range(NI):
                    self.mm(self.P[5][0:64, i * 64:i * 64 + 64], xc[0:64, i, :], xtc[0:64, i, :], True, True,
                            reads=[xck, xtck], writes=["P5"])
            S.op("act", lambda e, xn=xn: e.copy(xn[0:64], p3(4)), reads=["P4"], writes=[xnk])
            if j < 5:
                S.op("dve", lambda e, xtn=xtn: e.tensor_copy(xtn[0:64], p3(5)), reads=["P5"], writes=[xtnk])
            for i in range(NI):
                self.mm(self.P[7][0:64, i * 64:i * 64 + 64], xn[0:64, i, :], TTb[0:64, i, :], True, True,
                        reads=[xnk, "TTb"], writes=["P7"])
            S.op("dve", lambda e: e.tensor_tensor(TT[0:64], p3(7), TT[0:64], ALU.add), reads=["P7", kTT], writes=[kTT])
            if j < 5:
                S.op("act", lambda e: e.copy(TTb[0:64], TT[0:64]), reads=[kTT], writes=["TTb"])
            xc, xtc, xck, xtck = xn, xtn, xnk, xtnk

    def rwkv_TS(self, l, tb, hp):
        S, M, CST = self.S, self.M, self.CST
        pb = (3 * tb + hp) % 2
        AR, BK, BKH, VV, WC = (M[k][pb] for k in ("AR", "BK", "BKH", "VV", "WC"))
        kAR, kBK, kBKH, kVV, kWC = (("AR", pb), ("BK", pb), ("BKH", pb), ("VV", pb), ("WC", pb))
        GT, TT = M["GT"][pb], M["TT"][pb]
        kGT, kTT = ("GT", pb), ("TT", pb)
        BKHT, UV, Z = (M[k] for k in ("BKHT", "UV", "Z"))
        ident = CST[:, C_ID:C_ID + 128]

        def f2(ap):
            return ap.rearrange("p a b -> p (a b)")
        for hh in range(2):
            hs = slice(hh * 64, hh * 64 + 64)
            for c in range(NCH):
                S.op("pe", lambda e, c=c, hs=hs, hh=hh: e.transpose(self.P[2 + hh][:, c * 64:c * 64 + 64], f2(BKH[hs, c]), ident[hs, hs]),
                     reads=[kBKH, "cst"], writes=["P%d" % (2 + hh)], pe_meta=(hh * 64, 64, 2 + hh))
                S.op("pe", lambda e, c=c, hs=hs, hh=hh: e.transpose(self.P[4 + hh][:, c * 64:c * 64 + 64], f2(VV[hs, c]), ident[hs, hs]),
                     reads=[kVV, "cst"], writes=["P%d" % (4 + hh)], pe_meta=(hh * 64, 64, 4 + hh))
        for hh in range(2):
            S.op("dve", lambda e, hh=hh: e.tensor_copy(BKHT[:, hh * NCH:(hh + 1) * NCH, :],
                                                       self.P[2 + hh][:, 0:NCH * 64].rearrange("p (a b) -> p a b", b=64)),
                 reads=["P%d" % (2 + hh)], writes=["BKHT"])
            S.op("act", lambda e, hh=hh: e.copy(UV[64:128, hh * NCH:(hh + 1) * NCH, :],
                                                self.P[4 + hh][64:128, 0:NCH * 64].rearrange("p (a b) -> p a b", b=64)),
                 reads=["P%d" % (4 + hh)], writes=["UVv"])
        HSk = ("HS", hp)
        pZ, pU, pY, pH = self.P[1], self.P[7], self.P[0], self.P[5]
        UVh = UV[0:64].rearrange("p (h c) v -> p h c v", h=2)
        for c in range(NCH):
            for hh in range(2):
                hs = slice(hh * 64, hh * 64 + 64)
                i = hh * NCH + c
                t1 = self.mm(pZ[0:64, hh * 64:hh * 64 + 64], AR[hs, c, 0, :], self.HS[hs, hp, :], True, False,
                             reads=[kAR, HSk], writes=["P1"], inc=True)
                self.mm(pZ[0:64, hh * 64:hh * 64 + 64], GT[64:128, hh, c, 0:64], UV[64:128, i, :], False, True,
                        reads=[kGT, "UVv"], writes=["P1"], after=(t1 if hh == 0 else None))
            S.op("act", lambda e: e.copy(Z[0:64, :], pZ[0:64, 0:128]), reads=["P1"], writes=["Z"])
            for hh in range(2):
                i = hh * NCH + c
                self.mm(pU[0:64, hh * 64:hh * 64 + 64], TT[0:64, i, :], Z[0:64, hh * 64:hh * 64 + 64], True, True,
                        reads=[kTT, "Z"], writes=["P7"])
            S.op("dve", lambda e, c=c: e.tensor_copy(UVh[:, :, c, :], pU[0:64, 0:128].rearrange("p (a b) -> p a b", b=64)),
                 reads=["P7"], writes=[("UVu", c)])
            for hh in range(2):
                hs = slice(hh * 64, hh * 64 + 64)
                i = hh * NCH + c
                self.mm(pY[hs, c * 64:c * 64 + 64], self.HS[hs, hp, :], AR[hs, c, 1, :], True, False,
                        reads=[HSk, kAR], writes=["P0"])
                self.mm(pY[hs, c * 64:c * 64 + 64], UV[:, i, :], GT[:, hh, c, 64:128], False, True,
                        reads=[("UVu", c), "UVv", kGT], writes=["P0"])
            for hh in range(2):
                hs = slice(hh * 64, hh * 64 + 64)
                i = hh * NCH + c
                self.mm(pH[hs, 0:64], BKHT[:, i, :], UV[:, i, :], True, True,
                        reads=["BKHT", ("UVu", c), "UVv"], writes=["P5"])
            S.op("dve", lambda e, c=c: e.scalar_tensor_tensor(self.HS[:, hp, :], self.HS[:, hp, :], WC[:, c:c + 1], pH[:, 0:64], ALU.mult, ALU.add),
                 reads=["P5", kWC, HSk], writes=[HSk])

    def rwkv_post(self, l, tb, hp):
        S, M, PV, CST = self.S, self.M, self.PV, self.CST
        pb = (3 * tb + hp) % 2
        YS, YC, T1, T2 = M["YS"], M["YC"], M["PT1"], M["PT2"]
        G, BON = M["G"][pb], M["BON"][pb]
        bo = CST[:, C_BO:C_BO + 128]
        pY = self.P[0]
        P6 = self.P[6][:, 0:TB]

        def pcol(base):
            return PV[:, base + hp:base + hp + 1]
        S.op("act", lambda e: e.copy(YS, pY[:, 0:TB]), reads=["P0"], writes=["YS"])
        self.mm(P6, bo, YS, True, True, reads=["cst", "YS"], writes=["P6"])
        S.op("dve", lambda e: e.scalar_tensor_tensor(YC, P6, -1.0 / 64.0, YS, ALU.mult, ALU.add), reads=["P6", "YS"], writes=["YC"])
        S.op("act", lambda e: e.activation(T1, YC, AF.Square), reads=["YC"], writes=["PT1"])
        self.mm(P6, bo, T1, True, True, reads=["cst", "PT1"], writes=["P6"])
        S.op("act", lambda e: e.activation(T2, P6, AF.Ln, bias=GN_EPS, scale=1.0 / 64.0), reads=["P6"], writes=["PT2"])
        S.op("act", lambda e: e.activation(T2, T2, AF.Exp, scale=-0.5), reads=["PT2"], writes=["PT2"])
        S.op("dve", lambda e: e.tensor_tensor(YC, YC, T2, ALU.mult), reads=["YC", "PT2"], writes=["YC"])
        S.op("act", lambda e: e.activation(YC, YC, AF.Identity, bias=pcol(PV_GNB), scale=pcol(PV_GNG)), reads=["YC", "pv"], writes=["YC"])
        S.op("dve", lambda e: e.tensor_tensor(YC, YC, BON, ALU.add), reads=["YC", ("BON", pb)], writes=["YC"])
        S.op("dve", lambda e: e.tensor_tensor(M["MIX"][:, hp, :], YC, G, ALU.mult), reads=["YC", ("G", pb)], writes=[("mix", hp)])

    def attn_block(self, l, tb, winT):
        S, M, PV = self.S, self.M, self.PV
        t0 = tb * TB
        QF, QN, KB, VB, BIAS, MIX = M["QF"], M["QN"], M["KB"], M["VB"], M["BIAS"], M["MIX"]
        scr = M["scr"]

        def hnorm(dst, gcol, dkey):
            sq = scr["sq"][0][:, 0:TB]
            rt = scr["rt"][:, 0:TB]
            S.op("act", lambda e: e.activation(sq, QF, AF.Square), reads=["QF"], writes=[("sq", 0)])
            self.mm(self.P[6][:, 0:TB], self.bo_bf[:], sq, True, True, reads=["bo_bf", ("sq", 0)], writes=["P6"])
            S.op("act", lambda e: e.activation(rt, self.P[6][:, 0:TB], AF.Ln, bias=RMS_EPS, scale=1.0 / 64.0), reads=["P6"], writes=["rt"])
            S.op("act", lambda e: e.activation(rt, rt, AF.Exp, scale=-0.5), reads=["rt"], writes=["rt"])
            S.op("dve", lambda e: e.scalar_tensor_tensor(dst, QF, PV[:, gcol:gcol + 1], rt, ALU.mult, ALU.mult),
                 reads=["QF", "pv", "rt"], writes=[dkey])
        for part, c0 in (("q", 1280), ("k", 1664)):
            wsl, si = self.load_w_piece(winT, c0, 384)
            for hp in range(3):
                bank = (1, 7)[hp % 2]
                pp = self.proj_fm(wsl, si, hp * 128, bank)
                S.op("act", lambda e, pp=pp: e.copy(QF, pp[:, 0:TB]), reads=["P%d" % bank], writes=["QF"])
                if part == "q":
                    hnorm(QN[:, hp, :], PV_BQG, ("qn", hp))
                else:
                    hnorm(KB[:, hp, t0 % 768:t0 % 768 + TB], PV_BKG, ("kb", hp))
        wsl, si = self.load_w_piece(winT, 2048, 384)
        for i in range(NQT):
            qt = tb * NQT + i
            vbk = (1, 7)[i % 2]
            pp = self.P[vbk]
            for kc in range(KC):
                self.mm(pp[:, 0:384], M["HN"][:, kc, i * 128:(i + 1) * 128], wsl[:, kc, :], kc == 0, kc == KC - 1,
                        reads=[("ws", si)] + self.hk(kc, 0, TB), writes=["P%d" % vbk])
            S.op("act", lambda e, pp=pp, qt=qt: e.copy(VB[:, qt % 6, :], pp[:, 0:384]), reads=["P%d" % vbk], writes=["vb"])
        for i in range(NQT):
            qt = tb * NQT + i
            r0 = max(0, 4 - qt)
            for hp in range(3):
                ob = 4
                pO, pD = self.P[ob], self.P[ob + 1]
                for hh in range(2):
                    h = 2 * hp + hh
                    hs = slice(hh * 64, hh * 64 + 64)
                    b = hh
                    SBt, PT = M["SBt"][b], M["PT"][b]
                    sbk = ("sbt", 0)
                    for r in range(r0, 5):
                        kt = qt - 4 + r
                        sb = 6 if hh == 1 else 2
                        if r < 4:
                            out, bk = self.P[sb][:, r * 128:(r + 1) * 128], "P%d" % sb
                        else:
                            out, bk = self.P[sb + 1][:, 0:128], "P%d" % (sb + 1)
                        self.mm(out, KB[hs, hp, (kt % 6) * 128:(kt % 6 + 1) * 128], QN[hs, hp, i * 128:(i + 1) * 128], True, True,
                                reads=[("kb", hp), ("qn", hp)], writes=[bk])
                    if r0 < 4:
                        S.op("dve", lambda e, SBt=SBt, h=h, r0=r0, sb=sb: e.scalar_tensor_tensor(
                            SBt[:, r0 * 128:512], self.P[sb][:, r0 * 128:512], 0.125, BIAS[:, h, r0 * 128:512], ALU.mult, ALU.add),
                            reads=["P%d" % sb, "bias"], writes=[sbk])
                    S.op("dve", lambda e, SBt=SBt, h=h, sb=sb: e.scalar_tensor_tensor(
                        SBt[:, 512:640], self.P[sb + 1][:, 0:128], 0.125, BIAS[:, h, 512:640], ALU.mult, ALU.add),
                        reads=["P%d" % (sb + 1), "bias"], writes=[sbk])
                    S.op("act", lambda e, SBt=SBt, PT=PT, r0=r0: e.activation(PT[:, r0 * 128:640], SBt[:, r0 * 128:640], AF.Exp),
                         reads=[sbk], writes=[("pt", b)])
                    for r in range(r0, 5):
                        kt = qt - 4 + r
                        self.mm(pO[hs, 0:128], VB[:, kt % 6, h * 64:(h + 1) * 64], PT[:, r * 128:(r + 1) * 128], r == r0, r == 4,
                                reads=["vb", ("pt", b)], writes=["P%d" % ob])
                    for r in range(r0, 5):
                        self.mm(pD[hs, 0:128], self.ones_bf[:, 0:64], PT[:, r * 128:(r + 1) * 128], r == r0, r == 4,
                                reads=["ones_bf", ("pt", b)], writes=["P%d" % (ob + 1)])
                rd = M["rden"][hp % 2]
                S.op("act", lambda e, rd=rd, pD=pD: e.activation(rd, pD[:, 0:128], AF.Ln), reads=["P%d" % (ob + 1)], writes=[("rden", hp % 2)])
                S.op("act", lambda e, rd=rd: e.activation(rd, rd, AF.Exp, scale=-1.0), reads=[("rden", hp % 2)], writes=[("rden", hp % 2)])
                S.op("dve", lambda e, rd=rd, hp=hp, i=i, pO=pO: e.tensor_tensor(MIX[:, 3 + hp, i * 128:(i + 1) * 128], pO[:, 0:128], rd, ALU.mult),
                     reads=["P%d" % ob, ("rden", hp % 2)], writes=[("mix", 3 + hp)])

    def pool_block(self, l, tb, winT):
        S, M, PV, CST = self.S, self.M, self.PV, self.CST
        LV0, LV1, LV2 = M["LV"]
        POOLED, MIX = M["POOLED"], M["MIX"]
        W = TB + 16
        wsl, si = self.load_w_piece(winT, 2432, 256)
        for ch in range(2):
            pbk = (1, 7)[ch]
            pp = self.proj_fm(wsl, si, ch * 128, pbk)
            S.op("act", lambda e, pp=pp, ch=ch: e.copy(LV0[:, ch, 16:W], pp[:, 0:TB]), reads=["P%d" % pbk], writes=["lv0"])
        S.op("dve", lambda e: e.tensor_copy(LV0[:, :, 0:16], self.UHIST[:]), reads=["uhist"], writes=["lv0"])
        S.op("dve", lambda e: e.tensor_copy(self.UHIST[:], LV0[:, :, TB:W]), reads=["lv0"], writes=["uhist"])
        S.op("dve", lambda e: e.tensor_tensor(LV1[:, :, 1:W], LV0[:, :, 1:W], LV0[:, :, 0:W - 1], ALU.add), reads=["lv0"], writes=["lv1"])
        S.op("dve", lambda e: e.tensor_tensor(LV2[:, :, 3:W], LV1[:, :, 3:W], LV1[:, :, 1:W - 2], ALU.add), reads=["lv1"], writes=["lv2"])
        S.op("dve", lambda e: e.tensor_tensor(LV1[:, 1, 7:W], LV2[:, 1, 7:W], LV2[:, 1, 3:W - 4], ALU.add), reads=["lv2", "lv1"], writes=["lv1"])
        S.op("dve", lambda e: e.tensor_tensor(LV2[64:128, 1, 15:W], LV1[64:128, 1, 15:W], LV1[64:128, 1, 7:W - 8], ALU.add),
             reads=["lv1", "lv2"], writes=["lv2"])
        if tb == 0:
            pf = CST[:, C_PF:C_PF + 32].rearrange("p (a b) -> p a b", b=16)
            S.op("dve", lambda e: e.tensor_tensor(LV1[0:64, :, 16:32], LV1[0:64, :, 16:32], pf[0:64], ALU.mult), reads=["lv1", "cst"], writes=["lv1"])
            S.op("dve", lambda e: e.tensor_tensor(LV2[64:128, :, 16:32], LV2[64:128, :, 16:32], pf[64:128], ALU.mult), reads=["lv2", "cst"], writes=["lv2"])
        for ch in range(2):
            iw = CST[:, C_IW + ch:C_IW + ch + 1]
            S.op("dve", lambda e, ch=ch, iw=iw: e.scalar_tensor_tensor(
                POOLED[0:64, ch, :], LV1[0:64, ch, 16:W], iw[0:64], LV0[0:64, ch, 16:W], ALU.mult, ALU.subtract),
                reads=["lv1", "lv0", "cst"], writes=[("pooled", ch), "lv0"])
            S.op("dve", lambda e, ch=ch, iw=iw: e.scalar_tensor_tensor(
                POOLED[64:128, ch, :], LV2[64:128, ch, 16:W], iw[64:128], LV0[64:128, ch, 16:W], ALU.mult, ALU.subtract),
                reads=["lv2", "lv0", "cst"], writes=[("pooled", ch), "lv0"])
        for ch in range(2):
            pp = self.P[6 + ch]
            self.mm(pp[:, 0:TB], PV[:, PV_PW + ch * 128:PV_PW + (ch + 1) * 128], POOLED[:, ch, :], True, True,
                    reads=["pv", ("pooled", ch)], writes=["P%d" % (6 + ch)])
            S.op("dve", lambda e, ch=ch, pp=pp: e.tensor_scalar(MIX[:, 6 + ch, :], pp[:, 0:TB], PV[:, PV_PSC + ch:PV_PSC + ch + 1], None, ALU.mult),
                 reads=["P%d" % (6 + ch), "pv"], writes=[("mix", 6 + ch)])

    def store_out(self):
        S = self.S
        keys = []
        for kc in range(KC):
            for t8 in range(0, 8, 2):
                k = ("out", kc, t8)
                S.dma("sp", self.d_out[kc * 128:(kc + 1) * 128, t8 * 256:(t8 + 2) * 256],
                      self.XT[:, kc, t8 * 256:(t8 + 2) * 256], reads=self.xk(kc, t8 * 256, 512), writes=[k], semkey="out")
                keys.append(k)
        last = S.lastw[keys[-1]]
        for k in keys:
            S.lastw[k] = last
        S.wait_keys("sp", keys)


def build_program(cfg):
    nc = bass.Bass("TRN2", target_bir_lowering=False)
    with ExitStack() as st:
        mk = MK(nc, st, cfg)
        for (l, stage) in cfg["stages"]:
            if stage == "ffn1":
                mk.ffn(l, 0)
            elif stage == "ffn2":
                mk.ffn(l, 1)
            elif stage == "cross":
                mk.cross(l)
            elif stage.startswith("mix"):
                if len(stage) > 3:
                    mk.cfg["mixers"] = stage[3:]
                mk.mixer(l)
        mk.S.barrier()
        mk.store_out()
        mk.S.emit()
    return nc


def host_consts():
    c = np.zeros((128, C_N), np.float32)
    c[:, C_ID:C_ID + 128] = np.eye(128, dtype=np.float32)
    bo = np.zeros((128, 128), np.float32)
    bo[:64, :64] = 1.0
    bo[64:, 64:] = 1.0
    c[:, C_BO:C_BO + 128] = bo
    si = np.arange(64)[:, None]
    ti = np.arange(64)[None, :]
    mg = np.zeros((128, 128), np.float32)
    mg[:64, :64] = si < ti
    mg[:64, 64:] = si <= ti
    mg[64:, :64] = si < ti
    mg[64:, 64:] = si <= ti
    c[:, C_MG:C_MG + 128] = mg
    c[:, C_MG + 128:C_MG + 256] = mg
    ml = (si > ti).astype(np.float32)
    c[:64, C_ML:C_ML + 64] = ml
    c[64:, C_ML:C_ML + 64] = ml
    c[:, C_SM:C_SM + TB] = 1.0
    c[:, C_SM:C_SM + TB:64] = 0.0
    wins = {(0, 0): 2, (0, 1): 4, (1, 0): 8, (1, 1): 16}
    for (ch, half), win in wins.items():
        rows = slice(half * 64, half * 64 + 64)
        tt = np.arange(16)
        c[rows, C_PF + ch * 16:C_PF + (ch + 1) * 16] = (win / np.minimum(tt + 1, win)).astype(np.float32)[None, :]
        c[rows, C_IW + ch] = 1.0 / win
    return c


def host_layer_params(inp):
    pv = np.zeros((L, 128, PV_N), np.float32)

    def put(l, col, vec):
        n = vec.shape[0] // 128
        pv[l, :, col:col + n] = vec.reshape(n, 128).T
    for l in range(L):
        put(l, PV_FFN1, inp["norm_ffn1"][l])
        put(l, PV_MIX, inp["norm_mix"][l])
        put(l, PV_CROSS, inp["norm_cross"][l])
        put(l, PV_MEM, inp["norm_mem"][l])
        put(l, PV_FFN2, inp["norm_ffn2"][l])
        put(l, PV_XQ, inp["x_q_gain"][l])
        put(l, PV_XK, inp["x_k_gain"][l])
        put(l, PV_MU, inp["a_mu"][l])
        put(l, PV_W0, inp["a_w0"][l])
        put(l, PV_A0, inp["a_a0"][l])
        put(l, PV_KK, inp["a_k_k"][l])
        put(l, PV_KA, inp["a_k_a"][l])
        put(l, PV_RK, inp["a_r_k"][l].reshape(-1))
        put(l, PV_GNG, inp["a_gn_g"][l])
        put(l, PV_GNB, inp["a_gn_b"][l])
        if l >= 1:
            put(l, PV_V0, inp["a_v0"][l - 1])
            pv[l, 0:32, PV_VUP:PV_VUP + 384] = inp["a_v_up"][l - 1]
            pv[l, :, PV_VDOWN:PV_VDOWN + 96] = inp["a_v_down"][l - 1].reshape(3, 128, 32).transpose(1, 0, 2).reshape(128, 96)
        pv[l, :, PV_BQG] = np.tile(inp["b_q_gain"][l], 2)
        pv[l, :, PV_BKG] = np.tile(inp["b_k_gain"][l], 2)
        put(l, PV_PSC, inp["c_pool_scale"][l])
        pv[l, 0:32, PV_LORA:PV_LORA + 384] = inp["a_w_up"][l]
        pv[l, 32:64, PV_LORA:PV_LORA + 384] = inp["a_a_up"][l]
        pv[l, 64:128, PV_LORA:PV_LORA + 384] = inp["a_g_up"][l]
        for ch in range(2):
            for half in range(2):
                g = ch * 2 + half
                rows = slice(half * 64, half * 64 + 64)
                pv[l, rows, PV_PW + ch * 128 + half * 64:PV_PW + ch * 128 + half * 64 + 64] = inp["c_pool_w"][l, g]
    return pv


def host_bias(inp):
    j = np.arange(128)[:, None, None]
    r = np.arange(5)[None, :, None]
    i = np.arange(128)[None, None, :]
    dist = (4 - r) * 128 + i - j
    idx = np.clip(dist, -63, 256) + 63
    dchunk = (r - 4) * 2 + (j // 64) - (i // 64)
    valid = (dchunk >= -8) & (dchunk <= 0)
    out = np.zeros((L, 128, 6, 5, 128), np.float32)
    for l in range(L):
        for h in range(6):
            g = inp["b_rel_bias"][l, h][idx]
            out[l, :, h] = np.where(valid, g, np.float32(NEG))
    return out.reshape(L, 128, 6 * 640)


WEIGHT_KEYS = ("ffn1_wi", "ffn1_wo", "ffn2_wi", "ffn2_wo", "w_in", "w_out", "x_wq", "x_wkv", "x_wo")


def make_in_maps(inp, cores):
    shared = {"cst": host_consts(), "lyr": host_layer_params(inp), "biasT": host_bias(inp)}
    for k in WEIGHT_KEYS:
        shared[k] = np.ascontiguousarray(inp[k], dtype=np.float32)
    maps = []
    for b in cores:
        m = dict(shared)
        m["xT"] = np.ascontiguousarray(inp["x"][b].T)
        m["memT"] = np.ascontiguousarray(inp["mem"][b].T)
        maps.append(m)
    return maps


FULL_STAGES = [(l, s) for l in range(L) for s in ("ffn1", "mix", "cross", "ffn2")]
_CACHE = {}


def kernel(**inputs):
    inp = {k: np.asarray(v) for k, v in inputs.items()}
    if "nc" not in _CACHE:
        _CACHE["nc"] = build_program({"stages": FULL_STAGES})
    nc = _CACHE["nc"]
    maps = make_in_maps(inp, list(range(8)))
    res = run_bass_kernel_spmd(nc, maps, core_ids=list(range(8)))
    out = np.stack([np.ascontiguousarray(r["outT"].T) for r in res.results], axis=0)
    return out.astype(np.float32)
```
